# Optimizing a Trainium2 kernel written in Bass

```python
import math
import jax, jax.numpy as jnp
from jax import lax
import numpy as np

D_MODEL = 1024
BATCH = 16
SEQ = 2048
DEPTH = 1
DEC_BATCH = 32
DEC_SEQ = 8
PAST_LEN = 16384
PAGE_SIZE = 128

D_RNN = D_MODEL
RNN_BLOCKS = 16
RNN_BLOCK_W = D_RNN // RNN_BLOCKS
CONV_W = 4
LRU_C = 8.0
HEAD_DIM = 64
GROUPS = ((128, 1), (512, 4), (2048, 16))
HEADS_PER_GROUP = 4
N_HEADS = HEADS_PER_GROUP * len(GROUPS)
ATT_W = N_HEADS * HEAD_DIM
ATT_OUT_W = HEADS_PER_GROUP * HEAD_DIM
ROPE_THETA = 10000.0
D_FF = ((8 * D_MODEL + 2) // 3 + 255) // 256 * 256
IN_W = 2 * D_RNN + 3 * ATT_W + 2 * D_MODEL
EPS = 1e-6
NEG = -1e30

kernel_name = "hybrid_rglru_dilated_swa_decode_step"


def rms_norm(x, g):
    xf = x.astype(jnp.float32)
    y = xf * lax.rsqrt(jnp.mean(xf * xf, axis=-1, keepdims=True) + EPS) * g.astype(jnp.float32)
    return y.astype(x.dtype)


def rope(x, pos):
    half = HEAD_DIM // 2
    inv = jnp.exp(-math.log(ROPE_THETA) * jnp.arange(half, dtype=jnp.float32) * (2.0 / HEAD_DIM))
    ang = pos.astype(jnp.float32)[:, None] * inv[None, :]
    c = jnp.cos(ang)[None, :, None, :]
    s = jnp.sin(ang)[None, :, None, :]
    xf = x.astype(jnp.float32)
    x1, x2 = xf[..., :half], xf[..., half:]
    return jnp.concatenate([x1 * c - x2 * s, x2 * c + x1 * s], axis=-1).astype(x.dtype)


def causal_conv(x, buf, w, b):
    T = x.shape[1]
    xp = jnp.concatenate([buf.astype(x.dtype), x], axis=1)
    y = b + sum(xp[:, j:j + T] * w[j] for j in range(CONV_W))
    return y, xp[:, -(CONV_W - 1):]


def rg_lru(x, h0, gate_a_w, gate_a_b, gate_x_w, gate_x_b, lam):
    B, T, _ = x.shape
    xb = x.reshape(B, T, RNN_BLOCKS, RNN_BLOCK_W)
    r = jax.nn.sigmoid(jnp.einsum('btnc,ncd->btnd', xb, gate_a_w).reshape(B, T, D_RNN) + gate_a_b)
    i = jax.nn.sigmoid(jnp.einsum('btnc,ncd->btnd', xb, gate_x_w).reshape(B, T, D_RNN) + gate_x_b)
    log_a = -LRU_C * r.astype(jnp.float32) * jax.nn.softplus(-lam.astype(jnp.float32))
    a = jnp.exp(log_a)
    u = jnp.sqrt(-jnp.expm1(2.0 * log_a)) * (i * x).astype(jnp.float32)

    def step(h, au):
        a_t, u_t = au
        h = a_t * h + u_t
        return h, h

    hT, hs = lax.scan(step, h0.astype(jnp.float32), (jnp.swapaxes(a, 0, 1), jnp.swapaxes(u, 0, 1)))
    return jnp.swapaxes(hs, 0, 1).astype(x.dtype), hT.astype(h0.dtype)


def dilated_attn_prompt(q, k, v, dil, nback):
    B, S, H, Dh = q.shape
    L = S // dil
    nb = -(-L // nback)
    Lp = nb * nback

    def to_cls(t):
        t = t.reshape(B, L, dil, H, Dh).transpose(0, 2, 1, 3, 4)
        return jnp.pad(t, ((0, 0), (0, 0), (0, Lp - L), (0, 0), (0, 0)))

    def windows(t):
        t = jnp.pad(t, ((0, 0), (0, 0), (nback, 0), (0, 0), (0, 0))).reshape(B, dil, nb + 1, nback, H, Dh)
        return jnp.concatenate([t[:, :, :-1], t[:, :, 1:]], axis=3)

    qb = to_cls(q).reshape(B, dil, nb, nback, H, Dh)
    kw = windows(to_cls(k))
    vw = windows(to_cls(v))
    s = jnp.einsum('brnqhc,brnkhc->brnhqk', qb, kw, preferred_element_type=jnp.float32) * (HEAD_DIM ** -0.5)
    qi = jnp.arange(nback)[:, None]
    kj = jnp.arange(2 * nback)[None, :]
    dist = nback + qi - kj
    key_idx = jnp.arange(nb)[:, None, None] * nback + kj[None] - nback
    valid = (dist >= 0)[None] & (dist <= nback)[None] & (key_idx >= 0)
    s = jnp.where(valid[:, None], s, NEG)
    lse = jax.nn.logsumexp(s, axis=-1)
    p = jnp.exp(s - lse[..., None])
    o = jnp.einsum('brnhqk,brnkhc->brnqhc', p.astype(v.dtype), vw, preferred_element_type=jnp.float32)
    o = o.reshape(B, dil, Lp, H, Dh)[:, :, :L].transpose(0, 2, 1, 3, 4).reshape(B, S, H, Dh)
    lse = lse.transpose(0, 1, 2, 4, 3).reshape(B, dil, Lp, H)[:, :, :L].transpose(0, 2, 1, 3).reshape(B, S, H)
    return o, lse


def dilated_attn_sample(q, k_all, v_all, dil, nback):
    B, T, H, Dh = q.shape
    Wb = k_all.shape[1] - T
    idx = Wb + jnp.arange(T)[:, None] - jnp.arange(nback + 1)[None, :] * dil
    valid = idx >= 0
    idxc = jnp.clip(idx, 0)
    kg = k_all[:, idxc]
    vg = v_all[:, idxc]
    s = jnp.einsum('bthc,btmhc->bthm', q, kg, preferred_element_type=jnp.float32) * (HEAD_DIM ** -0.5)
    s = jnp.where(valid[:, None, :], s, NEG)
    lse = jax.nn.logsumexp(s, axis=-1)
    p = jnp.exp(s - lse[..., None])
    o = jnp.einsum('bthm,btmhc->bthc', p.astype(vg.dtype), vg, preferred_element_type=jnp.float32)
    return o, lse


def decoder_layer(x, pos, conv_buf, h0, kv_bufs, norm1_g, w_in, b_merge, conv_w, conv_b,
                  gate_a_w, gate_a_b, gate_x_w, gate_x_b, rg_lambda, w_branch_a, w_branch_b,
                  w_out, norm2_g, w_ffn_in, w_ffn_out):
    B, T, _ = x.shape
    xn = rms_norm(x, norm1_g)
    z = xn @ w_in
    cuts = [D_RNN, 2 * D_RNN, 2 * D_RNN + ATT_W, 2 * D_RNN + 2 * ATT_W, 2 * D_RNN + 3 * ATT_W]
    x_rnn, g_rnn, q, k, v, g_merge = jnp.split(z, cuts, axis=-1)
    xc, new_conv = causal_conv(x_rnn, conv_buf, conv_w, conv_b)
    hs, new_h = rg_lru(xc, h0, gate_a_w, gate_a_b, gate_x_w, gate_x_b, rg_lambda)
    branch_a = (jax.nn.gelu(g_rnn) * hs) @ w_branch_a
    q = rope(q.reshape(B, T, N_HEADS, HEAD_DIM), pos)
    k = rope(k.reshape(B, T, N_HEADS, HEAD_DIM), pos)
    v = v.reshape(B, T, N_HEADS, HEAD_DIM)
    outs, lses, new_kv = [], [], []
    for g, (win, dil) in enumerate(GROUPS):
        hsl = slice(g * HEADS_PER_GROUP, (g + 1) * HEADS_PER_GROUP)
        qg, kg, vg = q[:, :, hsl], k[:, :, hsl], v[:, :, hsl]
        nback = win // dil
        if kv_bufs is None:
            o, l = dilated_attn_prompt(qg, kg, vg, dil, nback)
            keep = min(win, T)
            new_kv.append((kg[:, -keep:], vg[:, -keep:]))
        else:
            kb, vb = kv_bufs[g]
            k_all = jnp.concatenate([kb.astype(kg.dtype), kg], axis=1)
            v_all = jnp.concatenate([vb.astype(vg.dtype), vg], axis=1)
            o, l = dilated_attn_sample(qg, k_all, v_all, dil, nback)
            keep = kb.shape[1]
            new_kv.append((k_all[:, -keep:], v_all[:, -keep:]))
        outs.append(o)
        lses.append(l)
    wts = jax.nn.softmax(jnp.stack(lses), axis=0)
    o = jnp.sum(wts[..., None] * jnp.stack(outs), axis=0).reshape(B, T, ATT_OUT_W).astype(x.dtype)
    branch_b = o @ w_branch_b
    gates = jax.nn.sigmoid((g_merge + b_merge).astype(jnp.float32)).astype(x.dtype)
    g_a, g_b = jnp.split(gates, 2, axis=-1)
    h = x + (g_a * branch_a + g_b * branch_b) @ w_out
    hn = rms_norm(h, norm2_g)
    f_gate, f_up = jnp.split(hn @ w_ffn_in, 2, axis=-1)
    h = h + (jax.nn.silu(f_gate) * f_up) @ w_ffn_out
    return h, new_conv, new_h, new_kv


def setup_inputs(seed: int = 0) -> dict:
    key = jax.random.key(seed)
    ks = iter(jax.random.split(key, 40))

    def nrm(shape, scale):
        return jax.random.normal(next(ks), shape, jnp.float32) * scale

    inp = {}
    inp['x_prompt'] = nrm((BATCH, SEQ, D_MODEL), 1.0)
    inp['x_sample'] = nrm((DEC_BATCH, DEC_SEQ, D_MODEL), 1.0)
    inp['state_conv'] = nrm((DEPTH, DEC_BATCH, CONV_W - 1, D_RNN), 0.5)
    inp['state_h'] = nrm((DEPTH, DEC_BATCH, D_RNN), 0.5)
    for g, (win, dil) in enumerate(GROUPS):
        wb = min(win, PAST_LEN)
        inp['cache_k_g%d' % g] = nrm((DEPTH, DEC_BATCH, wb, HEADS_PER_GROUP, HEAD_DIM), 1.0)
        inp['cache_v_g%d' % g] = nrm((DEPTH, DEC_BATCH, wb, HEADS_PER_GROUP, HEAD_DIM), 1.0)
    inp['norm1_g'] = 1.0 + nrm((DEPTH, D_MODEL), 0.01)
    inp['w_in'] = nrm((DEPTH, D_MODEL, IN_W), D_MODEL ** -0.5)
    inp['b_merge'] = nrm((DEPTH, 2 * D_MODEL), 0.01)
    inp['conv_w'] = nrm((DEPTH, CONV_W, D_RNN), CONV_W ** -0.5)
    inp['conv_b'] = nrm((DEPTH, D_RNN), 0.01)
    inp['gate_a_w'] = nrm((DEPTH, RNN_BLOCKS, RNN_BLOCK_W, RNN_BLOCK_W), RNN_BLOCK_W ** -0.5)
    inp['gate_a_b'] = nrm((DEPTH, D_RNN), 0.01)
    inp['gate_x_w'] = nrm((DEPTH, RNN_BLOCKS, RNN_BLOCK_W, RNN_BLOCK_W), RNN_BLOCK_W ** -0.5)
    inp['gate_x_b'] = nrm((DEPTH, D_RNN), 0.01)
    u = jax.random.uniform(next(ks), (DEPTH, D_RNN), jnp.float32, 0.9, 0.999)
    sa = u ** (1.0 / LRU_C)
    inp['rg_lambda'] = jnp.log(sa) - jnp.log1p(-sa)
    inp['w_branch_a'] = nrm((DEPTH, D_RNN, D_MODEL), D_RNN ** -0.5)
    inp['w_branch_b'] = nrm((DEPTH, ATT_OUT_W, D_MODEL), ATT_OUT_W ** -0.5)
    inp['w_out'] = nrm((DEPTH, D_MODEL, D_MODEL), D_MODEL ** -0.5)
    inp['norm2_g'] = 1.0 + nrm((DEPTH, D_MODEL), 0.01)
    inp['w_ffn_in'] = nrm((DEPTH, D_MODEL, 2 * D_FF), D_MODEL ** -0.5)
    inp['w_ffn_out'] = nrm((DEPTH, D_FF, D_MODEL), D_FF ** -0.5)
    inp['norm_f_g'] = 1.0 + nrm((D_MODEL,), 0.01)
    return inp


def reference(x_prompt, x_sample, state_conv, state_h, cache_k_g0, cache_v_g0, cache_k_g1,
              cache_v_g1, cache_k_g2, cache_v_g2, norm1_g, w_in, b_merge, conv_w, conv_b,
              gate_a_w, gate_a_b, gate_x_w, gate_x_b, rg_lambda, w_branch_a, w_branch_b,
              w_out, norm2_g, w_ffn_in, w_ffn_out, norm_f_g):
    pos_p = jnp.arange(x_prompt.shape[1], dtype=jnp.int32)
    pos_s = PAST_LEN + jnp.arange(x_sample.shape[1], dtype=jnp.int32)
    Bp = x_prompt.shape[0]
    xp, xs = x_prompt, x_sample
    p_conv, p_h, s_conv, s_h = [], [], [], []
    p_kv = [[[], []] for _ in GROUPS]
    s_kv = [[[], []] for _ in GROUPS]
    for l in range(DEPTH):
        lw = (norm1_g[l], w_in[l], b_merge[l], conv_w[l], conv_b[l], gate_a_w[l], gate_a_b[l],
              gate_x_w[l], gate_x_b[l], rg_lambda[l], w_branch_a[l], w_branch_b[l], w_out[l],
              norm2_g[l], w_ffn_in[l], w_ffn_out[l])
        conv0 = jnp.zeros((Bp, CONV_W - 1, D_RNN), xp.dtype)
        h0 = jnp.zeros((Bp, D_RNN), state_h.dtype)
        xp, cp, hp, kvp = decoder_layer(xp, pos_p, conv0, h0, None, *lw)
        bufs = [(cache_k_g0[l], cache_v_g0[l]), (cache_k_g1[l], cache_v_g1[l]), (cache_k_g2[l], cache_v_g2[l])]
        xs, cs, hs, kvs = decoder_layer(xs, pos_s, state_conv[l], state_h[l], bufs, *lw)
        p_conv.append(cp)
        p_h.append(hp)
        s_conv.append(cs)
        s_h.append(hs)
        for g in range(len(GROUPS)):
            p_kv[g][0].append(kvp[g][0])
            p_kv[g][1].append(kvp[g][1])
            s_kv[g][0].append(kvs[g][0])
            s_kv[g][1].append(kvs[g][1])
    y_prompt = rms_norm(xp, norm_f_g)
    y_sample = rms_norm(xs, norm_f_g)
    st = jnp.stack
    return (y_prompt, y_sample,
            st(p_conv), st(p_h),
            st(p_kv[0][0]), st(p_kv[0][1]), st(p_kv[1][0]), st(p_kv[1][1]), st(p_kv[2][0]), st(p_kv[2][1]),
            st(s_conv), st(s_h),
            st(s_kv[0][0]), st(s_kv[0][1]), st(s_kv[1][0]), st(s_kv[1][1]), st(s_kv[2][0]), st(s_kv[2][1]))
```

```python
import math
import numpy as np
from contextlib import ExitStack
import concourse.bass as bass
import concourse.mybir as mybir
from concourse.bass_utils import run_bass_kernel_spmd

F32 = mybir.dt.float32
BF16 = mybir.dt.bfloat16
ALU = mybir.AluOpType
AF = mybir.ActivationFunctionType

D = 1024
T = 2048
NB = 2
SB = 4
ST = 8
TS = SB * ST
PAST = 16384
DFF = 2816
NJ = DFF // 128
INW = 6400
WINS = (128, 512, 2048)
DILS = (1, 4, 16)
EPS = 1e-6
NCORES = 8


class Sched:
    def __init__(self, nc, es):
        self.nc = nc
        self.es = es
        self.engs = {'pe': None, 'act': None, 'dve': None, 'pool': None, 'sp': None}
        self.ops = {e: [] for e in self.engs}
        self.cnt = {e: 0 for e in self.engs}
        self.sems = {e: es.enter_context(nc.semaphore('s_' + e)) for e in self.engs}
        self.waited = {e: {} for e in self.engs}
        self.res = {}
        self.dcnt = {}
        self.pending = {e: [] for e in self.engs}

    def _dsem(self, name):
        if name not in self.sems:
            self.sems[name] = self.es.enter_context(self.nc.semaphore('d_' + name))
            self.dcnt[name] = 0
        return self.sems[name]

    def _need(self, eng, tok, waits):
        key, val = tok
        if key == 'pe' and eng == 'pe':
            return
        if self.waited[eng].get(key, 0) >= val:
            return
        self.waited[eng][key] = val
        waits.append(tok)

    def op(self, eng, fn, reads=(), writes=(), dma=None):
        waits = []
        for tok in self.pending[eng]:
            self._need(eng, tok, waits)
        self.pending[eng] = []
        for r in reads:
            st = self.res.setdefault(r, {'w': None, 'r': []})
            if st['w'] is not None:
                self._need(eng, st['w'], waits)
        for w in writes:
            st = self.res.setdefault(w, {'w': None, 'r': []})
            if st['w'] is not None:
                self._need(eng, st['w'], waits)
            for tok in st['r']:
                self._need(eng, tok, waits)
        if dma is not None:
            self._dsem(dma)
            self.dcnt[dma] += 16
            tok = (dma, self.dcnt[dma])
            self.ops[eng].append((waits, fn, (dma, 16)))
        else:
            self.cnt[eng] += 1
            tok = (eng, self.cnt[eng])
            self.ops[eng].append((waits, fn, (eng, 1)))
        for r in reads:
            self.res[r]['r'].append(tok)
        for w in writes:
            self.res[w]['w'] = tok
            self.res[w]['r'] = []
        return tok

    def barrier(self):
        toks = [(e, c) for e, c in self.cnt.items() if c > 0 and e != 'sp']
        toks += [(d, c) for d, c in self.dcnt.items() if c > 0]
        for e in self.engs:
            self.pending[e] = list(toks)

    def emit(self, block):
        nc = self.nc
        final = [(d, c) for d, c in self.dcnt.items() if c > 0]
        final += [(e, c) for e, c in self.cnt.items() if c > 0 and e != 'sp']

        def run(name, handle, tail=False):
            for waits, fn, inc in self.ops[name]:
                for key, val in waits:
                    handle.wait_ge(self.sems[key], val)
                ins = fn(handle)
                ins.then_inc(self.sems[inc[0]], inc[1])
            if tail:
                for key, val in final:
                    handle.wait_ge(self.sems[key], val)

        block.sync(lambda e: run('sp', e, True))
        block.scalar(lambda e: run('act', e))
        block.vector(lambda e: run('dve', e))
        block.gpsimd(lambda e: run('pool', e))
        block.tensor(lambda e: run('pe', e))


def build():
    nc = bass.Bass("TRN2", target_bir_lowering=False)

    def din(name, shape):
        return nc.dram_tensor(name, list(shape), F32, kind="ExternalInput").ap()

    def dout(name, shape):
        return nc.dram_tensor(name, list(shape), F32, kind="ExternalOutput").ap()

    xp = din("xp", [NB, T, D]); xs = din("xs", [TS, D])
    sconv = din("sconv", [SB * 3, D]); sh = din("sh", [SB, D])
    ck = [din("ck%d" % g, [SB, WINS[g], 256]) for g in range(3)]
    cv = [din("cv%d" % g, [SB, WINS[g], 256]) for g in range(3)]
    w_in = din("w_in", [D, INW]); w_a = din("w_a", [D, D]); w_b = din("w_b", [256, D])
    w_out = din("w_out", [D, D]); w_f1 = din("w_f1", [D, 2 * DFF]); w_f2 = din("w_f2", [DFF, D])
    vecs = din("vecs", [128, 96]); gfb_d = din("gfb", [128, D]); gwbd_d = din("gwbd", [128, 2 * 8 * 128])
    rope_d = din("rope", [3, 128, 2 * 16 * 32]); ropes_d = din("ropes", [TS, 64])
    cmask_d = din("cmask", [128, 872]); ident_d = din("ident", [128, 128])

    y_p = dout("y_p", [NB, T, D]); y_s = dout("y_s", [TS, D])
    p_conv = dout("p_conv", [NB, 3 * D]); p_h = dout("p_h", [NB, D])
    p_k = [dout("p_k%d" % g, [NB, WINS[g], 256]) for g in range(3)]
    p_v = [dout("p_v%d" % g, [NB, WINS[g], 256]) for g in range(3)]
    s_conv = dout("s_conv", [SB * 3 * D]); s_h = dout("s_h", [SB * D])
    s_k = [dout("s_k%d" % g, [SB, WINS[g], 256]) for g in range(3)]
    s_v = [dout("s_v%d" % g, [SB, WINS[g], 256]) for g in range(3)]

    es = ExitStack()
    with es:
        def sb(name, shape, dt=F32):
            return es.enter_context(nc.sbuf_tensor(name, list(shape), dt))

        S = Sched(nc, es)
        NTMAX = T + TS
        A_XNT = 0
        A_OT = 33280
        A_X = A_OT + 16640
        ARENA = A_X + 83520
        arena = sb("arena", [128, ARENA // 4], F32)

        def aview(off, nbytes, dt, pat=None, parts=128, **kw):
            v = arena[0:parts, off // 4:(off + nbytes) // 4]
            if dt is BF16:
                v = v.bitcast(BF16)
            if pat:
                v = v.rearrange(pat, **kw)
            return v

        xnT = aview(A_XNT, 33280, BF16, "p (k n) -> p k n", k=8)
        oT = aview(A_OT, 16640, BF16, "p (h n) -> p h n", h=4)
        QKT = aview(A_X, 49152, BF16, "p (g i n) -> p g i n", g=3, i=4)
        Vt = aview(A_X + 49152, 24960, BF16, "p (t g c) -> p t g c", t=16, g=3)
        RW = 2096
        r_xp = aview(A_X, RW * 4, F32)
        r_xc = aview(A_X + RW * 4, RW * 4, F32)
        r_ra = aview(A_X + 2 * RW * 4, RW * 4, F32)
        r_iu = aview(A_X + 3 * RW * 4, RW * 4, F32)
        r_e2 = aview(A_X + 4 * RW * 4, RW * 4, F32)
        r_xcb = aview(A_X + 5 * RW * 4, 4160, BF16)
        r_gg = aview(A_X + 5 * RW * 4 + 4160, 4160, BF16)
        assert 5 * RW * 4 + 8320 <= 50240
        ghsT = aview(A_X + 50240, 33280, BF16, "p (k n) -> p k n", k=8)
        mT = aview(A_X, 33280, BF16, "p (k n) -> p k n", k=8)
        hb1 = aview(A_XNT, 32768, F32, "p (t n) -> p t n", t=8)
        hb2 = aview(A_X + 33280, 36864, F32, "p (t n) -> p t n", t=9)
        woutT = aview(A_OT, 16384, BF16, "p (k n) -> p k n", k=8)
        actT = [aview(A_OT, 12480, BF16, "p (j n) -> p j n", j=3),
                aview(A_X + 70144, 12480, BF16, "p (j n) -> p j n", j=3)]

        def hbuf(ti):
            return hb1[:, ti, :] if ti < 8 else hb2[:, ti - 8, :]

        vec = sb("vec", [128, 96])
        gfb = aview(A_OT + 12480, 4096, F32)
        gwbd = sb("gwbds", [128, 2, 8, 128], BF16)
        identf = sb("identf", [128, 128]); identb = sb("identb", [128, 128], BF16)
        ones = sb("ones", [128, 64], BF16)
        onesf = sb("onesf", [128, 64])
        cmb = sb("cmb", [128, 872], BF16)
        misc = sb("misc", [128, 6656])
        ropeT = misc[:, 4608:5632].rearrange("p (a t c) -> p a t c", a=2, t=16)
        ropeS = sb("ropeS", [TS, 2, 32])
        cA = sb("cA", [128, 8]); cA2 = sb("cA2", [128, 8])
        stg = [sb("stg%d" % i, [128, 1024]) for i in range(2)]
        NWB = 8
        wbf = [sb("wbf%d" % i, [128, 1024], BF16) for i in range(NWB)]
        xtbig = sb("xtbig", [128, 2100])
        xt = [xtbig[:, i * D:(i + 1) * D] for i in range(2)]
        xsb = [sb("xsb%d" % i, [128, D], BF16) for i in range(2)]
        ss = [sb("ss%d" % i, [128, 1]) for i in range(2)]
        rs = [sb("rs%d" % i, [128, 1]) for i in range(2)]
        qkr = [misc[:, 2048 + i * 512:2048 + (i + 1) * 512] for i in range(2)]
        tm1 = misc[:, 3072:3328].rearrange("p (h c) -> p h c", h=8)
        tm2 = misc[:, 3328:3584].rearrange("p (h c) -> p h c", h=8)
        qkb = [misc[:, 3584 + i * 256:3584 + (i + 1) * 256].bitcast(BF16) for i in range(2)]
        vf = [misc[:, 4096 + i * 256:4096 + (i + 1) * 256] for i in range(2)]
        PT = [misc[:, 6144 + i * 128:6144 + (i + 1) * 128].bitcast(BF16) for i in range(4)]
        rec = misc[:, 5632:6144]
        sg = [misc[:, i * 512:(i + 1) * 512] for i in range(4)]
        rabuf_alt = misc[:, 0:RW]
        iubuf_alt = misc[:, RW:2 * RW]
        PTx = PT + [sg[i][:, j * 128:(j + 1) * 128].bitcast(BF16) for i in range(2) for j in range(4)]
        fin = sb("fin", [128, 128])
        fins = sb("fins", [128, 128])
        scT = sb("scT", [128, 8, 12]); h0T = sb("h0T", [128, 8, 4])
        xps = sb("xps", [128, 4, 11])
        sctm = aview(A_X + 74240, 4096, F32)
        fint = sb("fint", [32, 128])
        QKTs = sb("QKTs", [128, 3, 4, TS], BF16)
        Vs = sb("Vs", [TS, 3, 256], BF16)
        kc = [sg[2][:, i * 256:(i + 1) * 256] for i in range(2)]
        vc = [sg[3][:, i * 256:(i + 1) * 256] for i in range(2)]
        vcb = [sb("vcb%d" % i, [128, 256], BF16) for i in range(3)]
        kcT = [sb("kcT%d" % i, [128, 2, 128], BF16) for i in range(2)]
        PTs = [sb("PTs%d" % i, [128, 4, 8], BF16) for i in range(2)]
        PTn = [sb("PTn%d" % i, [TS, TS], BF16) for i in range(2)]
        oTs = sb("oTs", [64, 4, TS], BF16)

        psum = es.enter_context(nc.psum_tensor("psum", [128, 8 * 512], F32))

        def bank(i):
            return psum[:, i * 512:(i + 1) * 512]

        def bankb(i):
            return psum[:, i * 512:(i + 1) * 512].bitcast(BF16)

        bctr = [0]

        nbanks = [8]

        def nb():
            bctr[0] = (bctr[0] + 1) % nbanks[0]
            return bctr[0]

        def nb2():
            b = ((bctr[0] // 2 + 1) % 4) * 2
            bctr[0] = b + 1
            return b

        ctr = {'stg': 0, 'wbf': 0, 'x': 0, 'q': 0, 'pt': 0, 'sgm': 0, 'sgf': 0, 'k': 0, 'kn': 0, 'cast': 0, 'sb': 0, 'ob': 0, 'tb': 0, 'pb': 0, 'sbs': 0, 'k2': 0, 'v3': 0, 'p2': 0, 'sb3': 0, 'ptx': 0}

        def rr(name, n):
            v = ctr[name]
            ctr[name] = (v + 1) % n
            return v

        def dma(out, in_, reads=(), writes=(), sem=None):
            S.op('sp', lambda e, o=out, i=in_: e.dma_start(out=o, in_=i), reads=reads, writes=writes, dma=sem)

        def wload(srcs, n, parts=128, dest=None, dres=None):
            s = rr('stg', 2)
            for off, ap in srcs:
                sz = 1
                for d_ in ap.shape[1:]:
                    sz *= d_
                o = stg[s][0:ap.shape[0], off:off + sz]
                if len(ap.shape) == 3:
                    o = o.rearrange("p (a b) -> p a b", a=ap.shape[1])
                dma(o, ap, writes=[('stg', s)], sem='stg%d' % s)
            if dest is None:
                w = rr('wbf', NWB)
                dest = wbf[w][0:parts, 0:n]
                dres = ('wbf', w)
            ce = rr('cast', 2)
            if ce == 0:
                S.op('dve', lambda e, o=dest, i=stg[s][0:parts, 0:n]: e.tensor_copy(out=o, in_=i), reads=[('stg', s)], writes=[dres])
            else:
                S.op('act', lambda e, o=dest, i=stg[s][0:parts, 0:n]: e.copy(out=o, in_=i), reads=[('stg', s)], writes=[dres])
            return dest, dres

        def mm(out, pairs, first=True, last=True, skip=False):
            def fn(e):
                ins = None
                n = len(pairs)
                for i, (l, r) in enumerate(pairs):
                    ins = e.matmul(out, lhsT=l, rhs=r, start=(first and i == 0), stop=(last and i == n - 1),
                                   skip_group_check=skip)
                return ins
            return fn

        win3 = w_in.rearrange("(k p) n -> p k n", p=128)
        wa3 = w_a.rearrange("(k p) n -> p k n", p=128)
        wo3 = w_out.rearrange("(k p) n -> p k n", p=128)
        wf13 = w_f1.rearrange("(k p) n -> p k n", p=128)
        wb3 = w_b.rearrange("(h p) n -> p h n", p=64)

        dma(vec[:], vecs[:, :], writes=['vec'], sem='c_vec')
        dma(identf[:], ident_d[:, :], writes=['identf'], sem='c_id')
        cmf = xt[0][:, 0:872]
        dma(cmf, cmask_d[:, :], writes=[('xt', 0)], sem='c_cm')
        dma(ropeS[:].rearrange("p a b -> p (a b)"), ropes_d[:, :], writes=['ropeS'], sem='c_rs')
        S.op('pool', lambda e: e.tensor_copy(out=identb[:], in_=identf[:]), reads=['identf'], writes=['identb'])
        S.op('pool', lambda e: e.tensor_copy(out=cmb[:], in_=cmf), reads=[('xt', 0)], writes=['cmb'])
        S.op('pool', lambda e: e.memset(ones[:], 1.0), writes=['ones'])
        S.op('pool', lambda e: e.memset(onesf[:], 1.0), writes=['onesf'])
        for h_ in range(2):
            for q_ in range(2):
                wload([(0, gwbd_d[:, (h_ * 2 + q_) * 512:(h_ * 2 + q_ + 1) * 512])], 512,
                      dest=gwbd[:, h_, q_ * 4:(q_ + 1) * 4, :].rearrange("p a b -> p (a b)"), dres='gwbd')
        S.op('act', lambda e: e.activation(out=cA[:], in_=vec[:, 88:96], func=AF.Sigmoid), reads=['vec'], writes=['cA'])
        S.op('act', lambda e: e.activation(out=cA[:], in_=cA[:], func=AF.Ln), reads=['cA'], writes=['cA'])
        S.op('act', lambda e: e.mul(out=cA2[:], in_=cA[:], mul=16.0), reads=['cA'], writes=['cA2'])
        S.op('act', lambda e: e.mul(out=cA[:], in_=cA[:], mul=8.0), reads=['cA', 'cA2'], writes=['cA'])
        G1, G2, BM, CW, CB, GAB, GXB = 0, 8, 16, 32, 64, 72, 80
        mask2 = cmb[:, 0:256]
        maskg2 = cmb[:, 256:384]
        maskC = cmb[:, 384:392]
        maskN = cmb[0:TS, 392:488]
        nm2 = cmb[:, 488:744]
        nmg2 = cmb[:, 744:872]

        def pipeline(iters, skew, group=1):
            n = len(iters)
            nst = n + max(skew)
            for s0_ in range(0, nst, group):
                for k, sk in enumerate(skew):
                    for st_ in range(s0_, min(s0_ + group, nst)):
                        t_ = st_ - sk
                        if 0 <= t_ < n and iters[t_][k] is not None:
                            iters[t_][k]()

        class Pipe:
            def __init__(self, iters, skew, desc=False):
                self.iters, self.skew, self.st = iters, skew, 0
                self.nsteps = len(iters) + max(skew)
                self.order = list(range(len(skew)))
                if desc:
                    self.order.reverse()

            def step(self, n=1):
                for _ in range(n):
                    if self.st >= self.nsteps:
                        return
                    for k in self.order:
                        t_ = self.st - self.skew[k]
                        if 0 <= t_ < len(self.iters) and self.iters[t_][k] is not None:
                            self.iters[t_][k]()
                    self.st += 1

            def finish(self):
                self.step(self.nsteps)

        def rms_stages(src_tile, np_, dstT, col0, gcol, slot, src_res, dst_res):
            stt = {}

            def B():
                S.op('act', lambda e: e.activation(out=xsb[slot][0:np_, :], in_=src_tile, func=AF.Square, accum_out=ss[slot][0:np_, :]),
                     reads=[src_res], writes=[('xsb', slot), ('ss', slot)])
                S.op('act', lambda e: e.activation(out=rs[slot][0:np_, :], in_=ss[slot][0:np_, :], func=AF.Sqrt, scale=1.0 / D, bias=EPS),
                     reads=[('ss', slot)], writes=[('rs', slot)])
                S.op('dve', lambda e: e.reciprocal(out=rs[slot][0:np_, :], in_=rs[slot][0:np_, :]), reads=[('rs', slot)], writes=[('rs', slot)])
                S.op('act', lambda e: e.mul(out=xsb[slot][0:np_, :], in_=src_tile, mul=rs[slot][0:np_, 0:1]),
                     reads=[src_res, ('rs', slot)], writes=[('xsb', slot)])

            def C():
                b = 6 + rr('tb', 2)
                stt['b'] = b

                def tr(e):
                    ins = None
                    for k in range(8):
                        ins = e.transpose(bankb(b)[:, k * 128:k * 128 + np_], xsb[slot][0:np_, k * 128:(k + 1) * 128], identb[0:np_, 0:np_])
                    return ins
                S.op('pe', tr, reads=[('xsb', slot), 'identb'], writes=[('ps', b)])

            def Dd():
                b = stt['b']
                S.op('dve', lambda e: e.tensor_tensor(
                    out=dstT[:, :, col0:col0 + np_],
                    in0=bankb(b).rearrange("p (k n) -> p k n", k=8)[:, :, 0:np_],
                    in1=vec[:, gcol:gcol + 8].unsqueeze(2).to_broadcast([128, 8, np_]), op=ALU.mult),
                    reads=[('ps', b), 'vec'], writes=[dst_res])
            return B, C, Dd

        p3next = [{}]
        for ps_ in range(NB):
            has_s = (ps_ == NB - 1)
            NT = T + (TS if has_s else 0)
            segs = [(i * 512, 512) for i in range(4)] + ([(T, TS)] if has_s else [])
            tiles = [(i * 128, 128) for i in range(16)] + ([(T, TS)] if has_s else [])
            S.barrier()
            its = []
            for ti, (c0, np_) in enumerate(tiles):
                sl = ti % 2
                src = xp[ps_, c0:c0 + 128, :] if ti < 16 else xs[:, :]
                A_ = (lambda sl=sl, np_=np_, src=src: dma(xt[sl][0:np_, :], src, writes=[('xt', sl)], sem='xt%d' % sl))
                B_, C_, D_ = rms_stages(xt[sl][0:np_, :], np_, xnT, c0, G1, sl, ('xt', sl), ('xnT', ti))
                its.append((A_, B_, C_, D_))
            p1its = its

            def sample_states():
                dma(sctm[0:12, :], sconv[:, :], writes=['sctm'], sem='c1')
                dma(sctm[12:16, :], sh[:, :], writes=['sctm'], sem='c1')
                b = nb()

                def trs(e, b=b):
                    ins = None
                    for k in range(8):
                        ins = e.transpose(bank(b)[:, k * 16:(k + 1) * 16], sctm[0:16, k * 128:(k + 1) * 128], identf[0:16, 0:16])
                    return ins
                S.op('pe', trs, reads=['sctm', 'identf'], writes=[('ps', b)])
                S.op('dve', lambda e, b=b: e.tensor_copy(out=scT[:], in_=bank(b)[:, 0:128].rearrange("p (k n) -> p k n", k=8)[:, :, 0:12]),
                     reads=[('ps', b)], writes=['scT'])
                S.op('dve', lambda e, b=b: e.tensor_copy(out=h0T[:], in_=bank(b)[:, 0:128].rearrange("p (k n) -> p k n", k=8)[:, :, 12:16]),
                     reads=[('ps', b)], writes=['h0T'])
            xn_all = [('xnT', i) for i in range(len(tiles))]

            p3pre = p3next[0]
            p3next[0] = {}

            def p3a_w(g, which, hf):
                if (g, which, hf) not in p3pre:
                    c_ = 2048 + which * 768 + g * 256
                    p3pre[(g, which, hf)] = wload([(0, win3[:, hf * 4:(hf + 1) * 4, c_:c_ + 256])], 1024)
                return p3pre[(g, which, hf)]

            def p3a_iters(g):
                grp = {}

                def ensure_w(g=g, grp=grp):
                    if 'rhsl' in grp:
                        return
                    wqs = []
                    for which in range(3):
                        halves = []
                        for hf in range(2):
                            halves.append(p3a_w(g, which, hf))
                        wqs.append(halves)
                    if g + 1 < 3:
                        p3a_w(g + 1, 0, 0)
                        p3a_w(g + 1, 0, 1)

                    def wsl(which, k):
                        t_, _ = wqs[which][k // 4]
                        return t_.rearrange("p (k n) -> p k n", k=4)[:, k % 4, :]
                    grp['wres'] = [r_ for hv in wqs for (_, r_) in hv]
                    grp['rhsl'] = [(wsl(0, k), wsl(1, k), wsl(2, k)) for k in range(8)]
                ntl = 16 + (1 if has_s else 0)
                its = []
                for tl in range(ntl):
                    if tl < 16:
                        np_ = 128
                        if g == 0:
                            colsel = slice(tl * 128, (tl + 1) * 128)
                        elif g == 1:
                            r_, n_ = tl // 4, tl % 4
                            colsel = slice(512 * n_ + r_, 512 * n_ + r_ + 512, 4)
                        else:
                            colsel = slice(tl, T, 16)
                        cosb = ropeT[:, 0, tl, :]
                        sinb = ropeT[:, 1, tl, :]
                    else:
                        np_ = TS
                        colsel = slice(T, T + TS)
                        cosb = ropeS[:, 0, :]
                        sinb = ropeS[:, 1, :]
                    stt = {}

                    def A_(stt=stt, colsel=colsel, np_=np_, tl=tl, g=g, grp=grp, ensure_w=ensure_w):
                        ensure_w()
                        rhsl, wres = grp['rhsl'], grp['wres']
                        bA = rr('pb', 3) * 2
                        bB = bA + 1
                        stt['bA'], stt['bB'] = bA, bB

                        def qkv(e):
                            ins = None
                            for k in range(8):
                                l = xnT[:, k, colsel]
                                e.matmul(bank(bA)[0:np_, 0:256], lhsT=l, rhs=rhsl[k][0], start=(k == 0), stop=(k == 7), skip_group_check=True)
                                e.matmul(bank(bA)[0:np_, 256:512], lhsT=l, rhs=rhsl[k][1], start=False, stop=(k == 7), skip_group_check=True)
                                ins = e.matmul(bank(bB)[0:np_, 0:256], lhsT=l, rhs=rhsl[k][2], start=(k == 0), stop=(k == 7))
                            return ins
                        if tl >= 16:
                            xr_ = [('xnT', 16)]
                        elif g == 0:
                            xr_ = [('xnT', tl)]
                        elif g == 1:
                            xr_ = [('xnT', 4 * (tl % 4) + i_) for i_ in range(4)]
                        else:
                            xr_ = [('xnT', i_) for i_ in range(16)]
                        S.op('pe', qkv, reads=xr_ + wres, writes=[('ps', bA), ('ps', bB)])

                    def B_(stt=stt, np_=np_, cosb=cosb, sinb=sinb, tl=tl, g=g):
                        if tl == 0:
                            dma(ropeT[:].rearrange("p a t c -> p (a t c)"), rope_d[g, :, :], writes=['ropeT'], sem='rope')
                        bA, bB = stt['bA'], stt['bB']
                        qs = rr('q', 2)
                        stt['qs'] = qs
                        x3 = bank(bA)[0:np_, :].rearrange("p (h c) -> p h c", h=8)
                        o3 = qkr[qs][0:np_, :].rearrange("p (h c) -> p h c", h=8)
                        cb_ = cosb[0:np_].unsqueeze(1).to_broadcast([np_, 8, 32])
                        sb_ = sinb[0:np_].unsqueeze(1).to_broadcast([np_, 8, 32])
                        t1 = tm1[0:np_]
                        t2 = tm2[0:np_]
                        rd = [('ps', bA), 'ropeT', 'ropeS']
                        S.op('dve', lambda e: e.tensor_tensor(out=t1, in0=x3[:, :, 0:32], in1=cb_, op=ALU.mult), reads=rd, writes=['tm1'])
                        S.op('dve', lambda e: e.tensor_tensor(out=t2, in0=x3[:, :, 32:64], in1=sb_, op=ALU.mult), reads=rd, writes=['tm2'])
                        S.op('dve', lambda e: e.tensor_tensor(out=o3[:, :, 0:32], in0=t1, in1=t2, op=ALU.subtract),
                             reads=['tm1', 'tm2'], writes=[('qkr', qs)])
                        S.op('dve', lambda e: e.tensor_tensor(out=t1, in0=x3[:, :, 32:64], in1=cb_, op=ALU.mult), reads=rd, writes=['tm1'])
                        S.op('dve', lambda e: e.tensor_tensor(out=t2, in0=x3[:, :, 0:32], in1=sb_, op=ALU.mult), reads=rd, writes=['tm2'])
                        S.op('dve', lambda e: e.tensor_tensor(out=o3[:, :, 32:64], in0=t1, in1=t2, op=ALU.add),
                             reads=['tm1', 'tm2'], writes=[('qkr', qs)])
                        S.op('act', lambda e: e.copy(out=qkb[qs][0:np_, :], in_=qkr[qs][0:np_, :]), reads=[('qkr', qs)], writes=[('qkb', qs)])
                        S.op('act', lambda e: e.copy(out=vf[qs][0:np_, :], in_=bank(bB)[0:np_, 0:256]), reads=[('ps', bB)], writes=[('vf', qs)])
                        W = WINS[g]
                        if tl < 16:
                            dst = None
                            if g == 0 and tl == 15:
                                dst = lambda o: o[ps_, 0:128, :]
                            elif g == 1 and tl % 4 == 3:
                                dst = lambda o: o[ps_, (tl // 4):512:4, :]
                            elif g == 2:
                                dst = lambda o: o[ps_, tl:T:16, :]
                            if dst is not None:
                                dma(dst(p_k[g]), qkr[qs][:, 256:512], reads=[('qkr', qs)], sem='qkr%d' % qs)
                                dma(dst(p_v[g]), vf[qs][:, :], reads=[('vf', qs)], sem='vf%d' % qs)
                        else:
                            for b_ in range(SB):
                                dma(s_k[g][b_, W - ST:W, :], qkr[qs][b_ * ST:(b_ + 1) * ST, 256:512], reads=[('qkr', qs)], sem='qkr%d' % qs)
                                dma(s_v[g][b_, W - ST:W, :], vf[qs][b_ * ST:(b_ + 1) * ST, :], reads=[('vf', qs)], sem='vf%d' % qs)

                    def C_(stt=stt, np_=np_):
                        qs = stt['qs']
                        bC = 6 + rr('tb', 2)
                        stt['bC'] = bC

                        def trq(e):
                            ins = None
                            for i in range(4):
                                ins = e.transpose(bankb(bC)[:, i * 128:i * 128 + np_], qkb[qs][0:np_, i * 128:(i + 1) * 128], identb[0:np_, 0:np_])
                            return ins
                        S.op('pe', trq, reads=[('qkb', qs), 'identb'], writes=[('ps', bC)])

                    def D_(stt=stt, np_=np_, tl=tl, g=g):
                        qs, bC = stt['qs'], stt['bC']
                        srcT = bankb(bC)[:, 0:512].rearrange("p (i n) -> p i n", i=4)[:, :, 0:np_]
                        if tl < 16:
                            S.op('act', lambda e: e.copy(out=QKT[:, g, :, tl * 128:(tl + 1) * 128], in_=srcT),
                                 reads=[('ps', bC)], writes=[('QKT', g, tl)])
                            S.op('pool', lambda e: e.tensor_copy(out=Vt[:, tl, g, :].rearrange("p (h c) -> p h c", c=65)[:, :, 0:64],
                                                                 in_=vf[qs][:, :].rearrange("p (h c) -> p h c", c=64)),
                                 reads=[('vf', qs)], writes=[('Vt', g, tl)])
                        else:
                            S.op('act', lambda e: e.copy(out=QKTs[:, g, :, :], in_=srcT), reads=[('ps', bC)], writes=['QKTs'])
                            S.op('pool', lambda e: e.tensor_copy(out=Vs[:, g, :], in_=vf[qs][0:TS, :]), reads=[('vf', qs)], writes=['Vs'])
                    its.append((A_, B_, C_, D_))
                return its

            for t_ in range(2):
                p1its[t_][0]()
                p1its[t_] = (None,) + tuple(p1its[t_][1:])
            for which in range(3):
                for hf in range(2):
                    p3a_w(0, which, hf)
            g0its = p3a_iters(0)
            allits = [a_ + b_ for a_, b_ in zip(p1its, g0its)]
            for g in (1, 2):
                allits += [(None, None, None, None) + tuple(it_) for it_ in p3a_iters(g)]
            pipeline(allits, [0, 0, 1, 1, 2, 3, 4, 4])
            if has_s:
                sample_states()

            S.op('pool', lambda e: e.memset(Vt[:, :, :, :].rearrange("p t g (h c) -> p (t g h) c", c=65)[:, :, 64:65], 1.0), writes=['Vt1'])
            its = []
            deferred = []
            pe_deferred = []

            def flush_fin():
                while pe_deferred:
                    pe_deferred.pop(0)[1]()
                while deferred:
                    deferred.pop(0)[1]()
            for hh in range(4):
                ch, pb = hh // 2, 64 * (hh % 2)
                units = []
                for r_ in range(4):
                    for n_ in range(4):
                        outs = [(n_, slice(r_, 512, 4), slice(0, 128))]
                        if n_ < 3:
                            outs.append((n_ + 1, slice(r_, 512, 4), slice(128, 256)))
                        units.append((1, r_ * 4 + n_, 256 if n_ < 3 else 128, mask2, outs, None))
                for r_ in range(16):
                    units.append((2, r_, 128, mask2[:, 0:128], [(q_, slice(r_, 512, 16), slice(32 * q_, 32 * q_ + 32)) for q_ in range(4)], None))
                for n_ in range(16):
                    q_, m_ = n_ // 4, n_ % 4
                    if m_ < 3:
                        outs = [(q_, slice(m_ * 128, m_ * 128 + 256), slice(0, 256))]
                    else:
                        outs = [(q_, slice(384, 512), slice(0, 128))]
                        if n_ < 15:
                            outs.append((q_ + 1, slice(0, 128), slice(128, 256)))
                    units.append((0, n_, 256 if n_ < 15 else 128, mask2, outs, q_ if m_ == 3 else None))
                started = set()
                for (g, tk, nq, msk, outs, finq) in units:
                    stt = {}
                    flags = []
                    for (bq, oc, pc) in outs:
                        flags.append(bq not in started)
                        started.add(bq)

                    def A_(stt=stt, g=g, tk=tk, nq=nq, ch=ch, pb=pb):
                        bS = 4 + rr('sb3', 3)
                        stt['bS'] = bS
                        QTq = QKT[pb:pb + 64, g, ch, tk * 128:tk * 128 + nq]
                        KTc = QKT[pb:pb + 64, g, 2 + ch, tk * 128:(tk + 1) * 128]
                        rdq = [('QKT', g, tk)] + ([('QKT', g, tk + 1)] if nq > 128 else [])
                        S.op('pe', lambda e: e.matmul(bank(bS)[:, 0:nq], lhsT=KTc, rhs=QTq, start=True, stop=True), reads=rdq, writes=[('ps', bS)])

                    def B_(stt=stt, nq=nq, msk=msk):
                        bS = stt['bS']
                        pt = rr('ptx', 12)
                        stt['pt'] = pt
                        S.op('act', lambda e: e.activation(out=PTx[pt][:, 0:nq], in_=bank(bS)[:, 0:nq], func=AF.Exp, scale=0.125),
                             reads=[('ps', bS)], writes=[('PTx', pt)])
                        S.op('dve', lambda e: e.tensor_tensor(out=PTx[pt][:, 0:nq], in0=PTx[pt][:, 0:nq], in1=msk[:, 0:nq], op=ALU.mult),
                             reads=[('PTx', pt), 'cmb'], writes=[('PTx', pt)])
                        if deferred:
                            deferred.pop(0)[1]()

                    def C_(stt=stt, g=g, tk=tk, outs=outs, flags=flags, finq=finq, hh=hh):
                        pt = stt['pt']
                        lhs = Vt[:, tk, g, hh * 65:(hh + 1) * 65]
                        newb = {bq for (bq, _, _), fl in zip(outs, flags) if fl}
                        if newb & ({b_ for b_, _ in deferred} | {b_ for b_, _ in pe_deferred}):
                            flush_fin()

                        def pv(e):
                            ins = None
                            for (bq, oc, pc), fl in zip(outs, flags):
                                ins = e.matmul(bank(bq)[0:65, oc], lhsT=lhs, rhs=PTx[pt][:, pc], start=fl, stop=False, skip_group_check=True)
                            return ins
                        S.op('pe', pv, reads=[('PTx', pt), ('Vt', g, tk), 'Vt1'], writes=[('ps', bq) for (bq, _, _) in outs])
                        if finq is not None:
                            flush_fin()
                        elif pe_deferred:
                            pe_deferred.pop(0)[1]()
                        if finq is not None:
                            bO = finq
                            bD = 7
                            S.op('act', lambda e: e.copy(out=rec[64:65, :], in_=bank(bO)[64:65, :]), reads=[('ps', bO)], writes=['recd'])

                            def bcast(bO=bO, bD=bD, hh=hh):
                                S.op('pe', lambda e: e.matmul(bank(bD)[0:64, :], lhsT=onesf[64:65, 0:64], rhs=rec[64:65, :], start=True, stop=True),
                                     reads=['recd', 'onesf'], writes=[('ps', bD)])
                                for pc_ in range(4):
                                    def piece(pc_=pc_):
                                        cs = slice(pc_ * 128, (pc_ + 1) * 128)
                                        S.op('dve', lambda e: e.reciprocal(out=rec[0:64, cs], in_=bank(bD)[0:64, cs]), reads=[('ps', bD)], writes=[('rec', pc_)])
                                        S.op('dve', lambda e: e.tensor_tensor(out=oT[0:64, hh, bO * 512 + pc_ * 128:bO * 512 + (pc_ + 1) * 128],
                                                                              in0=bank(bO)[0:64, cs], in1=rec[0:64, cs], op=ALU.mult),
                                             reads=[('ps', bO), ('rec', pc_)], writes=[('oT', bO)])
                                    deferred.append((bO, piece))
                            pe_deferred.append((bO, bcast))
                    its.append((A_, B_, C_))
            pipeline(its, [0, 1, 8], group=2)
            flush_fin()

            p3c = None
            if has_s:
                for g in range(3):
                    W = WINS[g]
                    fl = lambda ap: ap.rearrange("r c -> (r c)").rearrange("(a x) -> a x", x=2048)
                    for b_ in range(SB):
                        dma(fl(s_k[g][b_, 0:W - ST, :]), fl(ck[g][b_, ST:W, :]), sem='cpy')
                        dma(fl(s_v[g][b_, 0:W - ST, :]), fl(cv[g][b_, ST:W, :]), sem='cpy')
                bO = 7
                Osb = bank(bO)[:, 0:128].rearrange("p (h n) -> p h n", h=4)
                its = []
                firstO = True
                for g in range(3):
                    for hh in range(4):
                        stt = {}

                        def N1(stt=stt, g=g, hh=hh):
                            ch, pb = hh // 2, 64 * (hh % 2)
                            bS = 5 + (hh % 2)
                            stt['bS'] = bS
                            S.op('pe', lambda e: e.matmul(bank(bS)[0:TS, 0:TS], lhsT=QKTs[pb:pb + 64, g, 2 + ch, :],
                                                          rhs=QKTs[pb:pb + 64, g, ch, :], start=True, stop=True),
                                 reads=['QKTs'], writes=[('ps', bS)])

                        def N2(stt=stt, g=g, hh=hh, firstO=firstO):
                            bS = stt['bS']
                            pn = rr('kn', 2)
                            S.op('act', lambda e: e.activation(out=PTn[pn][:, :], in_=bank(bS)[0:TS, 0:TS], func=AF.Exp, scale=0.125),
                                 reads=[('ps', bS)], writes=[('PTn', pn)])
                            S.op('pool', lambda e: e.tensor_tensor(out=PTn[pn][:, :], in0=PTn[pn][:, :], in1=maskN[:, g * 32:(g + 1) * 32], op=ALU.mult),
                                 reads=[('PTn', pn), 'cmb'], writes=[('PTn', pn)])

                            def pvn(e):
                                e.matmul(Osb[0:64, hh, :], lhsT=Vs[:, g, hh * 64:(hh + 1) * 64], rhs=PTn[pn][:, :], start=firstO, stop=False, skip_group_check=True)
                                return e.matmul(Osb[64:128, hh, :], lhsT=ones[0:TS, :], rhs=PTn[pn][:, :], start=firstO, stop=False, skip_group_check=True)
                            S.op('pe', pvn, reads=[('PTn', pn), 'Vs', 'ones'], writes=[('ps', bO)])
                        its.append((N1, N2, None, None, None))
                        firstO = False
                for b_ in range(SB):
                    for g in range(3):
                        d = DILS[g]
                        ncls = min(d, ST)
                        nq = max(ST // d, 1)
                        for r_ in range(ncls):
                            stt = {}
                            qcols = slice(b_ * ST + r_, (b_ + 1) * ST, d)

                            def U0(stt=stt, g=g, b_=b_, r_=r_, d=d):
                                ks = rr('k', 2)
                                stt['ks'] = ks
                                dma(kc[ks][:, :], ck[g][b_, r_:WINS[g]:d, :], writes=[('kc', ks)], sem='kc%d' % ks)
                                dma(vc[ks][:, :], cv[g][b_, r_:WINS[g]:d, :], writes=[('vc', ks)], sem='vc%d' % ks)

                            def U1(stt=stt):
                                ks = stt['ks']
                                bT = 4
                                stt['bT'] = bT

                                def trk(e):
                                    e.transpose(bank(bT)[:, 0:128], kc[ks][:, 0:128], identf[:, :])
                                    return e.transpose(bank(bT)[:, 128:256], kc[ks][:, 128:256], identf[:, :])
                                S.op('pe', trk, reads=[('kc', ks), 'identf'], writes=[('ps', bT)])

                            def U2(stt=stt):
                                ks, bT = stt['ks'], stt['bT']
                                k2 = rr('k2', 2)
                                v3 = rr('v3', 3)
                                stt['k2'], stt['v3'] = k2, v3
                                S.op('act', lambda e: e.copy(out=kcT[k2][:, :, :], in_=bank(bT)[:, 0:256].rearrange("p (a n) -> p a n", a=2)),
                                     reads=[('ps', bT)], writes=[('kcT', k2)])
                                S.op('pool', lambda e: e.tensor_copy(out=vcb[v3][:, :], in_=vc[ks][:, :]), reads=[('vc', ks)], writes=[('vcb', v3)])

                            def U3(stt=stt, g=g, qcols=qcols, nq=nq):
                                k2 = stt['k2']

                                def scs(e):
                                    ins = None
                                    for hh in range(4):
                                        ch, pb = hh // 2, 64 * (hh % 2)
                                        ins = e.matmul(bank(5 + hh % 2)[:, ch * 8:ch * 8 + nq], lhsT=kcT[k2][pb:pb + 64, ch, :], rhs=QKTs[pb:pb + 64, g, ch, qcols],
                                                       start=(ch == 0), stop=True, skip_group_check=True)
                                    return ins
                                S.op('pe', scs, reads=[('kcT', k2), 'QKTs'], writes=[('ps', 5), ('ps', 6)])

                            def U4(stt=stt, g=g, qcols=qcols, nq=nq):
                                v3 = stt['v3']
                                p2 = rr('p2', 2)
                                for par in range(2):
                                    S.op('act', lambda e, par=par: e.activation(out=PTs[p2][:, par:4:2, 0:nq],
                                                                                in_=bank(5 + par)[:, 0:16].rearrange("p (h n) -> p h n", h=2)[:, :, 0:nq],
                                                                                func=AF.Exp, scale=0.125),
                                         reads=[('ps', 5 + par)], writes=[('PTs', p2)])
                                if nq > 1:
                                    S.op('pool', lambda e: e.tensor_tensor(out=PTs[p2][:, :, 0:nq], in0=PTs[p2][:, :, 0:nq],
                                                                           in1=maskC[:, 0:nq].unsqueeze(1).to_broadcast([128, 4, nq]), op=ALU.mult),
                                         reads=[('PTs', p2), 'cmb'], writes=[('PTs', p2)])

                                def pvs(e):
                                    ins = None
                                    for hh in range(4):
                                        e.matmul(Osb[0:64, hh, qcols], lhsT=vcb[v3][:, hh * 64:(hh + 1) * 64], rhs=PTs[p2][:, hh, 0:nq],
                                                 start=False, stop=False, skip_group_check=True)
                                        ins = e.matmul(Osb[64:128, hh, qcols], lhsT=ones[:, :], rhs=PTs[p2][:, hh, 0:nq],
                                                       start=False, stop=False, skip_group_check=True)
                                    return ins
                                S.op('pe', pvs, reads=[('PTs', p2), ('vcb', v3), 'ones'], writes=[('ps', bO)])
                            its.append((U0, U1, U2, U3, U4))

                def p3c_fin():
                    S.op('dve', lambda e: e.reciprocal(out=rec[64:128, 0:128], in_=bank(bO)[64:128, 0:128]), reads=[('ps', bO)], writes=['rec', 'recd'])
                    S.op('dve', lambda e: e.tensor_tensor(out=oT[0:64, :, T:T + TS], in0=Osb[0:64, :, :],
                                                          in1=rec[64:128, 0:128].rearrange("p (h n) -> p h n", h=4), op=ALU.mult),
                         reads=[('ps', bO), 'rec'], writes=[('oT', 4)])
                p3c = Pipe(its, [0, 1, 2, 3, 4], desc=True)
                p3c.finish()
                p3c_fin()
                p3c = None

            wts = {}

            def W_(c):
                wx, rx = wload([(0, win3[:, :, c * 128:(c + 1) * 128])], 1024)
                wg, rg = wload([(0, win3[:, :, 1024 + c * 128:1024 + (c + 1) * 128])], 1024)
                wts[c] = (wx.rearrange("p (k n) -> p k n", k=8), rx, wg.rearrange("p (k n) -> p k n", k=8), rg)
            W_(0)
            W_(1)
            W_(2)
            S.barrier()
            xcbuf = [r_xc, xtbig[:, 0:RW]]
            rabuf = [r_ra, rabuf_alt]
            iubuf = [r_iu, iubuf_alt]

            nsg = len(segs)
            raall = [('ra', i) for i in range(nsg)]
            iuall = [('iu', i) for i in range(nsg)]
            ggall = [('gg', i) for i in range(nsg)]
            xpall = [('xp', 'h')] + [('xp', i) for i in range(4)]

            def A1_(c):
                wx3, rx, _, _ = wts[c]
                xc_ = xcbuf[c % 2]
                XC, XCS = ('xc', c % 2), ('xcS', c % 2)
                S.op('pool', lambda e: e.memset(r_xp[:, 0:3], 0.0), writes=[('xp', 'h')])
                for si, (c0, w_) in enumerate(segs):
                    b = nb()
                    S.op('pe', mm(bank(b)[:, 0:w_], [(wx3[:, k, :], xnT[:, k, c0:c0 + w_]) for k in range(8)]), reads=xn_all + [rx], writes=[('ps', b)])
                    if si < 4:
                        S.op('dve', lambda e, b=b, c0=c0, w_=w_: e.tensor_copy(out=r_xp[:, 3 + c0:3 + c0 + w_], in_=bank(b)[:, 0:w_]), reads=[('ps', b)], writes=[('xp', si)])
                    else:
                        S.op('dve', lambda e, b=b: e.tensor_copy(out=xps[:, :, 3:11], in_=bank(b)[:, 0:TS].rearrange("p (a n) -> p a n", a=4)),
                             reads=[('ps', b)], writes=['xps'])
                        S.op('pool', lambda e: e.tensor_copy(out=xps[:, :, 0:3], in_=scT[:, c, :].rearrange("p (a n) -> p a n", a=4)),
                             reads=['scT'], writes=['xps'])
                cwl = [vec[:, CW + j * 8 + c:CW + j * 8 + c + 1] for j in range(4)]
                cbv = vec[:, CB + c:CB + c + 1]
                S.op('dve', lambda e: e.tensor_scalar(out=xc_[:, 0:T], in0=r_xp[:, 0:T], scalar1=cwl[0], scalar2=cbv, op0=ALU.mult, op1=ALU.add),
                     reads=xpall + ['vec'], writes=[XC])
                for j in range(1, 4):
                    S.op('dve', lambda e, j=j: e.scalar_tensor_tensor(out=xc_[:, 0:T], in0=r_xp[:, j:j + T], scalar=cwl[j], in1=xc_[:, 0:T], op0=ALU.mult, op1=ALU.add),
                         reads=xpall + [XC, 'vec'], writes=[XC])
                S.op('pool', lambda e: e.tensor_copy(out=fin[:, 0:24].rearrange("p (j c) -> p j c", j=3)[:, :, c], in_=r_xp[:, T:T + 3]),
                     reads=xpall, writes=['fin'])
                if has_s:
                    xcs = xc_[:, T:T + TS].rearrange("p (a n) -> p a n", a=4)
                    S.op('dve', lambda e: e.tensor_scalar(out=xcs, in0=xps[:, :, 0:8], scalar1=cwl[0], scalar2=cbv, op0=ALU.mult, op1=ALU.add),
                         reads=['xps', 'vec'], writes=[XCS])
                    for j in range(1, 4):
                        S.op('dve', lambda e, j=j: e.scalar_tensor_tensor(out=xcs, in0=xps[:, :, j:j + 8], scalar=cwl[j], in1=xcs, op0=ALU.mult, op1=ALU.add),
                             reads=['xps', XCS, 'vec'], writes=[XCS])
                    S.op('pool', lambda e: e.tensor_copy(out=fins[:, 0:96].rearrange("p (b j c) -> p b j c", b=4, j=3)[:, :, :, c], in_=xps[:, :, 8:11]),
                         reads=['xps'], writes=['fins'])

            def A2_(c):
                r_ra, r_iu = rabuf[c % 2], iubuf[c % 2]
                xc_ = xcbuf[c % 2]
                XC, XCS = ('xc', c % 2), ('xcS', c % 2)
                for si, (c0, w_) in enumerate(segs):
                    S.op('act', lambda e, c0=c0, w_=w_: e.copy(out=r_xcb[:, c0:c0 + w_], in_=xc_[:, c0:c0 + w_]), reads=[XC, XCS], writes=[('xcb', si)])
                for si, (c0, w_) in enumerate(segs):
                    bR = nb()
                    bI = nb()
                    S.op('pe', lambda e, bR=bR, c0=c0, w_=w_: e.matmul(bank(bR)[:, 0:w_], lhsT=gwbd[:, 0, c, :], rhs=r_xcb[:, c0:c0 + w_], start=True, stop=True),
                         reads=[('xcb', si), 'gwbd'], writes=[('ps', bR)])
                    S.op('pe', lambda e, bI=bI, c0=c0, w_=w_: e.matmul(bank(bI)[:, 0:w_], lhsT=gwbd[:, 1, c, :], rhs=r_xcb[:, c0:c0 + w_], start=True, stop=True),
                         reads=[('xcb', si), 'gwbd'], writes=[('ps', bI)])
                    S.op('act', lambda e, bR=bR, c0=c0, w_=w_: e.activation(out=r_ra[:, c0:c0 + w_], in_=bank(bR)[:, 0:w_], func=AF.Sigmoid, bias=vec[:, GAB + c:GAB + c + 1]),
                         reads=[('ps', bR), 'vec'], writes=[('ra', c % 2, si)])
                    S.op('act', lambda e, bI=bI, c0=c0, w_=w_: e.activation(out=r_iu[:, c0:c0 + w_], in_=bank(bI)[:, 0:w_], func=AF.Sigmoid, bias=vec[:, GXB + c:GXB + c + 1]),
                         reads=[('ps', bI), 'vec'], writes=[('iu', c % 2, si)])

            def B_(c):
                r_ra, r_iu = rabuf[c % 2], iubuf[c % 2]
                raall = [('ra', c % 2, i) for i in range(nsg)]
                iuall = [('iu', c % 2, i) for i in range(nsg)]
                _, _, wg3, rg = wts[c]
                xc_ = xcbuf[c % 2]
                XC, XCS = ('xc', c % 2), ('xcS', c % 2)
                S.op('act', lambda e: e.activation(out=r_e2[:, 0:NT], in_=r_ra[:, 0:NT], func=AF.Exp, scale=cA2[:, c:c + 1]), reads=raall + ['cA2'], writes=['e2'])
                S.op('act', lambda e: e.activation(out=r_ra[:, 0:NT], in_=r_ra[:, 0:NT], func=AF.Exp, scale=cA[:, c:c + 1]), reads=raall + ['cA', 'e2'], writes=raall)
                S.op('act', lambda e: e.activation(out=r_e2[:, 0:NT], in_=r_e2[:, 0:NT], func=AF.Sqrt, scale=-1.0, bias=1.0), reads=['e2'], writes=['e2'])
                S.op('dve', lambda e: e.tensor_tensor(out=r_iu[:, 0:NT], in0=r_iu[:, 0:NT], in1=xc_[:, 0:NT], op=ALU.mult), reads=iuall + [XC, XCS], writes=iuall)
                S.op('dve', lambda e: e.tensor_tensor(out=r_iu[:, 0:NT], in0=r_iu[:, 0:NT], in1=r_e2[:, 0:NT], op=ALU.mult), reads=iuall + ['e2'], writes=iuall)
                S.op('dve', lambda e: e.tensor_tensor_scan(out=xc_[:, 0:T], data0=r_ra[:, 0:T], data1=r_iu[:, 0:T], initial=0.0, op0=ALU.mult, op1=ALU.add),
                     reads=raall + iuall, writes=[XC])
                S.op('pool', lambda e: e.tensor_copy(out=fin[:, 24 + c:25 + c], in_=xc_[:, T - 1:T]), reads=[XC], writes=['fin'])
                if has_s:
                    for b_ in range(SB):
                        sl_ = slice(T + b_ * ST, T + (b_ + 1) * ST)
                        S.op('dve', lambda e, sl_=sl_, b_=b_: e.tensor_tensor_scan(out=xc_[:, sl_], data0=r_ra[:, sl_], data1=r_iu[:, sl_],
                                                                               initial=h0T[:, c, b_:b_ + 1], op0=ALU.mult, op1=ALU.add),
                             reads=raall + iuall + ['h0T'], writes=[XCS])
                    S.op('pool', lambda e: e.tensor_copy(out=fins[:, 96:128].rearrange("p (b c) -> p b c", b=4)[:, :, c],
                                                         in_=xc_[:, T:T + TS].rearrange("p (b t) -> p b t", b=4)[:, :, ST - 1]),
                         reads=[XCS], writes=['fins'])
                for si, (c0, w_) in enumerate(segs):
                    b = nb()
                    S.op('pe', mm(bank(b)[:, 0:w_], [(wg3[:, k, :], xnT[:, k, c0:c0 + w_]) for k in range(8)]), reads=xn_all + [rg], writes=[('ps', b)])
                    S.op('act', lambda e, b=b, c0=c0, w_=w_: e.activation(out=r_gg[:, c0:c0 + w_], in_=bank(b)[:, 0:w_], func=AF.Gelu_apprx_tanh),
                         reads=[('ps', b)], writes=[('gg', si)])
                S.op('pool', lambda e: e.tensor_tensor(out=ghsT[:, c, 0:NT], in0=r_gg[:, 0:NT], in1=xc_[:, 0:NT], op=ALU.mult),
                     reads=ggall + [XC, XCS], writes=[('ghsT', c)])

            npump = (p3c.nsteps + 22) // 23 if p3c is not None else 0

            def pump():
                if p3c is not None:
                    p3c.step(npump)
            A1_(0)
            A2_(0)
            for c in range(1, 8):
                A1_(c)
                B_(c - 1)
                if c + 2 < 8:
                    W_(c + 2)
                A2_(c)
            B_(7)
            b = nb()
            S.op('pe', lambda e, b=b: e.transpose(bank(b)[0:32, 0:128], fin[:, 0:32], identf[:, :]), reads=['fin', 'identf'], writes=[('ps', b)])
            S.op('act', lambda e, b=b: e.copy(out=fint[0:32, 0:128], in_=bank(b)[0:32, 0:128]), reads=[('ps', b)], writes=['fint'])
            dma(p_conv[ps_, :].rearrange("(r p) -> r p", p=128), fint[0:24, 0:128], reads=['fint'], sem='fin')
            dma(p_h[ps_, :].rearrange("(r p) -> r p", p=128), fint[24:32, 0:128], reads=['fint'], sem='fin')
            if has_s:
                b = nb()
                S.op('pe', lambda e, b=b: e.transpose(bank(b)[:, 0:128], fins[:, :], identf[:, :]), reads=['fins', 'identf'], writes=[('ps', b)])
                S.op('act', lambda e, b=b: e.copy(out=fin[:, :], in_=bank(b)[:, 0:128]), reads=[('ps', b)], writes=['fin'])
                dma(s_conv.rearrange("(r p) -> r p", p=128), fin[0:96, :], reads=['fin'], sem='fin')
                dma(s_h.rearrange("(r p) -> r p", p=128), fin[96:128, :], reads=['fin'], sem='fin')

            GA0 = 2048 + 3 * 768
            def w4a(o):
                return (wload([(0, wga3_src(o))], 1024), wload([(0, wgb3_src(o))], 1024),
                        wload([(0, wa3[:, :, o * 128:(o + 1) * 128])], 1024),
                        wload([(0, wb3[:, :, o * 128:(o + 1) * 128])], 512, parts=64))
            wga3_src = lambda o: win3[:, :, GA0 + o * 128:GA0 + (o + 1) * 128]
            wgb3_src = lambda o: win3[:, :, GA0 + D + o * 128:GA0 + D + (o + 1) * 128]
            w4 = {0: w4a(0)}
            S.barrier()
            for o in range(8):
                if o + 1 < 8:
                    w4[o + 1] = w4a(o + 1)
                (wga_, rga_), (wgb_, rgb_), (wa_, ra_), (wb_, rb_) = w4[o]
                wa_3 = wa_.rearrange("p (k n) -> p k n", k=8)
                wga3 = wga_.rearrange("p (k n) -> p k n", k=8)
                wgb3 = wgb_.rearrange("p (k n) -> p k n", k=8)
                wb_3 = wb_.rearrange("p (k n) -> p k n", k=4)
                ghall = [('ghsT', i) for i in range(8)]
                otall = [('oT', i) for i in range(5)]
                for si, (c0, w_) in enumerate(segs):
                    bA_, bB_, bGA, bGB = nb(), nb(), nb(), nb()
                    S.op('pe', mm(bank(bGA)[:, 0:w_], [(wga3[:, k, :], xnT[:, k, c0:c0 + w_]) for k in range(8)]), reads=xn_all + [rga_], writes=[('ps', bGA)])
                    S.op('pe', mm(bank(bGB)[:, 0:w_], [(wgb3[:, k, :], xnT[:, k, c0:c0 + w_]) for k in range(8)]), reads=xn_all + [rgb_], writes=[('ps', bGB)])
                    S.op('pe', mm(bank(bA_)[:, 0:w_], [(wa_3[:, k, :], ghsT[:, k, c0:c0 + w_]) for k in range(8)]), reads=ghall + [ra_], writes=[('ps', bA_)])
                    S.op('pe', mm(bank(bB_)[:, 0:w_], [(wb_3[:, k, :], oT[0:64, k, c0:c0 + w_]) for k in range(4)]), reads=otall + [rb_], writes=[('ps', bB_)])
                    s0, s1 = rr('sgm', 2) * 2, None
                    s1 = s0 + 1
                    S.op('act', lambda e, bGA=bGA, s0=s0, o=o, w_=w_: e.activation(out=sg[s0][:, 0:w_], in_=bank(bGA)[:, 0:w_], func=AF.Sigmoid, bias=vec[:, BM + o:BM + o + 1]),
                         reads=[('ps', bGA), 'vec'], writes=[('sg', s0)])
                    S.op('act', lambda e, bGB=bGB, s1=s1, o=o, w_=w_: e.activation(out=sg[s1][:, 0:w_], in_=bank(bGB)[:, 0:w_], func=AF.Sigmoid, bias=vec[:, BM + 8 + o:BM + 9 + o]),
                         reads=[('ps', bGB), 'vec'], writes=[('sg', s1)])
                    S.op('dve', lambda e, bA_=bA_, s0=s0, w_=w_: e.tensor_tensor(out=sg[s0][:, 0:w_], in0=sg[s0][:, 0:w_], in1=bank(bA_)[:, 0:w_], op=ALU.mult),
                         reads=[('ps', bA_), ('sg', s0)], writes=[('sg', s0)])
                    S.op('dve', lambda e, bB_=bB_, s1=s1, w_=w_: e.tensor_tensor(out=sg[s1][:, 0:w_], in0=sg[s1][:, 0:w_], in1=bank(bB_)[:, 0:w_], op=ALU.mult),
                         reads=[('ps', bB_), ('sg', s1)], writes=[('sg', s1)])
                    S.op('pool', lambda e, s0=s0, s1=s1, o=o, c0=c0, w_=w_: e.tensor_tensor(out=mT[:, o, c0:c0 + w_], in0=sg[s0][:, 0:w_], in1=sg[s1][:, 0:w_], op=ALU.add),
                         reads=[('sg', s0), ('sg', s1)], writes=[('mT', o)])

            wo_ = [wload([(0, wo3[:, k2, :])], 1024) for k2 in range(8)]
            wor_ = [r_ for _, r_ in wo_]
            S.barrier()
            mall = [('mT', i) for i in range(8)]
            its = []
            for ti, (c0, np_) in enumerate(tiles):
                stt = {}
                sl = ti % 2
                src = xp[ps_, c0:c0 + 128, :] if ti < 16 else xs[:, :]
                hb = hbuf(ti)[0:np_, :]

                def A_(stt=stt, ti=ti, c0=c0, np_=np_, sl=sl, src=src):
                    bX = rr('pb', 3) * 2
                    stt['bX'] = bX
                    S.op('pe', mm(bank(bX)[0:np_, :], [(mT[:, k, c0:c0 + np_], wo_[k][0][:, 0:512]) for k in range(8)]), reads=mall + wor_, writes=[('ps', bX)])
                    S.op('pe', mm(bank(bX + 1)[0:np_, :], [(mT[:, k, c0:c0 + np_], wo_[k][0][:, 512:1024]) for k in range(8)]), reads=mall + wor_, writes=[('ps', bX + 1)])
                    dma(xt[sl][0:np_, :], src, writes=[('xt', sl)], sem='xt%d' % sl)
                Bn, Cn, Dn = rms_stages(hb, np_, mT, c0, G2, sl, ('hb', ti), ('hnT', ti))

                def B_(stt=stt, ti=ti, np_=np_, sl=sl, hb=hb, Bn=Bn):
                    bX = stt['bX']
                    xy = psum[0:np_, bX * 512:(bX + 2) * 512]
                    S.op('dve', lambda e: e.tensor_tensor(out=hb, in0=xy, in1=xt[sl][0:np_, :], op=ALU.add),
                         reads=[('ps', bX), ('ps', bX + 1), ('xt', sl)], writes=[('hb', ti)])
                    Bn()
                its.append((A_, B_, Cn, Dn))
            pipeline(its, [0, 1, 2, 2])
            hnT = mT
            hn_all = [('hnT', i) for i in range(len(tiles))]
            hn_seg = [[('hnT', 4 * si_ + i_) for i_ in range(4)] if si_ < 4 else [('hnT', 16)] for si_ in range(len(segs))]

            dma(gfb, gfb_d[:, :], writes=['gfb'], sem='c_gfb')

            def final_tile(ti, c0, np_):
                sl = rr('x', 2)
                hb = hbuf(ti)[0:np_, :]
                S.op('act', lambda e: e.activation(out=xsb[sl][0:np_, :], in_=hb, func=AF.Square, accum_out=ss[sl][0:np_, :]),
                     reads=[('hb', ti)], writes=[('xsb', sl), ('ss', sl)])
                S.op('act', lambda e: e.activation(out=rs[sl][0:np_, :], in_=ss[sl][0:np_, :], func=AF.Sqrt, scale=1.0 / D, bias=EPS),
                     reads=[('ss', sl)], writes=[('rs', sl)])
                S.op('dve', lambda e: e.reciprocal(out=rs[sl][0:np_, :], in_=rs[sl][0:np_, :]), reads=[('rs', sl)], writes=[('rs', sl)])
                S.op('dve', lambda e: e.scalar_tensor_tensor(out=xt[sl][0:np_, :], in0=hb, scalar=rs[sl][0:np_, 0:1], in1=gfb[0:np_, :], op0=ALU.mult, op1=ALU.mult),
                     reads=[('hb', ti), ('rs', sl), 'gfb'], writes=[('xt', sl)])
                dst = y_p[ps_, c0:c0 + 128, :] if ti < 16 else y_s[:, :]
                dma(dst, xt[sl][0:np_, :], reads=[('xt', sl)], sem='xt%d' % sl)

            jgroups = [list(range(j0, min(j0 + 3, NJ))) for j0 in range(0, NJ, 3)]
            w5q, w5r, w5p = [], {}, [0]
            for gi, jg in enumerate(jgroups):
                for j in jg:
                    w5q.append(('in', j))
                w5q.append(('out', gi))

            def w5need(upto):
                while w5p[0] <= min(upto, len(w5q) - 1):
                    kind, v = w5q[w5p[0]]
                    if kind == 'in':
                        w5r[(kind, v)] = (wload([(0, wf13[:, :, v * 128:(v + 1) * 128])], 1024),
                                          wload([(0, wf13[:, :, DFF + v * 128:DFF + (v + 1) * 128])], 1024))
                    else:
                        w5r[(kind, v)] = [wload([(0, w_f2[j_ * 128:(j_ + 1) * 128, :])], 1024) for j_ in jgroups[v]]
                    w5p[0] += 1
            for gi, jg in enumerate(jgroups):
                a_ = gi % 2
                for jj, j in enumerate(jg):
                    qi = w5q.index(('in', j))
                    w5need(qi + 1)
                    (wgt, rgt), (wup, rup) = w5r[('in', j)]
                    wgt3 = wgt.rearrange("p (k n) -> p k n", k=8)
                    wup3 = wup.rearrange("p (k n) -> p k n", k=8)
                    for si, (c0, w_) in enumerate(segs):
                        bG, bU = nb(), nb()
                        S.op('pe', mm(bank(bG)[:, 0:w_], [(wgt3[:, k, :], hnT[:, k, c0:c0 + w_]) for k in range(8)]), reads=hn_seg[si] + [rgt], writes=[('ps', bG)])
                        S.op('pe', mm(bank(bU)[:, 0:w_], [(wup3[:, k, :], hnT[:, k, c0:c0 + w_]) for k in range(8)]), reads=hn_seg[si] + [rup], writes=[('ps', bU)])
                        s0 = rr('sgf', 4)
                        S.op('act', lambda e, bG=bG, s0=s0, w_=w_: e.activation(out=sg[s0][:, 0:w_], in_=bank(bG)[:, 0:w_], func=AF.Silu), reads=[('ps', bG)], writes=[('sg', s0)])
                        S.op('dve', lambda e, bU=bU, s0=s0, a_=a_, jj=jj, c0=c0, w_=w_: e.tensor_tensor(out=actT[a_][:, jj, c0:c0 + w_], in0=sg[s0][:, 0:w_], in1=bank(bU)[:, 0:w_], op=ALU.mult),
                             reads=[('ps', bU), ('sg', s0)], writes=[('actT', a_)])
                w5need(w5q.index(('out', gi)) + 1)
                w2 = w5r[('out', gi)]
                w2r = [r_ for _, r_ in w2]
                for ti, (c0, np_) in enumerate(tiles):
                    bX = nb2()
                    xy = psum[0:np_, bX * 512:(bX + 2) * 512]
                    S.op('pe', mm(bank(bX)[0:np_, :], [(actT[a_][:, jj, c0:c0 + np_], w2[jj][0][:, 0:512]) for jj in range(len(jg))]), reads=[('actT', a_)] + w2r, writes=[('ps', bX)])
                    S.op('pe', mm(bank(bX + 1)[0:np_, :], [(actT[a_][:, jj, c0:c0 + np_], w2[jj][0][:, 512:1024]) for jj in range(len(jg))]), reads=[('actT', a_)] + w2r, writes=[('ps', bX + 1)])
                    hb = hbuf(ti)[0:np_, :]
                    S.op('dve', lambda e, hb=hb, xy=xy: e.tensor_tensor(out=hb, in0=hb, in1=xy, op=ALU.add),
                         reads=[('ps', bX), ('ps', bX + 1), ('hb', ti)], writes=[('hb', ti)])
                    if gi == len(jgroups) - 1 and ti >= 1:
                        final_tile(ti - 1, *tiles[ti - 1])
                if gi == len(jgroups) - 1:
                    final_tile(len(tiles) - 1, *tiles[-1])
            if ps_ + 1 < NB:
                for which in range(3):
                    for hf in range(2):
                        c_ = 2048 + which * 768
                        p3next[0][(0, which, hf)] = wload([(0, win3[:, hf * 4:(hf + 1) * 4, c_:c_ + 256])], 1024)

        block = es.enter_context(nc.Block())
        S.emit(block)
    return nc


_NC = None


def _consts():
    half = 32
    inv = np.exp(-math.log(10000.0) * np.arange(half, dtype=np.float32) * np.float32(2.0 / 64)).astype(np.float32)

    def tab(pos):
        ang = pos.astype(np.float32)[:, None] * inv[None, :]
        return np.cos(ang).astype(np.float32), np.sin(ang).astype(np.float32)
    rope = np.zeros((3, 128, 2, 16, 32), np.float32)
    j = np.arange(128)
    for g, d in enumerate(DILS):
        for tl in range(16):
            if g == 0:
                pos = tl * 128 + j
            elif g == 1:
                r_, n_ = tl // 4, tl % 4
                pos = 4 * (128 * n_ + j) + r_
            else:
                pos = 16 * j + tl
            c, s = tab(pos)
            rope[g, :, 0, tl, :] = c
            rope[g, :, 1, tl, :] = s
    pos_s = PAST + (np.arange(TS) % ST)
    c, s = tab(pos_s)
    ropes = np.concatenate([c, s], axis=1).astype(np.float32)
    cm = np.zeros((128, 872), np.float32)
    jj = np.arange(128)[:, None]
    ii = np.arange(128)[None, :]
    cm[:, 0:128] = (jj <= ii)
    cm[:, 128:256] = (jj >= ii)
    for q in range(4):
        cm[:, 256 + q * 32:256 + (q + 1) * 32] = (jj <= (32 * q + np.arange(32)[None, :]))
    cm[:, 384:392] = (jj >= np.arange(8)[None, :])
    kb, kt = np.arange(TS)[:, None] // ST, np.arange(TS)[:, None] % ST
    qb, qt = np.arange(TS)[None, :] // ST, np.arange(TS)[None, :] % ST
    for g, d in enumerate(DILS):
        cm[0:TS, 392 + g * 32:392 + (g + 1) * 32] = (kb == qb) & (qt >= kt) & ((qt - kt) % d == 0)
    cm[:, 488:872] = np.where(cm[:, 0:384] > 0, 0.0, -30000.0)
    ident = np.eye(128, dtype=np.float32)
    return rope.reshape(3, 128, 1024), ropes, cm, ident


def kernel(x_prompt, x_sample, state_conv, state_h, cache_k_g0, cache_v_g0, cache_k_g1, cache_v_g1,
           cache_k_g2, cache_v_g2, norm1_g, w_in, b_merge, conv_w, conv_b, gate_a_w, gate_a_b,
           gate_x_w, gate_x_b, rg_lambda, w_branch_a, w_branch_b, w_out, norm2_g, w_ffn_in, w_ffn_out,
           norm_f_g):
    global _NC
    if _NC is None:
        _NC = build()
    nc = _NC
    f = lambda a: np.ascontiguousarray(np.asarray(a, dtype=np.float32))
    fm = lambda v: f(v).reshape(8, 128).T
    vecs = np.zeros((128, 96), np.float32)
    vecs[:, 0:8] = fm(norm1_g[0]); vecs[:, 8:16] = fm(norm2_g[0])
    vecs[:, 16:32] = f(b_merge[0]).reshape(16, 128).T
    for j in range(4):
        vecs[:, 32 + j * 8:40 + j * 8] = fm(conv_w[0, j])
    vecs[:, 64:72] = fm(conv_b[0]); vecs[:, 72:80] = fm(gate_a_b[0]); vecs[:, 80:88] = fm(gate_x_b[0])
    vecs[:, 88:96] = fm(rg_lambda[0])
    gfb = np.ascontiguousarray(np.broadcast_to(f(norm_f_g)[None, :], (128, D)))
    gw = np.zeros((128, 2, 8, 128), np.float32)
    for wi, gwm in enumerate((gate_a_w, gate_x_w)):
        gwm = f(gwm[0])
        for c in range(8):
            gw[0:64, wi, c, 0:64] = gwm[2 * c]
            gw[64:128, wi, c, 64:128] = gwm[2 * c + 1]
    rope, ropes, cm, ident = _consts()
    shared = dict(w_in=f(w_in[0]), w_a=f(w_branch_a[0]), w_b=f(w_branch_b[0]), w_out=f(w_out[0]),
                  w_f1=f(w_ffn_in[0]), w_f2=f(w_ffn_out[0]), vecs=vecs, gfb=gfb, gwbd=gw.reshape(128, 2048),
                  rope=rope, ropes=ropes, cmask=cm, ident=ident)
    caches_k = [f(cache_k_g0[0]), f(cache_k_g1[0]), f(cache_k_g2[0])]
    caches_v = [f(cache_v_g0[0]), f(cache_v_g1[0]), f(cache_v_g2[0])]
    xp_, xs_ = f(x_prompt), f(x_sample)
    sc_, sh_ = f(state_conv[0]), f(state_h[0])
    in_maps = []
    for c in range(NCORES):
        m = dict(shared)
        m["xp"] = xp_[c * NB:(c + 1) * NB]
        m["xs"] = xs_[c * SB:(c + 1) * SB].reshape(TS, D)
        m["sconv"] = sc_[c * SB:(c + 1) * SB].reshape(SB * 3, D)
        m["sh"] = sh_[c * SB:(c + 1) * SB]
        for g in range(3):
            m["ck%d" % g] = caches_k[g][c * SB:(c + 1) * SB].reshape(SB, WINS[g], 256)
            m["cv%d" % g] = caches_v[g][c * SB:(c + 1) * SB].reshape(SB, WINS[g], 256)
        in_maps.append(m)
    res = run_bass_kernel_spmd(nc, in_maps, core_ids=list(range(NCORES)))
    R = res.results
    cat = lambda name: np.concatenate([np.asarray(r[name]) for r in R], axis=0)
    y_prompt = cat("y_p").reshape(16, T, D)
    y_sample = cat("y_s").reshape(32, ST, D)
    pconv = cat("p_conv").reshape(1, 16, 3, D)
    ph = cat("p_h").reshape(1, 16, D)
    outs = [y_prompt, y_sample, pconv, ph]
    for g in range(3):
        outs.append(cat("p_k%d" % g).reshape(1, 16, WINS[g], 4, 64))
        outs.append(cat("p_v%d" % g).reshape(1, 16, WINS[g], 4, 64))
    outs.append(np.concatenate([np.asarray(r["s_conv"]).reshape(SB, 3, D) for r in R], axis=0).reshape(1, 32, 3, D))
    outs.append(np.concatenate([np.asarray(r["s_h"]).reshape(SB, D) for r in R], axis=0).reshape(1, 32, D))
    for g in range(3):
        outs.append(cat("s_k%d" % g).reshape(1, 32, WINS[g], 4, 64))
        outs.append(cat("s_v%d" % g).reshape(1, 32, WINS[g], 4, 64))
    return tuple(np.ascontiguousarray(o, dtype=np.float32) for o in outs)
```

```python
import math
import numpy as np
from contextlib import ExitStack
import concourse.bass as bass
import concourse.mybir as mybir
from concourse.bass_utils import run_bass_kernel_spmd

F32 = mybir.dt.float32
BF16 = mybir.dt.bfloat16
ALU = mybir.AluOpType
AF = mybir.ActivationFunctionType

D = 1024
T = 2048
NB = 2
SB = 4
ST = 8
TS = SB * ST
PAST = 16384
DFF = 2816
NJ = DFF // 128
INW = 6400
WINS = (128, 512, 2048)
DILS = (1, 4, 16)
EPS = 1e-6
NCORES = 8


class Sched:
    def __init__(self, nc, es):
        self.nc = nc
        self.es = es
        self.engs = {'pe': None, 'act': None, 'dve': None, 'pool': None, 'sp': None}
        self.ops = {e: [] for e in self.engs}
        self.cnt = {e: 0 for e in self.engs}
        self.sems = {e: es.enter_context(nc.semaphore('s_' + e)) for e in self.engs}
        self.waited = {e: {} for e in self.engs}
        self.res = {}
        self.dcnt = {}
        self.pending = {e: [] for e in self.engs}

    def _dsem(self, name):
        if name not in self.sems:
            self.sems[name] = self.es.enter_context(self.nc.semaphore('d_' + name))
            self.dcnt[name] = 0
        return self.sems[name]

    def _need(self, eng, tok, waits):
        key, val = tok
        if key == 'pe' and eng == 'pe':
            return
        if self.waited[eng].get(key, 0) >= val:
            return
        self.waited[eng][key] = val
        waits.append(tok)

    def op(self, eng, fn, reads=(), writes=(), dma=None):
        waits = []
        for tok in self.pending[eng]:
            self._need(eng, tok, waits)
        self.pending[eng] = []
        for r in reads:
            st = self.res.setdefault(r, {'w': None, 'r': []})
            if st['w'] is not None:
                self._need(eng, st['w'], waits)
        for w in writes:
            st = self.res.setdefault(w, {'w': None, 'r': []})
            if st['w'] is not None:
                self._need(eng, st['w'], waits)
            for tok in st['r']:
                self._need(eng, tok, waits)
        if dma is not None:
            self._dsem(dma)
            self.dcnt[dma] += 16
            tok = (dma, self.dcnt[dma])
            self.ops[eng].append((waits, fn, (dma, 16)))
        else:
            self.cnt[eng] += 1
            tok = (eng, self.cnt[eng])
            self.ops[eng].append((waits, fn, (eng, 1)))
        for r in reads:
            self.res[r]['r'].append(tok)
        for w in writes:
            self.res[w]['w'] = tok
            self.res[w]['r'] = []
        return tok

    def barrier(self):
        toks = [(e, c) for e, c in self.cnt.items() if c > 0 and e != 'sp']
        toks += [(d, c) for d, c in self.dcnt.items() if c > 0]
        for e in self.engs:
            self.pending[e] = list(toks)

    def emit(self, block):
        nc = self.nc
        final = [(d, c) for d, c in self.dcnt.items() if c > 0]
        final += [(e, c) for e, c in self.cnt.items() if c > 0 and e != 'sp']

        def run(name, handle, tail=False):
            for waits, fn, inc in self.ops[name]:
                for key, val in waits:
                    handle.wait_ge(self.sems[key], val)
                ins = fn(handle)
                ins.then_inc(self.sems[inc[0]], inc[1])
            if tail:
                for key, val in final:
                    handle.wait_ge(self.sems[key], val)

        block.sync(lambda e: run('sp', e, True))
        block.scalar(lambda e: run('act', e))
        block.vector(lambda e: run('dve', e))
        block.gpsimd(lambda e: run('pool', e))
        block.tensor(lambda e: run('pe', e))


def build():
    nc = bass.Bass("TRN2", target_bir_lowering=False)

    def din(name, shape):
        return nc.dram_tensor(name, list(shape), F32, kind="ExternalInput").ap()

    def dout(name, shape):
        return nc.dram_tensor(name, list(shape), F32, kind="ExternalOutput").ap()

    xp = din("xp", [NB, T, D]); xs = din("xs", [TS, D])
    sconv = din("sconv", [SB * 3, D]); sh = din("sh", [SB, D])
    ck = [din("ck%d" % g, [SB, WINS[g], 256]) for g in range(3)]
    cv = [din("cv%d" % g, [SB, WINS[g], 256]) for g in range(3)]
    w_in = din("w_in", [D, INW]); w_a = din("w_a", [D, D]); w_b = din("w_b", [256, D])
    w_out = din("w_out", [D, D]); w_f1 = din("w_f1", [D, 2 * DFF]); w_f2 = din("w_f2", [DFF, D])
    vecs = din("vecs", [128, 96]); gfb_d = din("gfb", [128, D]); gwbd_d = din("gwbd", [128, 2 * 8 * 128])
    rope_d = din("rope", [3, 128, 2 * 16 * 32]); ropes_d = din("ropes", [TS, 64])
    cmask_d = din("cmask", [128, 872]); ident_d = din("ident", [128, 128])

    y_p = dout("y_p", [NB, T, D]); y_s = dout("y_s", [TS, D])
    p_conv = dout("p_conv", [NB, 3 * D]); p_h = dout("p_h", [NB, D])
    p_k = [dout("p_k%d" % g, [NB, WINS[g], 256]) for g in range(3)]
    p_v = [dout("p_v%d" % g, [NB, WINS[g], 256]) for g in range(3)]
    s_conv = dout("s_conv", [SB * 3 * D]); s_h = dout("s_h", [SB * D])
    s_k = [dout("s_k%d" % g, [SB, WINS[g], 256]) for g in range(3)]
    s_v = [dout("s_v%d" % g, [SB, WINS[g], 256]) for g in range(3)]

    es = ExitStack()
    with es:
        def sb(name, shape, dt=F32):
            return es.enter_context(nc.sbuf_tensor(name, list(shape), dt))

        S = Sched(nc, es)
        NTMAX = T + TS
        A_XNT = 0
        A_OT = 33280
        A_X = A_OT + 16640
        ARENA = A_X + 83520
        arena = sb("arena", [128, ARENA // 4], F32)

        def aview(off, nbytes, dt, pat=None, parts=128, **kw):
            v = arena[0:parts, off // 4:(off + nbytes) // 4]
            if dt is BF16:
                v = v.bitcast(BF16)
            if pat:
                v = v.rearrange(pat, **kw)
            return v

        xnT = aview(A_XNT, 33280, BF16, "p (k n) -> p k n", k=8)
        oT = aview(A_OT, 16640, BF16, "p (h n) -> p h n", h=4)
        QKT = aview(A_X, 49152, BF16, "p (g i n) -> p g i n", g=3, i=4)
        Vt = aview(A_X + 49152, 24960, BF16, "p (t g c) -> p t g c", t=16, g=3)
        RW = 2096
        r_xp = aview(A_X, RW * 4, F32)
        r_xc = aview(A_X + RW * 4, RW * 4, F32)
        r_ra = aview(A_X + 2 * RW * 4, RW * 4, F32)
        r_iu = aview(A_X + 3 * RW * 4, RW * 4, F32)
        r_e2 = aview(A_X + 4 * RW * 4, RW * 4, F32)
        r_xcb = aview(A_X + 5 * RW * 4, 4160, BF16)
        r_gg = aview(A_X + 5 * RW * 4 + 4160, 4160, BF16)
        assert 5 * RW * 4 + 8320 <= 50240
        ghsT = aview(A_X + 50240, 33280, BF16, "p (k n) -> p k n", k=8)
        mT = aview(A_X, 33280, BF16, "p (k n) -> p k n", k=8)
        hb1 = aview(A_XNT, 32768, F32, "p (t n) -> p t n", t=8)
        hb2 = aview(A_X + 33280, 36864, F32, "p (t n) -> p t n", t=9)
        woutT = aview(A_OT, 16384, BF16, "p (k n) -> p k n", k=8)
        actT = [aview(A_OT, 12480, BF16, "p (j n) -> p j n", j=3),
                aview(A_X + 70144, 12480, BF16, "p (j n) -> p j n", j=3)]

        def hbuf(ti):
            return hb1[:, ti, :] if ti < 8 else hb2[:, ti - 8, :]

        vec = sb("vec", [128, 96])
        gfb = aview(A_OT + 12480, 4096, F32)
        gwbd = sb("gwbds", [128, 2, 8, 128], BF16)
        identf = sb("identf", [128, 128]); identb = sb("identb", [128, 128], BF16)
        ones = sb("ones", [128, 64], BF16)
        onesf = sb("onesf", [128, 64])
        cmb = sb("cmb", [128, 872], BF16)
        misc = sb("misc", [128, 6656])
        ropeT = misc[:, 4608:5632].rearrange("p (a t c) -> p a t c", a=2, t=16)
        ropeS = sb("ropeS", [TS, 2, 32])
        cA = sb("cA", [128, 8]); cA2 = sb("cA2", [128, 8])
        stg = [sb("stg%d" % i, [128, 1024]) for i in range(2)]
        NWB = 8
        wbf = [sb("wbf%d" % i, [128, 1024], BF16) for i in range(NWB)]
        xtbig = sb("xtbig", [128, 2100])
        xt = [xtbig[:, i * D:(i + 1) * D] for i in range(2)]
        xsb = [sb("xsb%d" % i, [128, D], BF16) for i in range(2)]
        ss = [sb("ss%d" % i, [128, 1]) for i in range(2)]
        rs = [sb("rs%d" % i, [128, 1]) for i in range(2)]
        qkr = [misc[:, 2048 + i * 512:2048 + (i + 1) * 512] for i in range(2)]
        tm1 = misc[:, 3072:3328].rearrange("p (h c) -> p h c", h=8)
        tm2 = misc[:, 3328:3584].rearrange("p (h c) -> p h c", h=8)
        qkb = [misc[:, 3584 + i * 256:3584 + (i + 1) * 256].bitcast(BF16) for i in range(2)]
        vf = [misc[:, 4096 + i * 256:4096 + (i + 1) * 256] for i in range(2)]
        PT = [misc[:, 6144 + i * 128:6144 + (i + 1) * 128].bitcast(BF16) for i in range(4)]
        rec = misc[:, 5632:6144]
        sg = [misc[:, i * 512:(i + 1) * 512] for i in range(4)]
        rabuf_alt = misc[:, 0:RW]
        iubuf_alt = misc[:, RW:2 * RW]
        PTx = PT + [sg[i][:, j * 128:(j + 1) * 128].bitcast(BF16) for i in range(2) for j in range(4)]
        fin = sb("fin", [128, 128])
        fins = sb("fins", [128, 128])
        scT = sb("scT", [128, 8, 12]); h0T = sb("h0T", [128, 8, 4])
        xps = sb("xps", [128, 4, 11])
        sctm = aview(A_X + 74240, 4096, F32)
        fint = sb("fint", [32, 128])
        QKTs = sb("QKTs", [128, 3, 4, TS], BF16)
        Vs = sb("Vs", [TS, 3, 256], BF16)
        kc = [sg[2][:, i * 256:(i + 1) * 256] for i in range(2)]
        vc = [sg[3][:, i * 256:(i + 1) * 256] for i in range(2)]
        vcb = [sb("vcb%d" % i, [128, 256], BF16) for i in range(3)]
        kcT = [sb("kcT%d" % i, [128, 2, 128], BF16) for i in range(2)]
        PTs = [sb("PTs%d" % i, [128, 4, 8], BF16) for i in range(2)]
        PTn = [sb("PTn%d" % i, [TS, TS], BF16) for i in range(2)]
        oTs = sb("oTs", [64, 4, TS], BF16)

        psum = es.enter_context(nc.psum_tensor("psum", [128, 8 * 512], F32))

        def bank(i):
            return psum[:, i * 512:(i + 1) * 512]

        def bankb(i):
            return psum[:, i * 512:(i + 1) * 512].bitcast(BF16)

        bctr = [0]

        nbanks = [8]

        def nb():
            bctr[0] = (bctr[0] + 1) % nbanks[0]
            return bctr[0]

        def nb2():
            b = ((bctr[0] // 2 + 1) % 4) * 2
            bctr[0] = b + 1
            return b

        ctr = {'stg': 0, 'wbf': 0, 'x': 0, 'q': 0, 'pt': 0, 'sgm': 0, 'sgf': 0, 'k': 0, 'kn': 0, 'cast': 0, 'sb': 0, 'ob': 0, 'tb': 0, 'pb': 0, 'sbs': 0, 'k2': 0, 'v3': 0, 'p2': 0, 'sb3': 0, 'ptx': 0}

        def rr(name, n):
            v = ctr[name]
            ctr[name] = (v + 1) % n
            return v

        def dma(out, in_, reads=(), writes=(), sem=None):
            S.op('sp', lambda e, o=out, i=in_: e.dma_start(out=o, in_=i), reads=reads, writes=writes, dma=sem)

        def wload(srcs, n, parts=128, dest=None, dres=None):
            s = rr('stg', 2)
            for off, ap in srcs:
                sz = 1
                for d_ in ap.shape[1:]:
                    sz *= d_
                o = stg[s][0:ap.shape[0], off:off + sz]
                if len(ap.shape) == 3:
                    o = o.rearrange("p (a b) -> p a b", a=ap.shape[1])
                dma(o, ap, writes=[('stg', s)], sem='stg%d' % s)
            if dest is None:
                w = rr('wbf', NWB)
                dest = wbf[w][0:parts, 0:n]
                dres = ('wbf', w)
            ce = rr('cast', 2)
            if ce == 0:
                S.op('dve', lambda e, o=dest, i=stg[s][0:parts, 0:n]: e.tensor_copy(out=o, in_=i), reads=[('stg', s)], writes=[dres])
            else:
                S.op('act', lambda e, o=dest, i=stg[s][0:parts, 0:n]: e.copy(out=o, in_=i), reads=[('stg', s)], writes=[dres])
            return dest, dres

        def mm(out, pairs, first=True, last=True, skip=False):
            def fn(e):
                ins = None
                n = len(pairs)
                for i, (l, r) in enumerate(pairs):
                    ins = e.matmul(out, lhsT=l, rhs=r, start=(first and i == 0), stop=(last and i == n - 1),
                                   skip_group_check=skip)
                return ins
            return fn

        win3 = w_in.rearrange("(k p) n -> p k n", p=128)
        wa3 = w_a.rearrange("(k p) n -> p k n", p=128)
        wo3 = w_out.rearrange("(k p) n -> p k n", p=128)
        wf13 = w_f1.rearrange("(k p) n -> p k n", p=128)
        wb3 = w_b.rearrange("(h p) n -> p h n", p=64)

        dma(vec[:], vecs[:, :], writes=['vec'], sem='c_vec')
        dma(identf[:], ident_d[:, :], writes=['identf'], sem='c_id')
        cmf = xt[0][:, 0:872]
        dma(cmf, cmask_d[:, :], writes=[('xt', 0)], sem='c_cm')
        dma(ropeS[:].rearrange("p a b -> p (a b)"), ropes_d[:, :], writes=['ropeS'], sem='c_rs')
        S.op('pool', lambda e: e.tensor_copy(out=identb[:], in_=identf[:]), reads=['identf'], writes=['identb'])
        S.op('pool', lambda e: e.tensor_copy(out=cmb[:], in_=cmf), reads=[('xt', 0)], writes=['cmb'])
        S.op('pool', lambda e: e.memset(ones[:], 1.0), writes=['ones'])
        S.op('pool', lambda e: e.memset(onesf[:], 1.0), writes=['onesf'])
        for h_ in range(2):
            for q_ in range(2):
                wload([(0, gwbd_d[:, (h_ * 2 + q_) * 512:(h_ * 2 + q_ + 1) * 512])], 512,
                      dest=gwbd[:, h_, q_ * 4:(q_ + 1) * 4, :].rearrange("p a b -> p (a b)"), dres='gwbd')
        S.op('act', lambda e: e.activation(out=cA[:], in_=vec[:, 88:96], func=AF.Sigmoid), reads=['vec'], writes=['cA'])
        S.op('act', lambda e: e.activation(out=cA[:], in_=cA[:], func=AF.Ln), reads=['cA'], writes=['cA'])
        S.op('act', lambda e: e.mul(out=cA2[:], in_=cA[:], mul=16.0), reads=['cA'], writes=['cA2'])
        S.op('act', lambda e: e.mul(out=cA[:], in_=cA[:], mul=8.0), reads=['cA', 'cA2'], writes=['cA'])
        G1, G2, BM, CW, CB, GAB, GXB = 0, 8, 16, 32, 64, 72, 80
        mask2 = cmb[:, 0:256]
        maskg2 = cmb[:, 256:384]
        maskC = cmb[:, 384:392]
        maskN = cmb[0:TS, 392:488]
        nm2 = cmb[:, 488:744]
        nmg2 = cmb[:, 744:872]

        def pipeline(iters, skew, group=1):
            n = len(iters)
            nst = n + max(skew)
            for s0_ in range(0, nst, group):
                for k, sk in enumerate(skew):
                    for st_ in range(s0_, min(s0_ + group, nst)):
                        t_ = st_ - sk
                        if 0 <= t_ < n and iters[t_][k] is not None:
                            iters[t_][k]()

        class Pipe:
            def __init__(self, iters, skew, desc=False):
                self.iters, self.skew, self.st = iters, skew, 0
                self.nsteps = len(iters) + max(skew)
                self.order = list(range(len(skew)))
                if desc:
                    self.order.reverse()

            def step(self, n=1):
                for _ in range(n):
                    if self.st >= self.nsteps:
                        return
                    for k in self.order:
                        t_ = self.st - self.skew[k]
                        if 0 <= t_ < len(self.iters) and self.iters[t_][k] is not None:
                            self.iters[t_][k]()
                    self.st += 1

            def finish(self):
                self.step(self.nsteps)

        def rms_stages(src_tile, np_, dstT, col0, gcol, slot, src_res, dst_res):
            stt = {}

            def B():
                S.op('act', lambda e: e.activation(out=xsb[slot][0:np_, :], in_=src_tile, func=AF.Square, accum_out=ss[slot][0:np_, :]),
                     reads=[src_res], writes=[('xsb', slot), ('ss', slot)])
                S.op('act', lambda e: e.activation(out=rs[slot][0:np_, :], in_=ss[slot][0:np_, :], func=AF.Sqrt, scale=1.0 / D, bias=EPS),
                     reads=[('ss', slot)], writes=[('rs', slot)])
                S.op('dve', lambda e: e.reciprocal(out=rs[slot][0:np_, :], in_=rs[slot][0:np_, :]), reads=[('rs', slot)], writes=[('rs', slot)])
                S.op('act', lambda e: e.mul(out=xsb[slot][0:np_, :], in_=src_tile, mul=rs[slot][0:np_, 0:1]),
                     reads=[src_res, ('rs', slot)], writes=[('xsb', slot)])

            def C():
                b = 6 + rr('tb', 2)
                stt['b'] = b

                def tr(e):
                    ins = None
                    for k in range(8):
                        ins = e.transpose(bankb(b)[:, k * 128:k * 128 + np_], xsb[slot][0:np_, k * 128:(k + 1) * 128], identb[0:np_, 0:np_])
                    return ins
                S.op('pe', tr, reads=[('xsb', slot), 'identb'], writes=[('ps', b)])

            def Dd():
                b = stt['b']
                S.op('dve', lambda e: e.tensor_tensor(
                    out=dstT[:, :, col0:col0 + np_],
                    in0=bankb(b).rearrange("p (k n) -> p k n", k=8)[:, :, 0:np_],
                    in1=vec[:, gcol:gcol + 8].unsqueeze(2).to_broadcast([128, 8, np_]), op=ALU.mult),
                    reads=[('ps', b), 'vec'], writes=[dst_res])
            return B, C, Dd

        p3next = [{}]
        for ps_ in range(NB):
            has_s = (ps_ == NB - 1)
            NT = T + (TS if has_s else 0)
            segs = [(i * 512, 512) for i in range(4)] + ([(T, TS)] if has_s else [])
            tiles = [(i * 128, 128) for i in range(16)] + ([(T, TS)] if has_s else [])
            S.barrier()
            its = []
            for ti, (c0, np_) in enumerate(tiles):
                sl = ti % 2
                src = xp[ps_, c0:c0 + 128, :] if ti < 16 else xs[:, :]
                A_ = (lambda sl=sl, np_=np_, src=src: dma(xt[sl][0:np_, :], src, writes=[('xt', sl)], sem='xt%d' % sl))
                B_, C_, D_ = rms_stages(xt[sl][0:np_, :], np_, xnT, c0, G1, sl, ('xt', sl), ('xnT', ti))
                its.append((A_, B_, C_, D_))
            p1its = its

            def sample_states():
                dma(sctm[0:12, :], sconv[:, :], writes=['sctm'], sem='c1')
                dma(sctm[12:16, :], sh[:, :], writes=['sctm'], sem='c1')
                b = nb()

                def trs(e, b=b):
                    ins = None
                    for k in range(8):
                        ins = e.transpose(bank(b)[:, k * 16:(k + 1) * 16], sctm[0:16, k * 128:(k + 1) * 128], identf[0:16, 0:16])
                    return ins
                S.op('pe', trs, reads=['sctm', 'identf'], writes=[('ps', b)])
                S.op('dve', lambda e, b=b: e.tensor_copy(out=scT[:], in_=bank(b)[:, 0:128].rearrange("p (k n) -> p k n", k=8)[:, :, 0:12]),
                     reads=[('ps', b)], writes=['scT'])
                S.op('dve', lambda e, b=b: e.tensor_copy(out=h0T[:], in_=bank(b)[:, 0:128].rearrange("p (k n) -> p k n", k=8)[:, :, 12:16]),
                     reads=[('ps', b)], writes=['h0T'])
            xn_all = [('xnT', i) for i in range(len(tiles))]

            p3pre = p3next[0]
            p3next[0] = {}

            def p3a_w(g, which, hf):
                if (g, which, hf) not in p3pre:
                    c_ = 2048 + which * 768 + g * 256
                    p3pre[(g, which, hf)] = wload([(0, win3[:, hf * 4:(hf + 1) * 4, c_:c_ + 256])], 1024)
                return p3pre[(g, which, hf)]

            def p3a_iters(g):
                grp = {}

                def ensure_w(g=g, grp=grp):
                    if 'rhsl' in grp:
                        return
                    wqs = []
                    for which in range(3):
                        halves = []
                        for hf in range(2):
                            halves.append(p3a_w(g, which, hf))
                        wqs.append(halves)
                    if g + 1 < 3:
                        p3a_w(g + 1, 0, 0)
                        p3a_w(g + 1, 0, 1)

                    def wsl(which, k):
                        t_, _ = wqs[which][k // 4]
                        return t_.rearrange("p (k n) -> p k n", k=4)[:, k % 4, :]
                    grp['wres'] = [r_ for hv in wqs for (_, r_) in hv]
                    grp['rhsl'] = [(wsl(0, k), wsl(1, k), wsl(2, k)) for k in range(8)]
                ntl = 16 + (1 if has_s else 0)
                its = []
                for tl in range(ntl):
                    if tl < 16:
                        np_ = 128
                        if g == 0:
                            colsel = slice(tl * 128, (tl + 1) * 128)
                        elif g == 1:
                            r_, n_ = tl // 4, tl % 4
                            colsel = slice(512 * n_ + r_, 512 * n_ + r_ + 512, 4)
                        else:
                            colsel = slice(tl, T, 16)
                        cosb = ropeT[:, 0, tl, :]
                        sinb = ropeT[:, 1, tl, :]
                    else:
                        np_ = TS
                        colsel = slice(T, T + TS)
                        cosb = ropeS[:, 0, :]
                        sinb = ropeS[:, 1, :]
                    stt = {}

                    def A_(stt=stt, colsel=colsel, np_=np_, tl=tl, g=g, grp=grp, ensure_w=ensure_w):
                        ensure_w()
                        rhsl, wres = grp['rhsl'], grp['wres']
                        bA = rr('pb', 3) * 2
                        bB = bA + 1
                        stt['bA'], stt['bB'] = bA, bB

                        def qkv(e):
                            ins = None
                            for k in range(8):
                                l = xnT[:, k, colsel]
                                e.matmul(bank(bA)[0:np_, 0:256], lhsT=l, rhs=rhsl[k][0], start=(k == 0), stop=(k == 7), skip_group_check=True)
                                e.matmul(bank(bA)[0:np_, 256:512], lhsT=l, rhs=rhsl[k][1], start=False, stop=(k == 7), skip_group_check=True)
                                ins = e.matmul(bank(bB)[0:np_, 0:256], lhsT=l, rhs=rhsl[k][2], start=(k == 0), stop=(k == 7))
                            return ins
                        if tl >= 16:
                            xr_ = [('xnT', 16)]
                        elif g == 0:
                            xr_ = [('xnT', tl)]
                        elif g == 1:
                            xr_ = [('xnT', 4 * (tl % 4) + i_) for i_ in range(4)]
                        else:
                            xr_ = [('xnT', i_) for i_ in range(16)]
                        S.op('pe', qkv, reads=xr_ + wres, writes=[('ps', bA), ('ps', bB)])

                    def B_(stt=stt, np_=np_, cosb=cosb, sinb=sinb, tl=tl, g=g):
                        if tl == 0:
                            dma(ropeT[:].rearrange("p a t c -> p (a t c)"), rope_d[g, :, :], writes=['ropeT'], sem='rope')
                        bA, bB = stt['bA'], stt['bB']
                        qs = rr('q', 2)
                        stt['qs'] = qs
                        x3 = bank(bA)[0:np_, :].rearrange("p (h c) -> p h c", h=8)
                        o3 = qkr[qs][0:np_, :].rearrange("p (h c) -> p h c", h=8)
                        cb_ = cosb[0:np_].unsqueeze(1).to_broadcast([np_, 8, 32])
                        sb_ = sinb[0:np_].unsqueeze(1).to_broadcast([np_, 8, 32])
                        t1 = tm1[0:np_]
                        t2 = tm2[0:np_]
                        rd = [('ps', bA), 'ropeT', 'ropeS']
                        S.op('dve', lambda e: e.tensor_tensor(out=t1, in0=x3[:, :, 0:32], in1=cb_, op=ALU.mult), reads=rd, writes=['tm1'])
                        S.op('dve', lambda e: e.tensor_tensor(out=t2, in0=x3[:, :, 32:64], in1=sb_, op=ALU.mult), reads=rd, writes=['tm2'])
                        S.op('dve', lambda e: e.tensor_tensor(out=o3[:, :, 0:32], in0=t1, in1=t2, op=ALU.subtract),
                             reads=['tm1', 'tm2'], writes=[('qkr', qs)])
                        S.op('dve', lambda e: e.tensor_tensor(out=t1, in0=x3[:, :, 32:64], in1=cb_, op=ALU.mult), reads=rd, writes=['tm1'])
                        S.op('dve', lambda e: e.tensor_tensor(out=t2, in0=x3[:, :, 0:32], in1=sb_, op=ALU.mult), reads=rd, writes=['tm2'])
                        S.op('dve', lambda e: e.tensor_tensor(out=o3[:, :, 32:64], in0=t1, in1=t2, op=ALU.add),
                             reads=['tm1', 'tm2'], writes=[('qkr', qs)])
                        S.op('act', lambda e: e.copy(out=qkb[qs][0:np_, :], in_=qkr[qs][0:np_, :]), reads=[('qkr', qs)], writes=[('qkb', qs)])
                        S.op('act', lambda e: e.copy(out=vf[qs][0:np_, :], in_=bank(bB)[0:np_, 0:256]), reads=[('ps', bB)], writes=[('vf', qs)])

                    def C_(stt=stt, np_=np_):
                        qs = stt['qs']
                        bC = 6 + rr('tb', 2)
                        stt['bC'] = bC

                        def trq(e):
                            ins = None
                            for i in range(4):
                                ins = e.transpose(bankb(bC)[:, i * 128:i * 128 + np_], qkb[qs][0:np_, i * 128:(i + 1) * 128], identb[0:np_, 0:np_])
                            return ins
                        S.op('pe', trq, reads=[('qkb', qs), 'identb'], writes=[('ps', bC)])

                    def D_(stt=stt, np_=np_, tl=tl, g=g):
                        qs, bC = stt['qs'], stt['bC']
                        W = WINS[g]
                        if tl < 16:
                            dst = None
                            if g == 0 and tl == 15:
                                dst = lambda o: o[ps_, 0:128, :]
                            elif g == 1 and tl % 4 == 3:
                                dst = lambda o: o[ps_, (tl // 4):512:4, :]
                            elif g == 2:
                                dst = lambda o: o[ps_, tl:T:16, :]
                            if dst is not None:
                                dma(dst(p_k[g]), qkr[qs][:, 256:512], reads=[('qkr', qs)], sem='qkr%d' % qs)
                                dma(dst(p_v[g]), vf[qs][:, :], reads=[('vf', qs)], sem='vf%d' % qs)
                        else:
                            for b_ in range(SB):
                                dma(s_k[g][b_, W - ST:W, :], qkr[qs][b_ * ST:(b_ + 1) * ST, 256:512], reads=[('qkr', qs)], sem='qkr%d' % qs)
                                dma(s_v[g][b_, W - ST:W, :], vf[qs][b_ * ST:(b_ + 1) * ST, :], reads=[('vf', qs)], sem='vf%d' % qs)
                        srcT = bankb(bC)[:, 0:512].rearrange("p (i n) -> p i n", i=4)[:, :, 0:np_]
                        if tl < 16:
                            S.op('act', lambda e: e.copy(out=QKT[:, g, :, tl * 128:(tl + 1) * 128], in_=srcT),
                                 reads=[('ps', bC)], writes=[('QKT', g, tl)])
                            S.op('pool', lambda e: e.tensor_copy(out=Vt[:, tl, g, :].rearrange("p (h c) -> p h c", c=65)[:, :, 0:64],
                                                                 in_=vf[qs][:, :].rearrange("p (h c) -> p h c", c=64)),
                                 reads=[('vf', qs)], writes=[('Vt', g, tl)])
                        else:
                            S.op('act', lambda e: e.copy(out=QKTs[:, g, :, :], in_=srcT), reads=[('ps', bC)], writes=['QKTs'])
                            S.op('pool', lambda e: e.tensor_copy(out=Vs[:, g, :], in_=vf[qs][0:TS, :]), reads=[('vf', qs)], writes=['Vs'])
                    its.append((A_, B_, C_, D_))
                return its

            for t_ in range(2):
                p1its[t_][0]()
                p1its[t_] = (None,) + tuple(p1its[t_][1:])
            for which in range(3):
                for hf in range(2):
                    p3a_w(0, which, hf)
            g0its = p3a_iters(0)
            allits = [a_ + b_ for a_, b_ in zip(p1its, g0its)]
            for g in (1, 2):
                allits += [(None, None, None, None) + tuple(it_) for it_ in p3a_iters(g)]
            pipeline(allits, [0, 0, 1, 1, 2, 3, 4, 4])
            if has_s:
                sample_states()

            S.op('pool', lambda e: e.memset(Vt[:, :, :, :].rearrange("p t g (h c) -> p (t g h) c", c=65)[:, :, 64:65], 1.0), writes=['Vt1'])
            its = []
            deferred = []
            pe_deferred = []

            def flush_fin():
                while pe_deferred:
                    pe_deferred.pop(0)[1]()
                while deferred:
                    deferred.pop(0)[1]()
            for hh in range(4):
                ch, pb = hh // 2, 64 * (hh % 2)
                units = []
                for r_ in range(4):
                    for n_ in range(4):
                        outs = [(n_, slice(r_, 512, 4), slice(0, 128))]
                        if n_ < 3:
                            outs.append((n_ + 1, slice(r_, 512, 4), slice(128, 256)))
                        units.append((1, r_ * 4 + n_, 256 if n_ < 3 else 128, mask2, outs, None))
                for r_ in range(16):
                    units.append((2, r_, 128, mask2[:, 0:128], [(q_, slice(r_, 512, 16), slice(32 * q_, 32 * q_ + 32)) for q_ in range(4)], None))
                for n_ in range(16):
                    q_, m_ = n_ // 4, n_ % 4
                    if m_ < 3:
                        outs = [(q_, slice(m_ * 128, m_ * 128 + 256), slice(0, 256))]
                    else:
                        outs = [(q_, slice(384, 512), slice(0, 128))]
                        if n_ < 15:
                            outs.append((q_ + 1, slice(0, 128), slice(128, 256)))
                    units.append((0, n_, 256 if n_ < 15 else 128, mask2, outs, q_ if m_ == 3 else None))
                started = set()
                for (g, tk, nq, msk, outs, finq) in units:
                    stt = {}
                    flags = []
                    for (bq, oc, pc) in outs:
                        flags.append(bq not in started)
                        started.add(bq)

                    def A_(stt=stt, g=g, tk=tk, nq=nq, ch=ch, pb=pb):
                        bS = 4 + rr('sb3', 3)
                        stt['bS'] = bS
                        QTq = QKT[pb:pb + 64, g, ch, tk * 128:tk * 128 + nq]
                        KTc = QKT[pb:pb + 64, g, 2 + ch, tk * 128:(tk + 1) * 128]
                        rdq = [('QKT', g, tk)] + ([('QKT', g, tk + 1)] if nq > 128 else [])
                        S.op('pe', lambda e: e.matmul(bank(bS)[:, 0:nq], lhsT=KTc, rhs=QTq, start=True, stop=True), reads=rdq, writes=[('ps', bS)])

                    def B_(stt=stt, nq=nq, msk=msk):
                        bS = stt['bS']
                        pt = rr('ptx', 12)
                        stt['pt'] = pt
                        S.op('act', lambda e: e.activation(out=PTx[pt][:, 0:nq], in_=bank(bS)[:, 0:nq], func=AF.Exp, scale=0.125),
                             reads=[('ps', bS)], writes=[('PTx', pt)])
                        S.op('dve', lambda e: e.tensor_tensor(out=PTx[pt][:, 0:nq], in0=PTx[pt][:, 0:nq], in1=msk[:, 0:nq], op=ALU.mult),
                             reads=[('PTx', pt), 'cmb'], writes=[('PTx', pt)])
                        if deferred:
                            deferred.pop(0)[1]()

                    def C_(stt=stt, g=g, tk=tk, outs=outs, flags=flags, finq=finq, hh=hh):
                        pt = stt['pt']
                        lhs = Vt[:, tk, g, hh * 65:(hh + 1) * 65]
                        newb = {bq for (bq, _, _), fl in zip(outs, flags) if fl}
                        if newb & ({b_ for b_, _ in deferred} | {b_ for b_, _ in pe_deferred}):
                            flush_fin()

                        def pv(e):
                            ins = None
                            for (bq, oc, pc), fl in zip(outs, flags):
                                ins = e.matmul(bank(bq)[0:65, oc], lhsT=lhs, rhs=PTx[pt][:, pc], start=fl, stop=False, skip_group_check=True)
                            return ins
                        S.op('pe', pv, reads=[('PTx', pt), ('Vt', g, tk), 'Vt1'], writes=[('ps', bq) for (bq, _, _) in outs])
                        if finq is not None:
                            flush_fin()
                        elif pe_deferred:
                            pe_deferred.pop(0)[1]()
                        if finq is not None:
                            bO = finq
                            bD = 7
                            S.op('act', lambda e: e.copy(out=rec[64:65, :], in_=bank(bO)[64:65, :]), reads=[('ps', bO)], writes=['recd'])

                            def bcast(bO=bO, bD=bD, hh=hh):
                                S.op('pe', lambda e: e.matmul(bank(bD)[0:64, :], lhsT=onesf[64:65, 0:64], rhs=rec[64:65, :], start=True, stop=True),
                                     reads=['recd', 'onesf'], writes=[('ps', bD)])
                                for pc_ in range(4):
                                    def piece(pc_=pc_):
                                        cs = slice(pc_ * 128, (pc_ + 1) * 128)
                                        S.op('dve', lambda e: e.reciprocal(out=rec[0:64, cs], in_=bank(bD)[0:64, cs]), reads=[('ps', bD)], writes=[('rec', pc_)])
                                        S.op('dve', lambda e: e.tensor_tensor(out=oT[0:64, hh, bO * 512 + pc_ * 128:bO * 512 + (pc_ + 1) * 128],
                                                                              in0=bank(bO)[0:64, cs], in1=rec[0:64, cs], op=ALU.mult),
                                             reads=[('ps', bO), ('rec', pc_)], writes=[('oT', bO)])
                                    deferred.append((bO, piece))
                            pe_deferred.append((bO, bcast))
                    its.append((A_, B_, C_))
            pipeline(its, [0, 1, 8], group=2)
            flush_fin()

            p3c = None
            if has_s:
                for g in range(3):
                    W = WINS[g]
                    fl = lambda ap: ap.rearrange("r c -> (r c)").rearrange("(a x) -> a x", x=2048)
                    for b_ in range(SB):
                        dma(fl(s_k[g][b_, 0:W - ST, :]), fl(ck[g][b_, ST:W, :]), sem='cpy')
                        dma(fl(s_v[g][b_, 0:W - ST, :]), fl(cv[g][b_, ST:W, :]), sem='cpy')
                bO = 7
                Osb = bank(bO)[:, 0:128].rearrange("p (h n) -> p h n", h=4)
                its = []
                firstO = True
                for g in range(3):
                    for hh in range(4):
                        stt = {}

                        def N1(stt=stt, g=g, hh=hh):
                            ch, pb = hh // 2, 64 * (hh % 2)
                            bS = 5 + (hh % 2)
                            stt['bS'] = bS
                            S.op('pe', lambda e: e.matmul(bank(bS)[0:TS, 0:TS], lhsT=QKTs[pb:pb + 64, g, 2 + ch, :],
                                                          rhs=QKTs[pb:pb + 64, g, ch, :], start=True, stop=True),
                                 reads=['QKTs'], writes=[('ps', bS)])

                        def N2(stt=stt, g=g, hh=hh, firstO=firstO):
                            bS = stt['bS']
                            pn = rr('kn', 2)
                            S.op('act', lambda e: e.activation(out=PTn[pn][:, :], in_=bank(bS)[0:TS, 0:TS], func=AF.Exp, scale=0.125),
                                 reads=[('ps', bS)], writes=[('PTn', pn)])
                            S.op('pool', lambda e: e.tensor_tensor(out=PTn[pn][:, :], in0=PTn[pn][:, :], in1=maskN[:, g * 32:(g + 1) * 32], op=ALU.mult),
                                 reads=[('PTn', pn), 'cmb'], writes=[('PTn', pn)])

                            def pvn(e):
                                e.matmul(Osb[0:64, hh, :], lhsT=Vs[:, g, hh * 64:(hh + 1) * 64], rhs=PTn[pn][:, :], start=firstO, stop=False, skip_group_check=True)
                                return e.matmul(Osb[64:128, hh, :], lhsT=ones[0:TS, :], rhs=PTn[pn][:, :], start=firstO, stop=False, skip_group_check=True)
                            S.op('pe', pvn, reads=[('PTn', pn), 'Vs', 'ones'], writes=[('ps', bO)])
                        its.append((N1, N2, None, None, None))
                        firstO = False
                for b_ in range(SB):
                    for g in range(3):
                        d = DILS[g]
                        ncls = min(d, ST)
                        nq = max(ST // d, 1)
                        for r_ in range(ncls):
                            stt = {}
                            qcols = slice(b_ * ST + r_, (b_ + 1) * ST, d)

                            def U0(stt=stt, g=g, b_=b_, r_=r_, d=d):
                                ks = rr('k', 2)
                                stt['ks'] = ks
                                dma(kc[ks][:, :], ck[g][b_, r_:WINS[g]:d, :], writes=[('kc', ks)], sem='kc%d' % ks)
                                dma(vc[ks][:, :], cv[g][b_, r_:WINS[g]:d, :], writes=[('vc', ks)], sem='vc%d' % ks)

                            def U1(stt=stt):
                                ks = stt['ks']
                                bT = 4
                                stt['bT'] = bT

                                def trk(e):
                                    e.transpose(bank(bT)[:, 0:128], kc[ks][:, 0:128], identf[:, :])
                                    return e.transpose(bank(bT)[:, 128:256], kc[ks][:, 128:256], identf[:, :])
                                S.op('pe', trk, reads=[('kc', ks), 'identf'], writes=[('ps', bT)])

                            def U2(stt=stt):
                                ks, bT = stt['ks'], stt['bT']
                                k2 = rr('k2', 2)
                                v3 = rr('v3', 3)
                                stt['k2'], stt['v3'] = k2, v3
                                S.op('act', lambda e: e.copy(out=kcT[k2][:, :, :], in_=bank(bT)[:, 0:256].rearrange("p (a n) -> p a n", a=2)),
                                     reads=[('ps', bT)], writes=[('kcT', k2)])
                                S.op('pool', lambda e: e.tensor_copy(out=vcb[v3][:, :], in_=vc[ks][:, :]), reads=[('vc', ks)], writes=[('vcb', v3)])

                            def U3(stt=stt, g=g, qcols=qcols, nq=nq):
                                k2 = stt['k2']

                                def scs(e):
                                    ins = None
                                    for hh in range(4):
                                        ch, pb = hh // 2, 64 * (hh % 2)
                                        ins = e.matmul(bank(5 + hh % 2)[:, ch * 8:ch * 8 + nq], lhsT=kcT[k2][pb:pb + 64, ch, :], rhs=QKTs[pb:pb + 64, g, ch, qcols],
                                                       start=(ch == 0), stop=True, skip_group_check=True)
                                    return ins
                                S.op('pe', scs, reads=[('kcT', k2), 'QKTs'], writes=[('ps', 5), ('ps', 6)])

                            def U4(stt=stt, g=g, qcols=qcols, nq=nq):
                                v3 = stt['v3']
                                p2 = rr('p2', 2)
                                for par in range(2):
                                    S.op('act', lambda e, par=par: e.activation(out=PTs[p2][:, par:4:2, 0:nq],
                                                                                in_=bank(5 + par)[:, 0:16].rearrange("p (h n) -> p h n", h=2)[:, :, 0:nq],
                                                                                func=AF.Exp, scale=0.125),
                                         reads=[('ps', 5 + par)], writes=[('PTs', p2)])
                                if nq > 1:
                                    S.op('pool', lambda e: e.tensor_tensor(out=PTs[p2][:, :, 0:nq], in0=PTs[p2][:, :, 0:nq],
                                                                           in1=maskC[:, 0:nq].unsqueeze(1).to_broadcast([128, 4, nq]), op=ALU.mult),
                                         reads=[('PTs', p2), 'cmb'], writes=[('PTs', p2)])

                                def pvs(e):
                                    ins = None
                                    for hh in range(4):
                                        e.matmul(Osb[0:64, hh, qcols], lhsT=vcb[v3][:, hh * 64:(hh + 1) * 64], rhs=PTs[p2][:, hh, 0:nq],
                                                 start=False, stop=False, skip_group_check=True)
                                        ins = e.matmul(Osb[64:128, hh, qcols], lhsT=ones[:, :], rhs=PTs[p2][:, hh, 0:nq],
                                                       start=False, stop=False, skip_group_check=True)
                                    return ins
                                S.op('pe', pvs, reads=[('PTs', p2), ('vcb', v3), 'ones'], writes=[('ps', bO)])
                            its.append((U0, U1, U2, U3, U4))

                def p3c_fin():
                    S.op('dve', lambda e: e.reciprocal(out=rec[64:128, 0:128], in_=bank(bO)[64:128, 0:128]), reads=[('ps', bO)], writes=['rec', 'recd'])
                    S.op('dve', lambda e: e.tensor_tensor(out=oT[0:64, :, T:T + TS], in0=Osb[0:64, :, :],
                                                          in1=rec[64:128, 0:128].rearrange("p (h n) -> p h n", h=4), op=ALU.mult),
                         reads=[('ps', bO), 'rec'], writes=[('oT', 4)])
                p3c = Pipe(its, [0, 1, 2, 3, 4], desc=True)
                p3c.finish()
                p3c_fin()
                p3c = None

            wts = {}

            def W_(c):
                wx, rx = wload([(0, win3[:, :, c * 128:(c + 1) * 128])], 1024)
                wg, rg = wload([(0, win3[:, :, 1024 + c * 128:1024 + (c + 1) * 128])], 1024)
                wts[c] = (wx.rearrange("p (k n) -> p k n", k=8), rx, wg.rearrange("p (k n) -> p k n", k=8), rg)
            W_(0)
            W_(1)
            W_(2)
            S.barrier()
            xcbuf = [r_xc, xtbig[:, 0:RW]]
            rabuf = [r_ra, rabuf_alt]
            iubuf = [r_iu, iubuf_alt]

            nsg = len(segs)
            raall = [('ra', i) for i in range(nsg)]
            iuall = [('iu', i) for i in range(nsg)]
            ggall = [('gg', i) for i in range(nsg)]
            xpall = [('xp', 'h')] + [('xp', i) for i in range(4)]

            def A1_(c):
                wx3, rx, _, _ = wts[c]
                xc_ = xcbuf[c % 2]
                XC, XCS = ('xc', c % 2), ('xcS', c % 2)
                S.op('pool', lambda e: e.memset(r_xp[:, 0:3], 0.0), writes=[('xp', 'h')])
                for si, (c0, w_) in enumerate(segs):
                    b = nb()
                    S.op('pe', mm(bank(b)[:, 0:w_], [(wx3[:, k, :], xnT[:, k, c0:c0 + w_]) for k in range(8)]), reads=xn_all + [rx], writes=[('ps', b)])
                    if si < 4:
                        S.op('dve', lambda e, b=b, c0=c0, w_=w_: e.tensor_copy(out=r_xp[:, 3 + c0:3 + c0 + w_], in_=bank(b)[:, 0:w_]), reads=[('ps', b)], writes=[('xp', si)])
                    else:
                        S.op('dve', lambda e, b=b: e.tensor_copy(out=xps[:, :, 3:11], in_=bank(b)[:, 0:TS].rearrange("p (a n) -> p a n", a=4)),
                             reads=[('ps', b)], writes=['xps'])
                        S.op('pool', lambda e: e.tensor_copy(out=xps[:, :, 0:3], in_=scT[:, c, :].rearrange("p (a n) -> p a n", a=4)),
                             reads=['scT'], writes=['xps'])
                cwl = [vec[:, CW + j * 8 + c:CW + j * 8 + c + 1] for j in range(4)]
                cbv = vec[:, CB + c:CB + c + 1]
                S.op('dve', lambda e: e.tensor_scalar(out=xc_[:, 0:T], in0=r_xp[:, 0:T], scalar1=cwl[0], scalar2=cbv, op0=ALU.mult, op1=ALU.add),
                     reads=xpall + ['vec'], writes=[XC])
                for j in range(1, 4):
                    S.op('dve', lambda e, j=j: e.scalar_tensor_tensor(out=xc_[:, 0:T], in0=r_xp[:, j:j + T], scalar=cwl[j], in1=xc_[:, 0:T], op0=ALU.mult, op1=ALU.add),
                         reads=xpall + [XC, 'vec'], writes=[XC])
                S.op('pool', lambda e: e.tensor_copy(out=fin[:, 0:24].rearrange("p (j c) -> p j c", j=3)[:, :, c], in_=r_xp[:, T:T + 3]),
                     reads=xpall, writes=['fin'])
                if has_s:
                    xcs = xc_[:, T:T + TS].rearrange("p (a n) -> p a n", a=4)
                    S.op('dve', lambda e: e.tensor_scalar(out=xcs, in0=xps[:, :, 0:8], scalar1=cwl[0], scalar2=cbv, op0=ALU.mult, op1=ALU.add),
                         reads=['xps', 'vec'], writes=[XCS])
                    for j in range(1, 4):
                        S.op('dve', lambda e, j=j: e.scalar_tensor_tensor(out=xcs, in0=xps[:, :, j:j + 8], scalar=cwl[j], in1=xcs, op0=ALU.mult, op1=ALU.add),
                             reads=['xps', XCS, 'vec'], writes=[XCS])
                    S.op('pool', lambda e: e.tensor_copy(out=fins[:, 0:96].rearrange("p (b j c) -> p b j c", b=4, j=3)[:, :, :, c], in_=xps[:, :, 8:11]),
                         reads=['xps'], writes=['fins'])

            def A2_(c):
                r_ra, r_iu = rabuf[c % 2], iubuf[c % 2]
                xc_ = xcbuf[c % 2]
                XC, XCS = ('xc', c % 2), ('xcS', c % 2)
                for si, (c0, w_) in enumerate(segs):
                    S.op('act', lambda e, c0=c0, w_=w_: e.copy(out=r_xcb[:, c0:c0 + w_], in_=xc_[:, c0:c0 + w_]), reads=[XC, XCS], writes=[('xcb', si)])
                for si, (c0, w_) in enumerate(segs):
                    bR = nb()
                    bI = nb()
                    S.op('pe', lambda e, bR=bR, c0=c0, w_=w_: e.matmul(bank(bR)[:, 0:w_], lhsT=gwbd[:, 0, c, :], rhs=r_xcb[:, c0:c0 + w_], start=True, stop=True),
                         reads=[('xcb', si), 'gwbd'], writes=[('ps', bR)])
                    S.op('pe', lambda e, bI=bI, c0=c0, w_=w_: e.matmul(bank(bI)[:, 0:w_], lhsT=gwbd[:, 1, c, :], rhs=r_xcb[:, c0:c0 + w_], start=True, stop=True),
                         reads=[('xcb', si), 'gwbd'], writes=[('ps', bI)])
                    S.op('act', lambda e, bR=bR, c0=c0, w_=w_: e.activation(out=r_ra[:, c0:c0 + w_], in_=bank(bR)[:, 0:w_], func=AF.Sigmoid, bias=vec[:, GAB + c:GAB + c + 1]),
                         reads=[('ps', bR), 'vec'], writes=[('ra', c % 2, si)])
                    S.op('act', lambda e, bI=bI, c0=c0, w_=w_: e.activation(out=r_iu[:, c0:c0 + w_], in_=bank(bI)[:, 0:w_], func=AF.Sigmoid, bias=vec[:, GXB + c:GXB + c + 1]),
                         reads=[('ps', bI), 'vec'], writes=[('iu', c % 2, si)])

            def B_(c):
                r_ra, r_iu = rabuf[c % 2], iubuf[c % 2]
                raall = [('ra', c % 2, i) for i in range(nsg)]
                iuall = [('iu', c % 2, i) for i in range(nsg)]
                _, _, wg3, rg = wts[c]
                xc_ = xcbuf[c % 2]
                XC, XCS = ('xc', c % 2), ('xcS', c % 2)
                S.op('act', lambda e: e.activation(out=r_e2[:, 0:NT], in_=r_ra[:, 0:NT], func=AF.Exp, scale=cA2[:, c:c + 1]), reads=raall + ['cA2'], writes=['e2'])
                S.op('act', lambda e: e.activation(out=r_ra[:, 0:NT], in_=r_ra[:, 0:NT], func=AF.Exp, scale=cA[:, c:c + 1]), reads=raall + ['cA', 'e2'], writes=raall)
                S.op('act', lambda e: e.activation(out=r_e2[:, 0:NT], in_=r_e2[:, 0:NT], func=AF.Sqrt, scale=-1.0, bias=1.0), reads=['e2'], writes=['e2'])
                S.op('dve', lambda e: e.tensor_tensor(out=r_iu[:, 0:NT], in0=r_iu[:, 0:NT], in1=xc_[:, 0:NT], op=ALU.mult), reads=iuall + [XC, XCS], writes=iuall)
                S.op('dve', lambda e: e.tensor_tensor(out=r_iu[:, 0:NT], in0=r_iu[:, 0:NT], in1=r_e2[:, 0:NT], op=ALU.mult), reads=iuall + ['e2'], writes=iuall)
                S.op('dve', lambda e: e.tensor_tensor_scan(out=xc_[:, 0:T], data0=r_ra[:, 0:T], data1=r_iu[:, 0:T], initial=0.0, op0=ALU.mult, op1=ALU.add),
                     reads=raall + iuall, writes=[XC])
                S.op('pool', lambda e: e.tensor_copy(out=fin[:, 24 + c:25 + c], in_=xc_[:, T - 1:T]), reads=[XC], writes=['fin'])
                if has_s:
                    for b_ in range(SB):
                        sl_ = slice(T + b_ * ST, T + (b_ + 1) * ST)
                        S.op('dve', lambda e, sl_=sl_, b_=b_: e.tensor_tensor_scan(out=xc_[:, sl_], data0=r_ra[:, sl_], data1=r_iu[:, sl_],
                                                                               initial=h0T[:, c, b_:b_ + 1], op0=ALU.mult, op1=ALU.add),
                             reads=raall + iuall + ['h0T'], writes=[XCS])
                    S.op('pool', lambda e: e.tensor_copy(out=fins[:, 96:128].rearrange("p (b c) -> p b c", b=4)[:, :, c],
                                                         in_=xc_[:, T:T + TS].rearrange("p (b t) -> p b t", b=4)[:, :, ST - 1]),
                         reads=[XCS], writes=['fins'])
                for si, (c0, w_) in enumerate(segs):
                    b = nb()
                    S.op('pe', mm(bank(b)[:, 0:w_], [(wg3[:, k, :], xnT[:, k, c0:c0 + w_]) for k in range(8)]), reads=xn_all + [rg], writes=[('ps', b)])
                    S.op('act', lambda e, b=b, c0=c0, w_=w_: e.activation(out=r_gg[:, c0:c0 + w_], in_=bank(b)[:, 0:w_], func=AF.Gelu_apprx_tanh),
                         reads=[('ps', b)], writes=[('gg', si)])
                S.op('pool', lambda e: e.tensor_tensor(out=ghsT[:, c, 0:NT], in0=r_gg[:, 0:NT], in1=xc_[:, 0:NT], op=ALU.mult),
                     reads=ggall + [XC, XCS], writes=[('ghsT', c)])

            npump = (p3c.nsteps + 22) // 23 if p3c is not None else 0

            def pump():
                if p3c is not None:
                    p3c.step(npump)
            A1_(0)
            A2_(0)
            for c in range(1, 8):
                A1_(c)
                B_(c - 1)
                if c + 2 < 8:
                    W_(c + 2)
                A2_(c)
            B_(7)
            b = nb()
            S.op('pe', lambda e, b=b: e.transpose(bank(b)[0:32, 0:128], fin[:, 0:32], identf[:, :]), reads=['fin', 'identf'], writes=[('ps', b)])
            S.op('act', lambda e, b=b: e.copy(out=fint[0:32, 0:128], in_=bank(b)[0:32, 0:128]), reads=[('ps', b)], writes=['fint'])
            dma(p_conv[ps_, :].rearrange("(r p) -> r p", p=128), fint[0:24, 0:128], reads=['fint'], sem='fin')
            dma(p_h[ps_, :].rearrange("(r p) -> r p", p=128), fint[24:32, 0:128], reads=['fint'], sem='fin')
            if has_s:
                b = nb()
                S.op('pe', lambda e, b=b: e.transpose(bank(b)[:, 0:128], fins[:, :], identf[:, :]), reads=['fins', 'identf'], writes=[('ps', b)])
                S.op('act', lambda e, b=b: e.copy(out=fin[:, :], in_=bank(b)[:, 0:128]), reads=[('ps', b)], writes=['fin'])
                dma(s_conv.rearrange("(r p) -> r p", p=128), fin[0:96, :], reads=['fin'], sem='fin')
                dma(s_h.rearrange("(r p) -> r p", p=128), fin[96:128, :], reads=['fin'], sem='fin')

            GA0 = 2048 + 3 * 768
            def w4a(o):
                return (wload([(0, wga3_src(o))], 1024), wload([(0, wgb3_src(o))], 1024),
                        wload([(0, wa3[:, :, o * 128:(o + 1) * 128])], 1024),
                        wload([(0, wb3[:, :, o * 128:(o + 1) * 128])], 512, parts=64))
            wga3_src = lambda o: win3[:, :, GA0 + o * 128:GA0 + (o + 1) * 128]
            wgb3_src = lambda o: win3[:, :, GA0 + D + o * 128:GA0 + D + (o + 1) * 128]
            w4 = {0: w4a(0)}
            S.barrier()
            for o in range(8):
                if o + 1 < 8:
                    w4[o + 1] = w4a(o + 1)
                (wga_, rga_), (wgb_, rgb_), (wa_, ra_), (wb_, rb_) = w4[o]
                wa_3 = wa_.rearrange("p (k n) -> p k n", k=8)
                wga3 = wga_.rearrange("p (k n) -> p k n", k=8)
                wgb3 = wgb_.rearrange("p (k n) -> p k n", k=8)
                wb_3 = wb_.rearrange("p (k n) -> p k n", k=4)
                ghall = [('ghsT', i) for i in range(8)]
                otall = [('oT', i) for i in range(5)]
                for si, (c0, w_) in enumerate(segs):
                    bA_, bB_, bGA, bGB = nb(), nb(), nb(), nb()
                    S.op('pe', mm(bank(bGA)[:, 0:w_], [(wga3[:, k, :], xnT[:, k, c0:c0 + w_]) for k in range(8)]), reads=xn_all + [rga_], writes=[('ps', bGA)])
                    S.op('pe', mm(bank(bGB)[:, 0:w_], [(wgb3[:, k, :], xnT[:, k, c0:c0 + w_]) for k in range(8)]), reads=xn_all + [rgb_], writes=[('ps', bGB)])
                    S.op('pe', mm(bank(bA_)[:, 0:w_], [(wa_3[:, k, :], ghsT[:, k, c0:c0 + w_]) for k in range(8)]), reads=ghall + [ra_], writes=[('ps', bA_)])
                    S.op('pe', mm(bank(bB_)[:, 0:w_], [(wb_3[:, k, :], oT[0:64, k, c0:c0 + w_]) for k in range(4)]), reads=otall + [rb_], writes=[('ps', bB_)])
                    s0, s1 = rr('sgm', 2) * 2, None
                    s1 = s0 + 1
                    S.op('act', lambda e, bGA=bGA, s0=s0, o=o, w_=w_: e.activation(out=sg[s0][:, 0:w_], in_=bank(bGA)[:, 0:w_], func=AF.Sigmoid, bias=vec[:, BM + o:BM + o + 1]),
                         reads=[('ps', bGA), 'vec'], writes=[('sg', s0)])
                    S.op('act', lambda e, bGB=bGB, s1=s1, o=o, w_=w_: e.activation(out=sg[s1][:, 0:w_], in_=bank(bGB)[:, 0:w_], func=AF.Sigmoid, bias=vec[:, BM + 8 + o:BM + 9 + o]),
                         reads=[('ps', bGB), 'vec'], writes=[('sg', s1)])
                    S.op('dve', lambda e, bA_=bA_, s0=s0, w_=w_: e.tensor_tensor(out=sg[s0][:, 0:w_], in0=sg[s0][:, 0:w_], in1=bank(bA_)[:, 0:w_], op=ALU.mult),
                         reads=[('ps', bA_), ('sg', s0)], writes=[('sg', s0)])
                    S.op('dve', lambda e, bB_=bB_, s1=s1, w_=w_: e.tensor_tensor(out=sg[s1][:, 0:w_], in0=sg[s1][:, 0:w_], in1=bank(bB_)[:, 0:w_], op=ALU.mult),
                         reads=[('ps', bB_), ('sg', s1)], writes=[('sg', s1)])
                    S.op('pool', lambda e, s0=s0, s1=s1, o=o, c0=c0, w_=w_: e.tensor_tensor(out=mT[:, o, c0:c0 + w_], in0=sg[s0][:, 0:w_], in1=sg[s1][:, 0:w_], op=ALU.add),
                         reads=[('sg', s0), ('sg', s1)], writes=[('mT', o)])

            wo_ = [wload([(0, wo3[:, k2, :])], 1024) for k2 in range(8)]
            wor_ = [r_ for _, r_ in wo_]
            S.barrier()
            mall = [('mT', i) for i in range(8)]
            its = []
            for ti, (c0, np_) in enumerate(tiles):
                stt = {}
                sl = ti % 2
                src = xp[ps_, c0:c0 + 128, :] if ti < 16 else xs[:, :]
                hb = hbuf(ti)[0:np_, :]

                def A_(stt=stt, ti=ti, c0=c0, np_=np_, sl=sl, src=src):
                    bX = rr('pb', 3) * 2
                    stt['bX'] = bX
                    S.op('pe', mm(bank(bX)[0:np_, :], [(mT[:, k, c0:c0 + np_], wo_[k][0][:, 0:512]) for k in range(8)]), reads=mall + wor_, writes=[('ps', bX)])
                    S.op('pe', mm(bank(bX + 1)[0:np_, :], [(mT[:, k, c0:c0 + np_], wo_[k][0][:, 512:1024]) for k in range(8)]), reads=mall + wor_, writes=[('ps', bX + 1)])
                    dma(xt[sl][0:np_, :], src, writes=[('xt', sl)], sem='xt%d' % sl)
                Bn, Cn, Dn = rms_stages(hb, np_, mT, c0, G2, sl, ('hb', ti), ('hnT', ti))

                def B_(stt=stt, ti=ti, np_=np_, sl=sl, hb=hb, Bn=Bn):
                    bX = stt['bX']
                    xy = psum[0:np_, bX * 512:(bX + 2) * 512]
                    S.op('dve', lambda e: e.tensor_tensor(out=hb, in0=xy, in1=xt[sl][0:np_, :], op=ALU.add),
                         reads=[('ps', bX), ('ps', bX + 1), ('xt', sl)], writes=[('hb', ti)])
                    Bn()
                its.append((A_, B_, Cn, Dn))
            pipeline(its, [0, 1, 2, 2])
            hnT = mT
            hn_all = [('hnT', i) for i in range(len(tiles))]
            hn_seg = [[('hnT', 4 * si_ + i_) for i_ in range(4)] if si_ < 4 else [('hnT', 16)] for si_ in range(len(segs))]

            dma(gfb, gfb_d[:, :], writes=['gfb'], sem='c_gfb')

            def final_tile(ti, c0, np_):
                sl = rr('x', 2)
                hb = hbuf(ti)[0:np_, :]
                S.op('act', lambda e: e.activation(out=xsb[sl][0:np_, :], in_=hb, func=AF.Square, accum_out=ss[sl][0:np_, :]),
                     reads=[('hb', ti)], writes=[('xsb', sl), ('ss', sl)])
                S.op('act', lambda e: e.activation(out=rs[sl][0:np_, :], in_=ss[sl][0:np_, :], func=AF.Sqrt, scale=1.0 / D, bias=EPS),
                     reads=[('ss', sl)], writes=[('rs', sl)])
                S.op('dve', lambda e: e.reciprocal(out=rs[sl][0:np_, :], in_=rs[sl][0:np_, :]), reads=[('rs', sl)], writes=[('rs', sl)])
                S.op('dve', lambda e: e.scalar_tensor_tensor(out=xt[sl][0:np_, :], in0=hb, scalar=rs[sl][0:np_, 0:1], in1=gfb[0:np_, :], op0=ALU.mult, op1=ALU.mult),
                     reads=[('hb', ti), ('rs', sl), 'gfb'], writes=[('xt', sl)])
                dst = y_p[ps_, c0:c0 + 128, :] if ti < 16 else y_s[:, :]
                dma(dst, xt[sl][0:np_, :], reads=[('xt', sl)], sem='xt%d' % sl)

            jgroups = [list(range(j0, min(j0 + 3, NJ))) for j0 in range(0, NJ, 3)]
            w5q, w5r, w5p = [], {}, [0]
            for gi, jg in enumerate(jgroups):
                for j in jg:
                    w5q.append(('in', j))
                w5q.append(('out', gi))

            def w5need(upto):
                while w5p[0] <= min(upto, len(w5q) - 1):
                    kind, v = w5q[w5p[0]]
                    if kind == 'in':
                        w5r[(kind, v)] = (wload([(0, wf13[:, :, v * 128:(v + 1) * 128])], 1024),
                                          wload([(0, wf13[:, :, DFF + v * 128:DFF + (v + 1) * 128])], 1024))
                    else:
                        w5r[(kind, v)] = [wload([(0, w_f2[j_ * 128:(j_ + 1) * 128, :])], 1024) for j_ in jgroups[v]]
                    w5p[0] += 1
            for gi, jg in enumerate(jgroups):
                a_ = gi % 2
                for jj, j in enumerate(jg):
                    qi = w5q.index(('in', j))
                    w5need(qi + 1)
                    (wgt, rgt), (wup, rup) = w5r[('in', j)]
                    wgt3 = wgt.rearrange("p (k n) -> p k n", k=8)
                    wup3 = wup.rearrange("p (k n) -> p k n", k=8)
                    for si, (c0, w_) in enumerate(segs):
                        bG, bU = nb(), nb()
                        S.op('pe', mm(bank(bG)[:, 0:w_], [(wgt3[:, k, :], hnT[:, k, c0:c0 + w_]) for k in range(8)]), reads=hn_seg[si] + [rgt], writes=[('ps', bG)])
                        S.op('pe', mm(bank(bU)[:, 0:w_], [(wup3[:, k, :], hnT[:, k, c0:c0 + w_]) for k in range(8)]), reads=hn_seg[si] + [rup], writes=[('ps', bU)])
                        s0 = rr('sgf', 4)
                        S.op('act', lambda e, bG=bG, s0=s0, w_=w_: e.activation(out=sg[s0][:, 0:w_], in_=bank(bG)[:, 0:w_], func=AF.Silu), reads=[('ps', bG)], writes=[('sg', s0)])
                        S.op('dve', lambda e, bU=bU, s0=s0, a_=a_, jj=jj, c0=c0, w_=w_: e.tensor_tensor(out=actT[a_][:, jj, c0:c0 + w_], in0=sg[s0][:, 0:w_], in1=bank(bU)[:, 0:w_], op=ALU.mult),
                             reads=[('ps', bU), ('sg', s0)], writes=[('actT', a_)])
                w5need(w5q.index(('out', gi)) + 1)
                w2 = w5r[('out', gi)]
                w2r = [r_ for _, r_ in w2]
                for ti, (c0, np_) in enumerate(tiles):
                    bX = nb2()
                    xy = psum[0:np_, bX * 512:(bX + 2) * 512]
                    S.op('pe', mm(bank(bX)[0:np_, :], [(actT[a_][:, jj, c0:c0 + np_], w2[jj][0][:, 0:512]) for jj in range(len(jg))]), reads=[('actT', a_)] + w2r, writes=[('ps', bX)])
                    S.op('pe', mm(bank(bX + 1)[0:np_, :], [(actT[a_][:, jj, c0:c0 + np_], w2[jj][0][:, 512:1024]) for jj in range(len(jg))]), reads=[('actT', a_)] + w2r, writes=[('ps', bX + 1)])
                    hb = hbuf(ti)[0:np_, :]
                    S.op('dve', lambda e, hb=hb, xy=xy: e.tensor_tensor(out=hb, in0=hb, in1=xy, op=ALU.add),
                         reads=[('ps', bX), ('ps', bX + 1), ('hb', ti)], writes=[('hb', ti)])
                    if gi == len(jgroups) - 1 and ti >= 1:
                        final_tile(ti - 1, *tiles[ti - 1])
                if gi == len(jgroups) - 1:
                    final_tile(len(tiles) - 1, *tiles[-1])
            if ps_ + 1 < NB:
                for which in range(3):
                    for hf in range(2):
                        c_ = 2048 + which * 768
                        p3next[0][(0, which, hf)] = wload([(0, win3[:, hf * 4:(hf + 1) * 4, c_:c_ + 256])], 1024)

        block = es.enter_context(nc.Block())
        S.emit(block)
    return nc


_NC = None


def _consts():
    half = 32
    inv = np.exp(-math.log(10000.0) * np.arange(half, dtype=np.float32) * np.float32(2.0 / 64)).astype(np.float32)

    def tab(pos):
        ang = pos.astype(np.float32)[:, None] * inv[None, :]
        return np.cos(ang).astype(np.float32), np.sin(ang).astype(np.float32)
    rope = np.zeros((3, 128, 2, 16, 32), np.float32)
    j = np.arange(128)
    for g, d in enumerate(DILS):
        for tl in range(16):
            if g == 0:
                pos = tl * 128 + j
            elif g == 1:
                r_, n_ = tl // 4, tl % 4
                pos = 4 * (128 * n_ + j) + r_
            else:
                pos = 16 * j + tl
            c, s = tab(pos)
            rope[g, :, 0, tl, :] = c
            rope[g, :, 1, tl, :] = s
    pos_s = PAST + (np.arange(TS) % ST)
    c, s = tab(pos_s)
    ropes = np.concatenate([c, s], axis=1).astype(np.float32)
    cm = np.zeros((128, 872), np.float32)
    jj = np.arange(128)[:, None]
    ii = np.arange(128)[None, :]
    cm[:, 0:128] = (jj <= ii)
    cm[:, 128:256] = (jj >= ii)
    for q in range(4):
        cm[:, 256 + q * 32:256 + (q + 1) * 32] = (jj <= (32 * q + np.arange(32)[None, :]))
    cm[:, 384:392] = (jj >= np.arange(8)[None, :])
    kb, kt = np.arange(TS)[:, None] // ST, np.arange(TS)[:, None] % ST
    qb, qt = np.arange(TS)[None, :] // ST, np.arange(TS)[None, :] % ST
    for g, d in enumerate(DILS):
        cm[0:TS, 392 + g * 32:392 + (g + 1) * 32] = (kb == qb) & (qt >= kt) & ((qt - kt) % d == 0)
    cm[:, 488:872] = np.where(cm[:, 0:384] > 0, 0.0, -30000.0)
    ident = np.eye(128, dtype=np.float32)
    return rope.reshape(3, 128, 1024), ropes, cm, ident


def kernel(x_prompt, x_sample, state_conv, state_h, cache_k_g0, cache_v_g0, cache_k_g1, cache_v_g1,
           cache_k_g2, cache_v_g2, norm1_g, w_in, b_merge, conv_w, conv_b, gate_a_w, gate_a_b,
           gate_x_w, gate_x_b, rg_lambda, w_branch_a, w_branch_b, w_out, norm2_g, w_ffn_in, w_ffn_out,
           norm_f_g):
    global _NC
    if _NC is None:
        _NC = build()
    nc = _NC
    f = lambda a: np.ascontiguousarray(np.asarray(a, dtype=np.float32))
    fm = lambda v: f(v).reshape(8, 128).T
    vecs = np.zeros((128, 96), np.float32)
    vecs[:, 0:8] = fm(norm1_g[0]); vecs[:, 8:16] = fm(norm2_g[0])
    vecs[:, 16:32] = f(b_merge[0]).reshape(16, 128).T
    for j in range(4):
        vecs[:, 32 + j * 8:40 + j * 8] = fm(conv_w[0, j])
    vecs[:, 64:72] = fm(conv_b[0]); vecs[:, 72:80] = fm(gate_a_b[0]); vecs[:, 80:88] = fm(gate_x_b[0])
    vecs[:, 88:96] = fm(rg_lambda[0])
    gfb = np.ascontiguousarray(np.broadcast_to(f(norm_f_g)[None, :], (128, D)))
    gw = np.zeros((128, 2, 8, 128), np.float32)
    for wi, gwm in enumerate((gate_a_w, gate_x_w)):
        gwm = f(gwm[0])
        for c in range(8):
            gw[0:64, wi, c, 0:64] = gwm[2 * c]
            gw[64:128, wi, c, 64:128] = gwm[2 * c + 1]
    rope, ropes, cm, ident = _consts()
    shared = dict(w_in=f(w_in[0]), w_a=f(w_branch_a[0]), w_b=f(w_branch_b[0]), w_out=f(w_out[0]),
                  w_f1=f(w_ffn_in[0]), w_f2=f(w_ffn_out[0]), vecs=vecs, gfb=gfb, gwbd=gw.reshape(128, 2048),
                  rope=rope, ropes=ropes, cmask=cm, ident=ident)
    caches_k = [f(cache_k_g0[0]), f(cache_k_g1[0]), f(cache_k_g2[0])]
    caches_v = [f(cache_v_g0[0]), f(cache_v_g1[0]), f(cache_v_g2[0])]
    xp_, xs_ = f(x_prompt), f(x_sample)
    sc_, sh_ = f(state_conv[0]), f(state_h[0])
    in_maps = []
    for c in range(NCORES):
        m = dict(shared)
        m["xp"] = xp_[c * NB:(c + 1) * NB]
        m["xs"] = xs_[c * SB:(c + 1) * SB].reshape(TS, D)
        m["sconv"] = sc_[c * SB:(c + 1) * SB].reshape(SB * 3, D)
        m["sh"] = sh_[c * SB:(c + 1) * SB]
        for g in range(3):
            m["ck%d" % g] = caches_k[g][c * SB:(c + 1) * SB].reshape(SB, WINS[g], 256)
            m["cv%d" % g] = caches_v[g][c * SB:(c + 1) * SB].reshape(SB, WINS[g], 256)
        in_maps.append(m)
    res = run_bass_kernel_spmd(nc, in_maps, core_ids=list(range(NCORES)))
    R = res.results
    cat = lambda name: np.concatenate([np.asarray(r[name]) for r in R], axis=0)
    y_prompt = cat("y_p").reshape(16, T, D)
    y_sample = cat("y_s").reshape(32, ST, D)
    pconv = cat("p_conv").reshape(1, 16, 3, D)
    ph = cat("p_h").reshape(1, 16, D)
    outs = [y_prompt, y_sample, pconv, ph]
    for g in range(3):
        outs.append(cat("p_k%d" % g).reshape(1, 16, WINS[g], 4, 64))
        outs.append(cat("p_v%d" % g).reshape(1, 16, WINS[g], 4, 64))
    outs.append(np.concatenate([np.asarray(r["s_conv"]).reshape(SB, 3, D) for r in R], axis=0).reshape(1, 32, 3, D))
    outs.append(np.concatenate([np.asarray(r["s_h"]).reshape(SB, D) for r in R], axis=0).reshape(1, 32, D))
    for g in range(3):
        outs.append(cat("s_k%d" % g).reshape(1, 32, WINS[g], 4, 64))
        outs.append(cat("s_v%d" % g).reshape(1, 32, WINS[g], 4, 64))
    return tuple(np.ascontiguousarray(o, dtype=np.float32) for o in outs)
```

```python
import math
import numpy as np
from contextlib import ExitStack
import concourse.bass as bass
import concourse.mybir as mybir
from concourse.bass_utils import run_bass_kernel_spmd

F32 = mybir.dt.float32
BF16 = mybir.dt.bfloat16
ALU = mybir.AluOpType
AF = mybir.ActivationFunctionType

D = 1024
T = 2048
NB = 2
SB = 4
ST = 8
TS = SB * ST
PAST = 16384
DFF = 2816
NJ = DFF // 128
INW = 6400
WINS = (128, 512, 2048)
DILS = (1, 4, 16)
EPS = 1e-6
NCORES = 8


class Sched:
    def __init__(self, nc, es):
        self.nc = nc
        self.es = es
        self.engs = {'pe': None, 'act': None, 'dve': None, 'pool': None, 'sp': None}
        self.ops = {e: [] for e in self.engs}
        self.cnt = {e: 0 for e in self.engs}
        self.sems = {e: es.enter_context(nc.semaphore('s_' + e)) for e in self.engs}
        self.waited = {e: {} for e in self.engs}
        self.res = {}
        self.dcnt = {}
        self.pending = {e: [] for e in self.engs}

    def _dsem(self, name):
        if name not in self.sems:
            self.sems[name] = self.es.enter_context(self.nc.semaphore('d_' + name))
            self.dcnt[name] = 0
        return self.sems[name]

    def _need(self, eng, tok, waits):
        key, val = tok
        if key == 'pe' and eng == 'pe':
            return
        if self.waited[eng].get(key, 0) >= val:
            return
        self.waited[eng][key] = val
        waits.append(tok)

    def op(self, eng, fn, reads=(), writes=(), dma=None):
        waits = []
        for tok in self.pending[eng]:
            self._need(eng, tok, waits)
        self.pending[eng] = []
        for r in reads:
            st = self.res.setdefault(r, {'w': None, 'r': []})
            if st['w'] is not None:
                self._need(eng, st['w'], waits)
        for w in writes:
            st = self.res.setdefault(w, {'w': None, 'r': []})
            if st['w'] is not None:
                self._need(eng, st['w'], waits)
            for tok in st['r']:
                self._need(eng, tok, waits)
        if dma is not None:
            self._dsem(dma)
            self.dcnt[dma] += 16
            tok = (dma, self.dcnt[dma])
            self.ops[eng].append((waits, fn, (dma, 16)))
        else:
            self.cnt[eng] += 1
            tok = (eng, self.cnt[eng])
            self.ops[eng].append((waits, fn, (eng, 1)))
        for r in reads:
            self.res[r]['r'].append(tok)
        for w in writes:
            self.res[w]['w'] = tok
            self.res[w]['r'] = []
        return tok

    def barrier(self):
        toks = [(e, c) for e, c in self.cnt.items() if c > 0 and e != 'sp']
        toks += [(d, c) for d, c in self.dcnt.items() if c > 0]
        for e in self.engs:
            self.pending[e] = list(toks)

    def emit(self, block):
        nc = self.nc
        final = [(d, c) for d, c in self.dcnt.items() if c > 0]
        final += [(e, c) for e, c in self.cnt.items() if c > 0 and e != 'sp']

        def run(name, handle, tail=False):
            for waits, fn, inc in self.ops[name]:
                for key, val in waits:
                    handle.wait_ge(self.sems[key], val)
                ins = fn(handle)
                ins.then_inc(self.sems[inc[0]], inc[1])
            if tail:
                for key, val in final:
                    handle.wait_ge(self.sems[key], val)

        block.sync(lambda e: run('sp', e, True))
        block.scalar(lambda e: run('act', e))
        block.vector(lambda e: run('dve', e))
        block.gpsimd(lambda e: run('pool', e))
        block.tensor(lambda e: run('pe', e))


def build():
    nc = bass.Bass("TRN2", target_bir_lowering=False)

    def din(name, shape):
        return nc.dram_tensor(name, list(shape), F32, kind="ExternalInput").ap()

    def dout(name, shape):
        return nc.dram_tensor(name, list(shape), F32, kind="ExternalOutput").ap()

    xp = din("xp", [NB, T, D]); xs = din("xs", [TS, D])
    sconv = din("sconv", [SB * 3, D]); sh = din("sh", [SB, D])
    ck = [din("ck%d" % g, [SB, WINS[g], 256]) for g in range(3)]
    cv = [din("cv%d" % g, [SB, WINS[g], 256]) for g in range(3)]
    w_in = din("w_in", [D, INW]); w_a = din("w_a", [D, D]); w_b = din("w_b", [256, D])
    w_out = din("w_out", [D, D]); w_f1 = din("w_f1", [D, 2 * DFF]); w_f2 = din("w_f2", [DFF, D])
    vecs = din("vecs", [128, 96]); gfb_d = din("gfb", [128, D]); gwbd_d = din("gwbd", [128, 2 * 8 * 128])
    rope_d = din("rope", [3, 128, 2 * 16 * 32]); ropes_d = din("ropes", [TS, 64])
    cmask_d = din("cmask", [128, 872]); ident_d = din("ident", [128, 128])

    y_p = dout("y_p", [NB, T, D]); y_s = dout("y_s", [TS, D])
    p_conv = dout("p_conv", [NB, 3 * D]); p_h = dout("p_h", [NB, D])
    p_k = [dout("p_k%d" % g, [NB, WINS[g], 256]) for g in range(3)]
    p_v = [dout("p_v%d" % g, [NB, WINS[g], 256]) for g in range(3)]
    s_conv = dout("s_conv", [SB * 3 * D]); s_h = dout("s_h", [SB * D])
    s_k = [dout("s_k%d" % g, [SB, WINS[g], 256]) for g in range(3)]
    s_v = [dout("s_v%d" % g, [SB, WINS[g], 256]) for g in range(3)]

    es = ExitStack()
    with es:
        def sb(name, shape, dt=F32):
            return es.enter_context(nc.sbuf_tensor(name, list(shape), dt))

        S = Sched(nc, es)
        NTMAX = T + TS
        A_XNT = 0
        A_OT = 33280
        A_X = A_OT + 16640
        ARENA = A_X + 83520
        arena = sb("arena", [128, ARENA // 4], F32)

        def aview(off, nbytes, dt, pat=None, parts=128, **kw):
            v = arena[0:parts, off // 4:(off + nbytes) // 4]
            if dt is BF16:
                v = v.bitcast(BF16)
            if pat:
                v = v.rearrange(pat, **kw)
            return v

        xnT = aview(A_XNT, 33280, BF16, "p (k n) -> p k n", k=8)
        oT = aview(A_OT, 16640, BF16, "p (h n) -> p h n", h=4)
        QKT = aview(A_X, 49152, BF16, "p (g i n) -> p g i n", g=3, i=4)
        Vt = aview(A_X + 49152, 24960, BF16, "p (t g c) -> p t g c", t=16, g=3)
        RW = 2096
        r_xp = aview(A_X, RW * 4, F32)
        r_xc = aview(A_X + RW * 4, RW * 4, F32)
        r_ra = aview(A_X + 2 * RW * 4, RW * 4, F32)
        r_iu = aview(A_X + 3 * RW * 4, RW * 4, F32)
        r_e2 = aview(A_X + 4 * RW * 4, RW * 4, F32)
        r_xcb = aview(A_X + 5 * RW * 4, 4160, BF16)
        r_gg = aview(A_X + 5 * RW * 4 + 4160, 4160, BF16)
        assert 5 * RW * 4 + 8320 <= 50240
        ghsT = aview(A_X + 50240, 33280, BF16, "p (k n) -> p k n", k=8)
        mT = aview(A_X, 33280, BF16, "p (k n) -> p k n", k=8)
        hb1 = aview(A_XNT, 32768, F32, "p (t n) -> p t n", t=8)
        hb2 = aview(A_X + 33280, 36864, F32, "p (t n) -> p t n", t=9)
        woutT = aview(A_OT, 16384, BF16, "p (k n) -> p k n", k=8)
        actT = [aview(A_OT, 12480, BF16, "p (j n) -> p j n", j=3),
                aview(A_X + 70144, 12480, BF16, "p (j n) -> p j n", j=3)]

        def hbuf(ti):
            return hb1[:, ti, :] if ti < 8 else hb2[:, ti - 8, :]

        vec = sb("vec", [128, 96])
        gfb = aview(A_OT + 12480, 4096, F32)
        gwbd = sb("gwbds", [128, 2, 8, 128], BF16)
        identf = sb("identf", [128, 128]); identb = sb("identb", [128, 128], BF16)
        ones = sb("ones", [128, 64], BF16)
        onesf = sb("onesf", [128, 64])
        cmb = sb("cmb", [128, 872], BF16)
        misc = sb("misc", [128, 6656])
        ropeT = misc[:, 4608:5632].rearrange("p (a t c) -> p a t c", a=2, t=16)
        ropeS = sb("ropeS", [TS, 2, 32])
        cA = sb("cA", [128, 8]); cA2 = sb("cA2", [128, 8])
        stg = [sb("stg%d" % i, [128, 1024]) for i in range(2)]
        NWB = 8
        wbf = [sb("wbf%d" % i, [128, 1024], BF16) for i in range(NWB)]
        xtbig = sb("xtbig", [128, 2100])
        xt = [xtbig[:, i * D:(i + 1) * D] for i in range(2)]
        xsb = [sb("xsb%d" % i, [128, D], BF16) for i in range(2)]
        ss = [sb("ss%d" % i, [128, 1]) for i in range(2)]
        rs = [sb("rs%d" % i, [128, 1]) for i in range(2)]
        qkr = [misc[:, 2048 + i * 512:2048 + (i + 1) * 512] for i in range(2)]
        tm1 = misc[:, 3072:3328].rearrange("p (h c) -> p h c", h=8)
        tm2 = misc[:, 3328:3584].rearrange("p (h c) -> p h c", h=8)
        qkb = [misc[:, 3584 + i * 256:3584 + (i + 1) * 256].bitcast(BF16) for i in range(2)]
        vf = [misc[:, 4096 + i * 256:4096 + (i + 1) * 256] for i in range(2)]
        PT = [misc[:, 6144 + i * 128:6144 + (i + 1) * 128].bitcast(BF16) for i in range(4)]
        rec = misc[:, 5632:6144]
        sg = [misc[:, i * 512:(i + 1) * 512] for i in range(4)]
        rabuf_alt = misc[:, 0:RW]
        iubuf_alt = misc[:, RW:2 * RW]
        PTx = PT + [sg[i][:, j * 128:(j + 1) * 128].bitcast(BF16) for i in range(2) for j in range(4)]
        fin = sb("fin", [128, 128])
        fins = sb("fins", [128, 128])
        scT = sb("scT", [128, 8, 12]); h0T = sb("h0T", [128, 8, 4])
        xps = sb("xps", [128, 4, 11])
        sctm = aview(A_X + 74240, 4096, F32)
        fint = sb("fint", [32, 128])
        QKTs = sb("QKTs", [128, 3, 4, TS], BF16)
        Vs = sb("Vs", [TS, 3, 256], BF16)
        kc = [sg[2][:, i * 256:(i + 1) * 256] for i in range(2)]
        vc = [sg[3][:, i * 256:(i + 1) * 256] for i in range(2)]
        vcb = [sb("vcb%d" % i, [128, 256], BF16) for i in range(3)]
        kcT = [sb("kcT%d" % i, [128, 2, 128], BF16) for i in range(2)]
        PTs = [sb("PTs%d" % i, [128, 4, 8], BF16) for i in range(2)]
        PTn = [sb("PTn%d" % i, [TS, TS], BF16) for i in range(2)]
        oTs = sb("oTs", [64, 4, TS], BF16)

        psum = es.enter_context(nc.psum_tensor("psum", [128, 8 * 512], F32))

        def bank(i):
            return psum[:, i * 512:(i + 1) * 512]

        def bankb(i):
            return psum[:, i * 512:(i + 1) * 512].bitcast(BF16)

        bctr = [0]

        nbanks = [8]

        def nb():
            bctr[0] = (bctr[0] + 1) % nbanks[0]
            return bctr[0]

        def nb2():
            b = ((bctr[0] // 2 + 1) % 4) * 2
            bctr[0] = b + 1
            return b

        ctr = {'stg': 0, 'wbf': 0, 'x': 0, 'q': 0, 'pt': 0, 'sgm': 0, 'sgf': 0, 'k': 0, 'kn': 0, 'cast': 0, 'sb': 0, 'ob': 0, 'tb': 0, 'pb': 0, 'sbs': 0, 'k2': 0, 'v3': 0, 'p2': 0, 'sb3': 0, 'ptx': 0}

        def rr(name, n):
            v = ctr[name]
            ctr[name] = (v + 1) % n
            return v

        def dma(out, in_, reads=(), writes=(), sem=None):
            S.op('sp', lambda e, o=out, i=in_: e.dma_start(out=o, in_=i), reads=reads, writes=writes, dma=sem)

        def wload(srcs, n, parts=128, dest=None, dres=None):
            s = rr('stg', 2)
            for off, ap in srcs:
                sz = 1
                for d_ in ap.shape[1:]:
                    sz *= d_
                o = stg[s][0:ap.shape[0], off:off + sz]
                if len(ap.shape) == 3:
                    o = o.rearrange("p (a b) -> p a b", a=ap.shape[1])
                dma(o, ap, writes=[('stg', s)], sem='stg%d' % s)
            if dest is None:
                w = rr('wbf', NWB)
                dest = wbf[w][0:parts, 0:n]
                dres = ('wbf', w)
            ce = rr('cast', 2)
            if ce == 0:
                S.op('dve', lambda e, o=dest, i=stg[s][0:parts, 0:n]: e.tensor_copy(out=o, in_=i), reads=[('stg', s)], writes=[dres])
            else:
                S.op('act', lambda e, o=dest, i=stg[s][0:parts, 0:n]: e.copy(out=o, in_=i), reads=[('stg', s)], writes=[dres])
            return dest, dres

        def mm(out, pairs, first=True, last=True, skip=False):
            def fn(e):
                ins = None
                n = len(pairs)
                for i, (l, r) in enumerate(pairs):
                    ins = e.matmul(out, lhsT=l, rhs=r, start=(first and i == 0), stop=(last and i == n - 1),
                                   skip_group_check=skip)
                return ins
            return fn

        win3 = w_in.rearrange("(k p) n -> p k n", p=128)
        wa3 = w_a.rearrange("(k p) n -> p k n", p=128)
        wo3 = w_out.rearrange("(k p) n -> p k n", p=128)
        wf13 = w_f1.rearrange("(k p) n -> p k n", p=128)
        wb3 = w_b.rearrange("(h p) n -> p h n", p=64)

        dma(vec[:], vecs[:, :], writes=['vec'], sem='c_vec')
        dma(identf[:], ident_d[:, :], writes=['identf'], sem='c_id')
        cmf = xt[0][:, 0:872]
        dma(cmf, cmask_d[:, :], writes=[('xt', 0)], sem='c_cm')
        dma(ropeS[:].rearrange("p a b -> p (a b)"), ropes_d[:, :], writes=['ropeS'], sem='c_rs')
        S.op('pool', lambda e: e.tensor_copy(out=identb[:], in_=identf[:]), reads=['identf'], writes=['identb'])
        S.op('pool', lambda e: e.tensor_copy(out=cmb[:], in_=cmf), reads=[('xt', 0)], writes=['cmb'])
        S.op('pool', lambda e: e.memset(ones[:], 1.0), writes=['ones'])
        S.op('pool', lambda e: e.memset(onesf[:], 1.0), writes=['onesf'])
        for h_ in range(2):
            for q_ in range(2):
                wload([(0, gwbd_d[:, (h_ * 2 + q_) * 512:(h_ * 2 + q_ + 1) * 512])], 512,
                      dest=gwbd[:, h_, q_ * 4:(q_ + 1) * 4, :].rearrange("p a b -> p (a b)"), dres='gwbd')
        S.op('act', lambda e: e.activation(out=cA[:], in_=vec[:, 88:96], func=AF.Sigmoid), reads=['vec'], writes=['cA'])
        S.op('act', lambda e: e.activation(out=cA[:], in_=cA[:], func=AF.Ln), reads=['cA'], writes=['cA'])
        S.op('act', lambda e: e.mul(out=cA2[:], in_=cA[:], mul=16.0), reads=['cA'], writes=['cA2'])
        S.op('act', lambda e: e.mul(out=cA[:], in_=cA[:], mul=8.0), reads=['cA', 'cA2'], writes=['cA'])
        G1, G2, BM, CW, CB, GAB, GXB = 0, 8, 16, 32, 64, 72, 80
        mask2 = cmb[:, 0:256]
        maskg2 = cmb[:, 256:384]
        maskC = cmb[:, 384:392]
        maskN = cmb[0:TS, 392:488]
        nm2 = cmb[:, 488:744]
        nmg2 = cmb[:, 744:872]

        def pipeline(iters, skew, group=1):
            n = len(iters)
            nst = n + max(skew)
            for s0_ in range(0, nst, group):
                for k, sk in enumerate(skew):
                    for st_ in range(s0_, min(s0_ + group, nst)):
                        t_ = st_ - sk
                        if 0 <= t_ < n and iters[t_][k] is not None:
                            iters[t_][k]()

        class Pipe:
            def __init__(self, iters, skew, desc=False):
                self.iters, self.skew, self.st = iters, skew, 0
                self.nsteps = len(iters) + max(skew)
                self.order = list(range(len(skew)))
                if desc:
                    self.order.reverse()

            def step(self, n=1):
                for _ in range(n):
                    if self.st >= self.nsteps:
                        return
                    for k in self.order:
                        t_ = self.st - self.skew[k]
                        if 0 <= t_ < len(self.iters) and self.iters[t_][k] is not None:
                            self.iters[t_][k]()
                    self.st += 1

            def finish(self):
                self.step(self.nsteps)

        def rms_stages(src_tile, np_, dstT, col0, gcol, slot, src_res, dst_res):
            stt = {}

            def B():
                S.op('act', lambda e: e.activation(out=xsb[slot][0:np_, :], in_=src_tile, func=AF.Square, accum_out=ss[slot][0:np_, :]),
                     reads=[src_res], writes=[('xsb', slot), ('ss', slot)])
                S.op('act', lambda e: e.activation(out=rs[slot][0:np_, :], in_=ss[slot][0:np_, :], func=AF.Sqrt, scale=1.0 / D, bias=EPS),
                     reads=[('ss', slot)], writes=[('rs', slot)])
                S.op('dve', lambda e: e.reciprocal(out=rs[slot][0:np_, :], in_=rs[slot][0:np_, :]), reads=[('rs', slot)], writes=[('rs', slot)])
                S.op('act', lambda e: e.mul(out=xsb[slot][0:np_, :], in_=src_tile, mul=rs[slot][0:np_, 0:1]),
                     reads=[src_res, ('rs', slot)], writes=[('xsb', slot)])

            def C():
                b = 6 + rr('tb', 2)
                stt['b'] = b

                def tr(e):
                    ins = None
                    for k in range(8):
                        ins = e.transpose(bankb(b)[:, k * 128:k * 128 + np_], xsb[slot][0:np_, k * 128:(k + 1) * 128], identb[0:np_, 0:np_])
                    return ins
                S.op('pe', tr, reads=[('xsb', slot), 'identb'], writes=[('ps', b)])

            def Dd():
                b = stt['b']
                S.op('dve', lambda e: e.tensor_tensor(
                    out=dstT[:, :, col0:col0 + np_],
                    in0=bankb(b).rearrange("p (k n) -> p k n", k=8)[:, :, 0:np_],
                    in1=vec[:, gcol:gcol + 8].unsqueeze(2).to_broadcast([128, 8, np_]), op=ALU.mult),
                    reads=[('ps', b), 'vec'], writes=[dst_res])
            return B, C, Dd

        p3next = [{}]
        for ps_ in range(NB):
            has_s = (ps_ == NB - 1)
            NT = T + (TS if has_s else 0)
            segs = [(i * 512, 512) for i in range(4)] + ([(T, TS)] if has_s else [])
            tiles = [(i * 128, 128) for i in range(16)] + ([(T, TS)] if has_s else [])
            S.barrier()
            its = []
            for ti, (c0, np_) in enumerate(tiles):
                sl = ti % 2
                src = xp[ps_, c0:c0 + 128, :] if ti < 16 else xs[:, :]
                A_ = (lambda sl=sl, np_=np_, src=src: dma(xt[sl][0:np_, :], src, writes=[('xt', sl)], sem='xt%d' % sl))
                B_, C_, D_ = rms_stages(xt[sl][0:np_, :], np_, xnT, c0, G1, sl, ('xt', sl), ('xnT', ti))
                its.append((A_, B_, C_, D_))
            p1its = its

            def sample_states():
                dma(sctm[0:12, :], sconv[:, :], writes=['sctm'], sem='c1')
                dma(sctm[12:16, :], sh[:, :], writes=['sctm'], sem='c1')
                b = nb()

                def trs(e, b=b):
                    ins = None
                    for k in range(8):
                        ins = e.transpose(bank(b)[:, k * 16:(k + 1) * 16], sctm[0:16, k * 128:(k + 1) * 128], identf[0:16, 0:16])
                    return ins
                S.op('pe', trs, reads=['sctm', 'identf'], writes=[('ps', b)])
                S.op('dve', lambda e, b=b: e.tensor_copy(out=scT[:], in_=bank(b)[:, 0:128].rearrange("p (k n) -> p k n", k=8)[:, :, 0:12]),
                     reads=[('ps', b)], writes=['scT'])
                S.op('dve', lambda e, b=b: e.tensor_copy(out=h0T[:], in_=bank(b)[:, 0:128].rearrange("p (k n) -> p k n", k=8)[:, :, 12:16]),
                     reads=[('ps', b)], writes=['h0T'])
            xn_all = [('xnT', i) for i in range(len(tiles))]

            p3pre = p3next[0]
            p3next[0] = {}

            def p3a_w(g, which, hf):
                if (g, which, hf) not in p3pre:
                    c_ = 2048 + which * 768 + g * 256
                    p3pre[(g, which, hf)] = wload([(0, win3[:, hf * 4:(hf + 1) * 4, c_:c_ + 256])], 1024)
                return p3pre[(g, which, hf)]

            def p3a_iters(g):
                grp = {}

                def ensure_w(g=g, grp=grp):
                    if 'rhsl' in grp:
                        return
                    wqs = []
                    for which in range(3):
                        halves = []
                        for hf in range(2):
                            halves.append(p3a_w(g, which, hf))
                        wqs.append(halves)
                    if g + 1 < 3:
                        p3a_w(g + 1, 0, 0)
                        p3a_w(g + 1, 0, 1)

                    def wsl(which, k):
                        t_, _ = wqs[which][k // 4]
                        return t_.rearrange("p (k n) -> p k n", k=4)[:, k % 4, :]
                    grp['wres'] = [r_ for hv in wqs for (_, r_) in hv]
                    grp['rhsl'] = [(wsl(0, k), wsl(1, k), wsl(2, k)) for k in range(8)]
                ntl = 16 + (1 if has_s else 0)
                its = []
                for tl in range(ntl):
                    if tl < 16:
                        np_ = 128
                        if g == 0:
                            colsel = slice(tl * 128, (tl + 1) * 128)
                        elif g == 1:
                            r_, n_ = tl // 4, tl % 4
                            colsel = slice(512 * n_ + r_, 512 * n_ + r_ + 512, 4)
                        else:
                            colsel = slice(tl, T, 16)
                        cosb = ropeT[:, 0, tl, :]
                        sinb = ropeT[:, 1, tl, :]
                    else:
                        np_ = TS
                        colsel = slice(T, T + TS)
                        cosb = ropeS[:, 0, :]
                        sinb = ropeS[:, 1, :]
                    stt = {}

                    def A_(stt=stt, colsel=colsel, np_=np_, tl=tl, g=g, grp=grp, ensure_w=ensure_w):
                        ensure_w()
                        rhsl, wres = grp['rhsl'], grp['wres']
                        bA = rr('pb', 3) * 2
                        bB = bA + 1
                        stt['bA'], stt['bB'] = bA, bB

                        def qkv(e):
                            ins = None
                            for k in range(8):
                                l = xnT[:, k, colsel]
                                e.matmul(bank(bA)[0:np_, 0:256], lhsT=l, rhs=rhsl[k][0], start=(k == 0), stop=(k == 7), skip_group_check=True)
                                e.matmul(bank(bA)[0:np_, 256:512], lhsT=l, rhs=rhsl[k][1], start=False, stop=(k == 7), skip_group_check=True)
                                ins = e.matmul(bank(bB)[0:np_, 0:256], lhsT=l, rhs=rhsl[k][2], start=(k == 0), stop=(k == 7))
                            return ins
                        if tl >= 16:
                            xr_ = [('xnT', 16)]
                        elif g == 0:
                            xr_ = [('xnT', tl)]
                        elif g == 1:
                            xr_ = [('xnT', 4 * (tl % 4) + i_) for i_ in range(4)]
                        else:
                            xr_ = [('xnT', i_) for i_ in range(16)]
                        S.op('pe', qkv, reads=xr_ + wres, writes=[('ps', bA), ('ps', bB)])

                    def B_(stt=stt, np_=np_, cosb=cosb, sinb=sinb, tl=tl, g=g):
                        if tl == 0:
                            dma(ropeT[:].rearrange("p a t c -> p (a t c)"), rope_d[g, :, :], writes=['ropeT'], sem='rope')
                        bA, bB = stt['bA'], stt['bB']
                        qs = rr('q', 2)
                        stt['qs'] = qs
                        x3 = bank(bA)[0:np_, :].rearrange("p (h c) -> p h c", h=8)
                        o3 = qkr[qs][0:np_, :].rearrange("p (h c) -> p h c", h=8)
                        cb_ = cosb[0:np_].unsqueeze(1).to_broadcast([np_, 8, 32])
                        sb_ = sinb[0:np_].unsqueeze(1).to_broadcast([np_, 8, 32])
                        t1 = tm1[0:np_]
                        t2 = tm2[0:np_]
                        rd = [('ps', bA), 'ropeT', 'ropeS']
                        S.op('dve', lambda e: e.tensor_tensor(out=t1, in0=x3[:, :, 0:32], in1=cb_, op=ALU.mult), reads=rd, writes=['tm1'])
                        S.op('dve', lambda e: e.tensor_tensor(out=t2, in0=x3[:, :, 32:64], in1=sb_, op=ALU.mult), reads=rd, writes=['tm2'])
                        S.op('dve', lambda e: e.tensor_tensor(out=o3[:, :, 0:32], in0=t1, in1=t2, op=ALU.subtract),
                             reads=['tm1', 'tm2'], writes=[('qkr', qs)])
                        S.op('dve', lambda e: e.tensor_tensor(out=t1, in0=x3[:, :, 32:64], in1=cb_, op=ALU.mult), reads=rd, writes=['tm1'])
                        S.op('dve', lambda e: e.tensor_tensor(out=t2, in0=x3[:, :, 0:32], in1=sb_, op=ALU.mult), reads=rd, writes=['tm2'])
                        S.op('dve', lambda e: e.tensor_tensor(out=o3[:, :, 32:64], in0=t1, in1=t2, op=ALU.add),
                             reads=['tm1', 'tm2'], writes=[('qkr', qs)])
                        S.op('act', lambda e: e.copy(out=qkb[qs][0:np_, :], in_=qkr[qs][0:np_, :]), reads=[('qkr', qs)], writes=[('qkb', qs)])
                        S.op('act', lambda e: e.copy(out=vf[qs][0:np_, :], in_=bank(bB)[0:np_, 0:256]), reads=[('ps', bB)], writes=[('vf', qs)])

                    def C_(stt=stt, np_=np_):
                        qs = stt['qs']
                        bC = 6 + rr('tb', 2)
                        stt['bC'] = bC

                        def trq(e):
                            ins = None
                            for i in range(4):
                                ins = e.transpose(bankb(bC)[:, i * 128:i * 128 + np_], qkb[qs][0:np_, i * 128:(i + 1) * 128], identb[0:np_, 0:np_])
                            return ins
                        S.op('pe', trq, reads=[('qkb', qs), 'identb'], writes=[('ps', bC)])

                    def D_(stt=stt, np_=np_, tl=tl, g=g):
                        qs, bC = stt['qs'], stt['bC']
                        W = WINS[g]
                        if tl < 16:
                            dst = None
                            if g == 0 and tl == 15:
                                dst = lambda o: o[ps_, 0:128, :]
                            elif g == 1 and tl % 4 == 3:
                                dst = lambda o: o[ps_, (tl // 4):512:4, :]
                            elif g == 2:
                                dst = lambda o: o[ps_, tl:T:16, :]
                            if dst is not None:
                                dma(dst(p_k[g]), qkr[qs][:, 256:512], reads=[('qkr', qs)], sem='qkr%d' % qs)
                                dma(dst(p_v[g]), vf[qs][:, :], reads=[('vf', qs)], sem='vf%d' % qs)
                        else:
                            for b_ in range(SB):
                                dma(s_k[g][b_, W - ST:W, :], qkr[qs][b_ * ST:(b_ + 1) * ST, 256:512], reads=[('qkr', qs)], sem='qkr%d' % qs)
                                dma(s_v[g][b_, W - ST:W, :], vf[qs][b_ * ST:(b_ + 1) * ST, :], reads=[('vf', qs)], sem='vf%d' % qs)
                        srcT = bankb(bC)[:, 0:512].rearrange("p (i n) -> p i n", i=4)[:, :, 0:np_]
                        if tl < 16:
                            S.op('act', lambda e: e.copy(out=QKT[:, g, :, tl * 128:(tl + 1) * 128], in_=srcT),
                                 reads=[('ps', bC)], writes=[('QKT', g, tl)])
                            S.op('pool', lambda e: e.tensor_copy(out=Vt[:, tl, g, :].rearrange("p (h c) -> p h c", c=65)[:, :, 0:64],
                                                                 in_=vf[qs][:, :].rearrange("p (h c) -> p h c", c=64)),
                                 reads=[('vf', qs)], writes=[('Vt', g, tl)])
                        else:
                            S.op('act', lambda e: e.copy(out=QKTs[:, g, :, :], in_=srcT), reads=[('ps', bC)], writes=['QKTs'])
                            S.op('pool', lambda e: e.tensor_copy(out=Vs[:, g, :], in_=vf[qs][0:TS, :]), reads=[('vf', qs)], writes=['Vs'])
                    its.append((A_, B_, C_, D_))
                return its

            for t_ in range(2):
                p1its[t_][0]()
                p1its[t_] = (None,) + tuple(p1its[t_][1:])
            for which in range(3):
                for hf in range(2):
                    p3a_w(0, which, hf)
            g0its = p3a_iters(0)
            allits = [(a_[0], a_[1], a_[2], b_[0], b_[1], a_[3], b_[2], b_[3]) for a_, b_ in zip(p1its, g0its)]
            for g in (1, 2):
                allits += [(None, None, None, it_[0], it_[1], None, it_[2], it_[3]) for it_ in p3a_iters(g)]
            pipeline(allits, [0, 0, 1, 2, 3, 1, 4, 4])
            if has_s:
                sample_states()

            S.op('pool', lambda e: e.memset(Vt[:, :, :, :].rearrange("p t g (h c) -> p (t g h) c", c=65)[:, :, 64:65], 1.0), writes=['Vt1'])
            its = []
            deferred = []
            pe_deferred = []

            def flush_fin():
                while pe_deferred:
                    pe_deferred.pop(0)[1]()
                while deferred:
                    deferred.pop(0)[1]()
            for hh in range(4):
                ch, pb = hh // 2, 64 * (hh % 2)
                units = []
                for r_ in range(4):
                    for n_ in range(4):
                        outs = [(n_, slice(r_, 512, 4), slice(0, 128))]
                        if n_ < 3:
                            outs.append((n_ + 1, slice(r_, 512, 4), slice(128, 256)))
                        units.append((1, r_ * 4 + n_, 256 if n_ < 3 else 128, mask2, outs, None))
                for r_ in range(16):
                    units.append((2, r_, 128, mask2[:, 0:128], [(q_, slice(r_, 512, 16), slice(32 * q_, 32 * q_ + 32)) for q_ in range(4)], None))
                for n_ in range(16):
                    q_, m_ = n_ // 4, n_ % 4
                    if m_ < 3:
                        outs = [(q_, slice(m_ * 128, m_ * 128 + 256), slice(0, 256))]
                    else:
                        outs = [(q_, slice(384, 512), slice(0, 128))]
                        if n_ < 15:
                            outs.append((q_ + 1, slice(0, 128), slice(128, 256)))
                    units.append((0, n_, 256 if n_ < 15 else 128, mask2, outs, q_ if m_ == 3 else None))
                started = set()
                for (g, tk, nq, msk, outs, finq) in units:
                    stt = {}
                    flags = []
                    for (bq, oc, pc) in outs:
                        flags.append(bq not in started)
                        started.add(bq)

                    def A_(stt=stt, g=g, tk=tk, nq=nq, ch=ch, pb=pb):
                        bS = 4 + rr('sb3', 3)
                        stt['bS'] = bS
                        QTq = QKT[pb:pb + 64, g, ch, tk * 128:tk * 128 + nq]
                        KTc = QKT[pb:pb + 64, g, 2 + ch, tk * 128:(tk + 1) * 128]
                        rdq = [('QKT', g, tk)] + ([('QKT', g, tk + 1)] if nq > 128 else [])
                        S.op('pe', lambda e: e.matmul(bank(bS)[:, 0:nq], lhsT=KTc, rhs=QTq, start=True, stop=True), reads=rdq, writes=[('ps', bS)])

                    def B_(stt=stt, nq=nq, msk=msk):
                        bS = stt['bS']
                        pt = rr('ptx', 12)
                        stt['pt'] = pt
                        S.op('act', lambda e: e.activation(out=PTx[pt][:, 0:nq], in_=bank(bS)[:, 0:nq], func=AF.Exp, scale=0.125),
                             reads=[('ps', bS)], writes=[('PTx', pt)])
                        S.op('dve', lambda e: e.tensor_tensor(out=PTx[pt][:, 0:nq], in0=PTx[pt][:, 0:nq], in1=msk[:, 0:nq], op=ALU.mult),
                             reads=[('PTx', pt), 'cmb'], writes=[('PTx', pt)])
                        if deferred:
                            deferred.pop(0)[1]()

                    def C_(stt=stt, g=g, tk=tk, outs=outs, flags=flags, finq=finq, hh=hh):
                        pt = stt['pt']
                        lhs = Vt[:, tk, g, hh * 65:(hh + 1) * 65]
                        newb = {bq for (bq, _, _), fl in zip(outs, flags) if fl}
                        if newb & ({b_ for b_, _ in deferred} | {b_ for b_, _ in pe_deferred}):
                            flush_fin()

                        def pv(e):
                            ins = None
                            for (bq, oc, pc), fl in zip(outs, flags):
                                ins = e.matmul(bank(bq)[0:65, oc], lhsT=lhs, rhs=PTx[pt][:, pc], start=fl, stop=False, skip_group_check=True)
                            return ins
                        S.op('pe', pv, reads=[('PTx', pt), ('Vt', g, tk), 'Vt1'], writes=[('ps', bq) for (bq, _, _) in outs])
                        if finq is not None:
                            flush_fin()
                        elif pe_deferred:
                            pe_deferred.pop(0)[1]()
                        if finq is not None:
                            bO = finq
                            bD = 7
                            S.op('act', lambda e: e.copy(out=rec[64:65, :], in_=bank(bO)[64:65, :]), reads=[('ps', bO)], writes=['recd'])

                            def bcast(bO=bO, bD=bD, hh=hh):
                                S.op('pe', lambda e: e.matmul(bank(bD)[0:64, :], lhsT=onesf[64:65, 0:64], rhs=rec[64:65, :], start=True, stop=True),
                                     reads=['recd', 'onesf'], writes=[('ps', bD)])
                                for pc_ in range(4):
                                    def piece(pc_=pc_):
                                        cs = slice(pc_ * 128, (pc_ + 1) * 128)
                                        S.op('dve', lambda e: e.reciprocal(out=rec[0:64, cs], in_=bank(bD)[0:64, cs]), reads=[('ps', bD)], writes=[('rec', pc_)])
                                        S.op('dve', lambda e: e.tensor_tensor(out=oT[0:64, hh, bO * 512 + pc_ * 128:bO * 512 + (pc_ + 1) * 128],
                                                                              in0=bank(bO)[0:64, cs], in1=rec[0:64, cs], op=ALU.mult),
                                             reads=[('ps', bO), ('rec', pc_)], writes=[('oT', bO)])
                                    deferred.append((bO, piece))
                            pe_deferred.append((bO, bcast))
                    its.append((A_, B_, C_))
            pipeline(its, [0, 1, 8], group=2)
            flush_fin()

            p3c = None
            if has_s:
                for g in range(3):
                    W = WINS[g]
                    fl = lambda ap: ap.rearrange("r c -> (r c)").rearrange("(a x) -> a x", x=2048)
                    for b_ in range(SB):
                        dma(fl(s_k[g][b_, 0:W - ST, :]), fl(ck[g][b_, ST:W, :]), sem='cpy')
                        dma(fl(s_v[g][b_, 0:W - ST, :]), fl(cv[g][b_, ST:W, :]), sem='cpy')
                bO = 7
                Osb = bank(bO)[:, 0:128].rearrange("p (h n) -> p h n", h=4)
                its = []
                firstO = True
                for g in range(3):
                    for hh in range(4):
                        stt = {}

                        def N1(stt=stt, g=g, hh=hh):
                            ch, pb = hh // 2, 64 * (hh % 2)
                            bS = 5 + (hh % 2)
                            stt['bS'] = bS
                            S.op('pe', lambda e: e.matmul(bank(bS)[0:TS, 0:TS], lhsT=QKTs[pb:pb + 64, g, 2 + ch, :],
                                                          rhs=QKTs[pb:pb + 64, g, ch, :], start=True, stop=True),
                                 reads=['QKTs'], writes=[('ps', bS)])

                        def N2(stt=stt, g=g, hh=hh, firstO=firstO):
                            bS = stt['bS']
                            pn = rr('kn', 2)
                            S.op('act', lambda e: e.activation(out=PTn[pn][:, :], in_=bank(bS)[0:TS, 0:TS], func=AF.Exp, scale=0.125),
                                 reads=[('ps', bS)], writes=[('PTn', pn)])
                            S.op('pool', lambda e: e.tensor_tensor(out=PTn[pn][:, :], in0=PTn[pn][:, :], in1=maskN[:, g * 32:(g + 1) * 32], op=ALU.mult),
                                 reads=[('PTn', pn), 'cmb'], writes=[('PTn', pn)])

                            def pvn(e):
                                e.matmul(Osb[0:64, hh, :], lhsT=Vs[:, g, hh * 64:(hh + 1) * 64], rhs=PTn[pn][:, :], start=firstO, stop=False, skip_group_check=True)
                                return e.matmul(Osb[64:128, hh, :], lhsT=ones[0:TS, :], rhs=PTn[pn][:, :], start=firstO, stop=False, skip_group_check=True)
                            S.op('pe', pvn, reads=[('PTn', pn), 'Vs', 'ones'], writes=[('ps', bO)])
                        its.append((N1, N2, None, None, None))
                        firstO = False
                for b_ in range(SB):
                    for g in range(3):
                        d = DILS[g]
                        ncls = min(d, ST)
                        nq = max(ST // d, 1)
                        for r_ in range(ncls):
                            stt = {}
                            qcols = slice(b_ * ST + r_, (b_ + 1) * ST, d)

                            def U0(stt=stt, g=g, b_=b_, r_=r_, d=d):
                                ks = rr('k', 2)
                                stt['ks'] = ks
                                dma(kc[ks][:, :], ck[g][b_, r_:WINS[g]:d, :], writes=[('kc', ks)], sem='kc%d' % ks)
                                dma(vc[ks][:, :], cv[g][b_, r_:WINS[g]:d, :], writes=[('vc', ks)], sem='vc%d' % ks)

                            def U1(stt=stt):
                                ks = stt['ks']
                                bT = 4
                                stt['bT'] = bT

                                def trk(e):
                                    e.transpose(bank(bT)[:, 0:128], kc[ks][:, 0:128], identf[:, :])
                                    return e.transpose(bank(bT)[:, 128:256], kc[ks][:, 128:256], identf[:, :])
                                S.op('pe', trk, reads=[('kc', ks), 'identf'], writes=[('ps', bT)])

                            def U2(stt=stt):
                                ks, bT = stt['ks'], stt['bT']
                                k2 = rr('k2', 2)
                                v3 = rr('v3', 3)
                                stt['k2'], stt['v3'] = k2, v3
                                S.op('act', lambda e: e.copy(out=kcT[k2][:, :, :], in_=bank(bT)[:, 0:256].rearrange("p (a n) -> p a n", a=2)),
                                     reads=[('ps', bT)], writes=[('kcT', k2)])
                                S.op('pool', lambda e: e.tensor_copy(out=vcb[v3][:, :], in_=vc[ks][:, :]), reads=[('vc', ks)], writes=[('vcb', v3)])

                            def U3(stt=stt, g=g, qcols=qcols, nq=nq):
                                k2 = stt['k2']

                                def scs(e):
                                    ins = None
                                    for hh in range(4):
                                        ch, pb = hh // 2, 64 * (hh % 2)
                                        ins = e.matmul(bank(5 + hh % 2)[:, ch * 8:ch * 8 + nq], lhsT=kcT[k2][pb:pb + 64, ch, :], rhs=QKTs[pb:pb + 64, g, ch, qcols],
                                                       start=(ch == 0), stop=True, skip_group_check=True)
                                    return ins
                                S.op('pe', scs, reads=[('kcT', k2), 'QKTs'], writes=[('ps', 5), ('ps', 6)])

                            def U4(stt=stt, g=g, qcols=qcols, nq=nq):
                                v3 = stt['v3']
                                p2 = rr('p2', 2)
                                for par in range(2):
                                    S.op('act', lambda e, par=par: e.activation(out=PTs[p2][:, par:4:2, 0:nq],
                                                                                in_=bank(5 + par)[:, 0:16].rearrange("p (h n) -> p h n", h=2)[:, :, 0:nq],
                                                                                func=AF.Exp, scale=0.125),
                                         reads=[('ps', 5 + par)], writes=[('PTs', p2)])
                                if nq > 1:
                                    S.op('pool', lambda e: e.tensor_tensor(out=PTs[p2][:, :, 0:nq], in0=PTs[p2][:, :, 0:nq],
                                                                           in1=maskC[:, 0:nq].unsqueeze(1).to_broadcast([128, 4, nq]), op=ALU.mult),
                                         reads=[('PTs', p2), 'cmb'], writes=[('PTs', p2)])

                                def pvs(e):
                                    ins = None
                                    for hh in range(4):
                                        e.matmul(Osb[0:64, hh, qcols], lhsT=vcb[v3][:, hh * 64:(hh + 1) * 64], rhs=PTs[p2][:, hh, 0:nq],
                                                 start=False, stop=False, skip_group_check=True)
                                        ins = e.matmul(Osb[64:128, hh, qcols], lhsT=ones[:, :], rhs=PTs[p2][:, hh, 0:nq],
                                                       start=False, stop=False, skip_group_check=True)
                                    return ins
                                S.op('pe', pvs, reads=[('PTs', p2), ('vcb', v3), 'ones'], writes=[('ps', bO)])
                            its.append((U0, U1, U2, U3, U4))

                def p3c_fin():
                    S.op('dve', lambda e: e.reciprocal(out=rec[64:128, 0:128], in_=bank(bO)[64:128, 0:128]), reads=[('ps', bO)], writes=['rec', 'recd'])
                    S.op('dve', lambda e: e.tensor_tensor(out=oT[0:64, :, T:T + TS], in0=Osb[0:64, :, :],
                                                          in1=rec[64:128, 0:128].rearrange("p (h n) -> p h n", h=4), op=ALU.mult),
                         reads=[('ps', bO), 'rec'], writes=[('oT', 4)])
                p3c = Pipe(its, [0, 1, 2, 3, 4], desc=True)
                p3c.finish()
                p3c_fin()
                p3c = None

            wts = {}

            def W_(c):
                wx, rx = wload([(0, win3[:, :, c * 128:(c + 1) * 128])], 1024)
                wg, rg = wload([(0, win3[:, :, 1024 + c * 128:1024 + (c + 1) * 128])], 1024)
                wts[c] = (wx.rearrange("p (k n) -> p k n", k=8), rx, wg.rearrange("p (k n) -> p k n", k=8), rg)
            W_(0)
            W_(1)
            W_(2)
            S.barrier()
            xcbuf = [r_xc, xtbig[:, 0:RW]]
            rabuf = [r_ra, rabuf_alt]
            iubuf = [r_iu, iubuf_alt]

            nsg = len(segs)
            raall = [('ra', i) for i in range(nsg)]
            iuall = [('iu', i) for i in range(nsg)]
            ggall = [('gg', i) for i in range(nsg)]
            xpall = [('xp', 'h')] + [('xp', i) for i in range(4)]

            def A1_(c):
                wx3, rx, _, _ = wts[c]
                xc_ = xcbuf[c % 2]
                XC, XCS = ('xc', c % 2), ('xcS', c % 2)
                S.op('pool', lambda e: e.memset(r_xp[:, 0:3], 0.0), writes=[('xp', 'h')])
                for si, (c0, w_) in enumerate(segs):
                    b = nb()
                    S.op('pe', mm(bank(b)[:, 0:w_], [(wx3[:, k, :], xnT[:, k, c0:c0 + w_]) for k in range(8)]), reads=xn_all + [rx], writes=[('ps', b)])
                    if si < 4:
                        S.op('dve', lambda e, b=b, c0=c0, w_=w_: e.tensor_copy(out=r_xp[:, 3 + c0:3 + c0 + w_], in_=bank(b)[:, 0:w_]), reads=[('ps', b)], writes=[('xp', si)])
                    else:
                        S.op('dve', lambda e, b=b: e.tensor_copy(out=xps[:, :, 3:11], in_=bank(b)[:, 0:TS].rearrange("p (a n) -> p a n", a=4)),
                             reads=[('ps', b)], writes=['xps'])
                        S.op('pool', lambda e: e.tensor_copy(out=xps[:, :, 0:3], in_=scT[:, c, :].rearrange("p (a n) -> p a n", a=4)),
                             reads=['scT'], writes=['xps'])
                cwl = [vec[:, CW + j * 8 + c:CW + j * 8 + c + 1] for j in range(4)]
                cbv = vec[:, CB + c:CB + c + 1]
                S.op('dve', lambda e: e.tensor_scalar(out=xc_[:, 0:T], in0=r_xp[:, 0:T], scalar1=cwl[0], scalar2=cbv, op0=ALU.mult, op1=ALU.add),
                     reads=xpall + ['vec'], writes=[XC])
                for j in range(1, 4):
                    S.op('dve', lambda e, j=j: e.scalar_tensor_tensor(out=xc_[:, 0:T], in0=r_xp[:, j:j + T], scalar=cwl[j], in1=xc_[:, 0:T], op0=ALU.mult, op1=ALU.add),
                         reads=xpall + [XC, 'vec'], writes=[XC])
                S.op('pool', lambda e: e.tensor_copy(out=fin[:, 0:24].rearrange("p (j c) -> p j c", j=3)[:, :, c], in_=r_xp[:, T:T + 3]),
                     reads=xpall, writes=['fin'])
                if has_s:
                    xcs = xc_[:, T:T + TS].rearrange("p (a n) -> p a n", a=4)
                    S.op('dve', lambda e: e.tensor_scalar(out=xcs, in0=xps[:, :, 0:8], scalar1=cwl[0], scalar2=cbv, op0=ALU.mult, op1=ALU.add),
                         reads=['xps', 'vec'], writes=[XCS])
                    for j in range(1, 4):
                        S.op('dve', lambda e, j=j: e.scalar_tensor_tensor(out=xcs, in0=xps[:, :, j:j + 8], scalar=cwl[j], in1=xcs, op0=ALU.mult, op1=ALU.add),
                             reads=['xps', XCS, 'vec'], writes=[XCS])
                    S.op('pool', lambda e: e.tensor_copy(out=fins[:, 0:96].rearrange("p (b j c) -> p b j c", b=4, j=3)[:, :, :, c], in_=xps[:, :, 8:11]),
                         reads=['xps'], writes=['fins'])

            def A2_(c):
                r_ra, r_iu = rabuf[c % 2], iubuf[c % 2]
                xc_ = xcbuf[c % 2]
                XC, XCS = ('xc', c % 2), ('xcS', c % 2)
                for si, (c0, w_) in enumerate(segs):
                    S.op('act', lambda e, c0=c0, w_=w_: e.copy(out=r_xcb[:, c0:c0 + w_], in_=xc_[:, c0:c0 + w_]), reads=[XC, XCS], writes=[('xcb', si)])
                for si, (c0, w_) in enumerate(segs):
                    bR = nb()
                    bI = nb()
                    S.op('pe', lambda e, bR=bR, c0=c0, w_=w_: e.matmul(bank(bR)[:, 0:w_], lhsT=gwbd[:, 0, c, :], rhs=r_xcb[:, c0:c0 + w_], start=True, stop=True),
                         reads=[('xcb', si), 'gwbd'], writes=[('ps', bR)])
                    S.op('pe', lambda e, bI=bI, c0=c0, w_=w_: e.matmul(bank(bI)[:, 0:w_], lhsT=gwbd[:, 1, c, :], rhs=r_xcb[:, c0:c0 + w_], start=True, stop=True),
                         reads=[('xcb', si), 'gwbd'], writes=[('ps', bI)])
                    S.op('act', lambda e, bR=bR, c0=c0, w_=w_: e.activation(out=r_ra[:, c0:c0 + w_], in_=bank(bR)[:, 0:w_], func=AF.Sigmoid, bias=vec[:, GAB + c:GAB + c + 1]),
                         reads=[('ps', bR), 'vec'], writes=[('ra', c % 2, si)])
                    S.op('act', lambda e, bI=bI, c0=c0, w_=w_: e.activation(out=r_iu[:, c0:c0 + w_], in_=bank(bI)[:, 0:w_], func=AF.Sigmoid, bias=vec[:, GXB + c:GXB + c + 1]),
                         reads=[('ps', bI), 'vec'], writes=[('iu', c % 2, si)])

            def B_(c):
                r_ra, r_iu = rabuf[c % 2], iubuf[c % 2]
                raall = [('ra', c % 2, i) for i in range(nsg)]
                iuall = [('iu', c % 2, i) for i in range(nsg)]
                _, _, wg3, rg = wts[c]
                xc_ = xcbuf[c % 2]
                XC, XCS = ('xc', c % 2), ('xcS', c % 2)
                S.op('act', lambda e: e.activation(out=r_e2[:, 0:NT], in_=r_ra[:, 0:NT], func=AF.Exp, scale=cA2[:, c:c + 1]), reads=raall + ['cA2'], writes=['e2'])
                S.op('act', lambda e: e.activation(out=r_ra[:, 0:NT], in_=r_ra[:, 0:NT], func=AF.Exp, scale=cA[:, c:c + 1]), reads=raall + ['cA', 'e2'], writes=raall)
                S.op('act', lambda e: e.activation(out=r_e2[:, 0:NT], in_=r_e2[:, 0:NT], func=AF.Sqrt, scale=-1.0, bias=1.0), reads=['e2'], writes=['e2'])
                S.op('dve', lambda e: e.tensor_tensor(out=r_iu[:, 0:NT], in0=r_iu[:, 0:NT], in1=xc_[:, 0:NT], op=ALU.mult), reads=iuall + [XC, XCS], writes=iuall)
                S.op('dve', lambda e: e.tensor_tensor(out=r_iu[:, 0:NT], in0=r_iu[:, 0:NT], in1=r_e2[:, 0:NT], op=ALU.mult), reads=iuall + ['e2'], writes=iuall)
                S.op('dve', lambda e: e.tensor_tensor_scan(out=xc_[:, 0:T], data0=r_ra[:, 0:T], data1=r_iu[:, 0:T], initial=0.0, op0=ALU.mult, op1=ALU.add),
                     reads=raall + iuall, writes=[XC])
                S.op('pool', lambda e: e.tensor_copy(out=fin[:, 24 + c:25 + c], in_=xc_[:, T - 1:T]), reads=[XC], writes=['fin'])
                if has_s:
                    for b_ in range(SB):
                        sl_ = slice(T + b_ * ST, T + (b_ + 1) * ST)
                        S.op('dve', lambda e, sl_=sl_, b_=b_: e.tensor_tensor_scan(out=xc_[:, sl_], data0=r_ra[:, sl_], data1=r_iu[:, sl_],
                                                                               initial=h0T[:, c, b_:b_ + 1], op0=ALU.mult, op1=ALU.add),
                             reads=raall + iuall + ['h0T'], writes=[XCS])
                    S.op('pool', lambda e: e.tensor_copy(out=fins[:, 96:128].rearrange("p (b c) -> p b c", b=4)[:, :, c],
                                                         in_=xc_[:, T:T + TS].rearrange("p (b t) -> p b t", b=4)[:, :, ST - 1]),
                         reads=[XCS], writes=['fins'])
                for si, (c0, w_) in enumerate(segs):
                    b = nb()
                    S.op('pe', mm(bank(b)[:, 0:w_], [(wg3[:, k, :], xnT[:, k, c0:c0 + w_]) for k in range(8)]), reads=xn_all + [rg], writes=[('ps', b)])
                    S.op('act', lambda e, b=b, c0=c0, w_=w_: e.activation(out=r_gg[:, c0:c0 + w_], in_=bank(b)[:, 0:w_], func=AF.Gelu_apprx_tanh),
                         reads=[('ps', b)], writes=[('gg', si)])
                S.op('pool', lambda e: e.tensor_tensor(out=ghsT[:, c, 0:NT], in0=r_gg[:, 0:NT], in1=xc_[:, 0:NT], op=ALU.mult),
                     reads=ggall + [XC, XCS], writes=[('ghsT', c)])

            npump = (p3c.nsteps + 22) // 23 if p3c is not None else 0

            def pump():
                if p3c is not None:
                    p3c.step(npump)
            A1_(0)
            A2_(0)
            for c in range(1, 8):
                A1_(c)
                B_(c - 1)
                if c + 2 < 8:
                    W_(c + 2)
                A2_(c)
            B_(7)
            b = nb()
            S.op('pe', lambda e, b=b: e.transpose(bank(b)[0:32, 0:128], fin[:, 0:32], identf[:, :]), reads=['fin', 'identf'], writes=[('ps', b)])
            S.op('act', lambda e, b=b: e.copy(out=fint[0:32, 0:128], in_=bank(b)[0:32, 0:128]), reads=[('ps', b)], writes=['fint'])
            dma(p_conv[ps_, :].rearrange("(r p) -> r p", p=128), fint[0:24, 0:128], reads=['fint'], sem='fin')
            dma(p_h[ps_, :].rearrange("(r p) -> r p", p=128), fint[24:32, 0:128], reads=['fint'], sem='fin')
            if has_s:
                b = nb()
                S.op('pe', lambda e, b=b: e.transpose(bank(b)[:, 0:128], fins[:, :], identf[:, :]), reads=['fins', 'identf'], writes=[('ps', b)])
                S.op('act', lambda e, b=b: e.copy(out=fin[:, :], in_=bank(b)[:, 0:128]), reads=[('ps', b)], writes=['fin'])
                dma(s_conv.rearrange("(r p) -> r p", p=128), fin[0:96, :], reads=['fin'], sem='fin')
                dma(s_h.rearrange("(r p) -> r p", p=128), fin[96:128, :], reads=['fin'], sem='fin')

            GA0 = 2048 + 3 * 768
            def w4a(o):
                return (wload([(0, wga3_src(o))], 1024), wload([(0, wgb3_src(o))], 1024),
                        wload([(0, wa3[:, :, o * 128:(o + 1) * 128])], 1024),
                        wload([(0, wb3[:, :, o * 128:(o + 1) * 128])], 512, parts=64))
            wga3_src = lambda o: win3[:, :, GA0 + o * 128:GA0 + (o + 1) * 128]
            wgb3_src = lambda o: win3[:, :, GA0 + D + o * 128:GA0 + D + (o + 1) * 128]
            w4 = {0: w4a(0)}
            S.barrier()
            for o in range(8):
                if o + 1 < 8:
                    w4[o + 1] = w4a(o + 1)
                (wga_, rga_), (wgb_, rgb_), (wa_, ra_), (wb_, rb_) = w4[o]
                wa_3 = wa_.rearrange("p (k n) -> p k n", k=8)
                wga3 = wga_.rearrange("p (k n) -> p k n", k=8)
                wgb3 = wgb_.rearrange("p (k n) -> p k n", k=8)
                wb_3 = wb_.rearrange("p (k n) -> p k n", k=4)
                ghall = [('ghsT', i) for i in range(8)]
                otall = [('oT', i) for i in range(5)]
                for si, (c0, w_) in enumerate(segs):
                    bA_, bB_, bGA, bGB = nb(), nb(), nb(), nb()
                    S.op('pe', mm(bank(bGA)[:, 0:w_], [(wga3[:, k, :], xnT[:, k, c0:c0 + w_]) for k in range(8)]), reads=xn_all + [rga_], writes=[('ps', bGA)])
                    S.op('pe', mm(bank(bGB)[:, 0:w_], [(wgb3[:, k, :], xnT[:, k, c0:c0 + w_]) for k in range(8)]), reads=xn_all + [rgb_], writes=[('ps', bGB)])
                    S.op('pe', mm(bank(bA_)[:, 0:w_], [(wa_3[:, k, :], ghsT[:, k, c0:c0 + w_]) for k in range(8)]), reads=ghall + [ra_], writes=[('ps', bA_)])
                    S.op('pe', mm(bank(bB_)[:, 0:w_], [(wb_3[:, k, :], oT[0:64, k, c0:c0 + w_]) for k in range(4)]), reads=otall + [rb_], writes=[('ps', bB_)])
                    s0, s1 = rr('sgm', 2) * 2, None
                    s1 = s0 + 1
                    S.op('act', lambda e, bGA=bGA, s0=s0, o=o, w_=w_: e.activation(out=sg[s0][:, 0:w_], in_=bank(bGA)[:, 0:w_], func=AF.Sigmoid, bias=vec[:, BM + o:BM + o + 1]),
                         reads=[('ps', bGA), 'vec'], writes=[('sg', s0)])
                    S.op('act', lambda e, bGB=bGB, s1=s1, o=o, w_=w_: e.activation(out=sg[s1][:, 0:w_], in_=bank(bGB)[:, 0:w_], func=AF.Sigmoid, bias=vec[:, BM + 8 + o:BM + 9 + o]),
                         reads=[('ps', bGB), 'vec'], writes=[('sg', s1)])
                    S.op('dve', lambda e, bA_=bA_, s0=s0, w_=w_: e.tensor_tensor(out=sg[s0][:, 0:w_], in0=sg[s0][:, 0:w_], in1=bank(bA_)[:, 0:w_], op=ALU.mult),
                         reads=[('ps', bA_), ('sg', s0)], writes=[('sg', s0)])
                    S.op('dve', lambda e, bB_=bB_, s1=s1, w_=w_: e.tensor_tensor(out=sg[s1][:, 0:w_], in0=sg[s1][:, 0:w_], in1=bank(bB_)[:, 0:w_], op=ALU.mult),
                         reads=[('ps', bB_), ('sg', s1)], writes=[('sg', s1)])
                    S.op('pool', lambda e, s0=s0, s1=s1, o=o, c0=c0, w_=w_: e.tensor_tensor(out=mT[:, o, c0:c0 + w_], in0=sg[s0][:, 0:w_], in1=sg[s1][:, 0:w_], op=ALU.add),
                         reads=[('sg', s0), ('sg', s1)], writes=[('mT', o)])

            wo_ = [wload([(0, wo3[:, k2, :])], 1024) for k2 in range(8)]
            wor_ = [r_ for _, r_ in wo_]
            S.barrier()
            mall = [('mT', i) for i in range(8)]
            its = []
            for ti, (c0, np_) in enumerate(tiles):
                stt = {}
                sl = ti % 2
                src = xp[ps_, c0:c0 + 128, :] if ti < 16 else xs[:, :]
                hb = hbuf(ti)[0:np_, :]

                def A_(stt=stt, ti=ti, c0=c0, np_=np_, sl=sl, src=src):
                    bX = rr('pb', 3) * 2
                    stt['bX'] = bX
                    S.op('pe', mm(bank(bX)[0:np_, :], [(mT[:, k, c0:c0 + np_], wo_[k][0][:, 0:512]) for k in range(8)]), reads=mall + wor_, writes=[('ps', bX)])
                    S.op('pe', mm(bank(bX + 1)[0:np_, :], [(mT[:, k, c0:c0 + np_], wo_[k][0][:, 512:1024]) for k in range(8)]), reads=mall + wor_, writes=[('ps', bX + 1)])
                    dma(xt[sl][0:np_, :], src, writes=[('xt', sl)], sem='xt%d' % sl)
                Bn, Cn, Dn = rms_stages(hb, np_, mT, c0, G2, sl, ('hb', ti), ('hnT', ti))

                def B_(stt=stt, ti=ti, np_=np_, sl=sl, hb=hb, Bn=Bn):
                    bX = stt['bX']
                    xy = psum[0:np_, bX * 512:(bX + 2) * 512]
                    S.op('dve', lambda e: e.tensor_tensor(out=hb, in0=xy, in1=xt[sl][0:np_, :], op=ALU.add),
                         reads=[('ps', bX), ('ps', bX + 1), ('xt', sl)], writes=[('hb', ti)])
                    Bn()
                its.append((A_, B_, Cn, Dn))
            pipeline(its, [0, 1, 2, 2])
            hnT = mT
            hn_all = [('hnT', i) for i in range(len(tiles))]
            hn_seg = [[('hnT', 4 * si_ + i_) for i_ in range(4)] if si_ < 4 else [('hnT', 16)] for si_ in range(len(segs))]

            dma(gfb, gfb_d[:, :], writes=['gfb'], sem='c_gfb')

            def final_tile(ti, c0, np_):
                sl = rr('x', 2)
                hb = hbuf(ti)[0:np_, :]
                S.op('act', lambda e: e.activation(out=xsb[sl][0:np_, :], in_=hb, func=AF.Square, accum_out=ss[sl][0:np_, :]),
                     reads=[('hb', ti)], writes=[('xsb', sl), ('ss', sl)])
                S.op('act', lambda e: e.activation(out=rs[sl][0:np_, :], in_=ss[sl][0:np_, :], func=AF.Sqrt, scale=1.0 / D, bias=EPS),
                     reads=[('ss', sl)], writes=[('rs', sl)])
                S.op('dve', lambda e: e.reciprocal(out=rs[sl][0:np_, :], in_=rs[sl][0:np_, :]), reads=[('rs', sl)], writes=[('rs', sl)])
                S.op('dve', lambda e: e.scalar_tensor_tensor(out=xt[sl][0:np_, :], in0=hb, scalar=rs[sl][0:np_, 0:1], in1=gfb[0:np_, :], op0=ALU.mult, op1=ALU.mult),
                     reads=[('hb', ti), ('rs', sl), 'gfb'], writes=[('xt', sl)])
                dst = y_p[ps_, c0:c0 + 128, :] if ti < 16 else y_s[:, :]
                dma(dst, xt[sl][0:np_, :], reads=[('xt', sl)], sem='xt%d' % sl)

            jgroups = [list(range(j0, min(j0 + 3, NJ))) for j0 in range(0, NJ, 3)]
            w5q, w5r, w5p = [], {}, [0]
            for gi, jg in enumerate(jgroups):
                for j in jg:
                    w5q.append(('in', j))
                w5q.append(('out', gi))

            def w5need(upto):
                while w5p[0] <= min(upto, len(w5q) - 1):
                    kind, v = w5q[w5p[0]]
                    if kind == 'in':
                        w5r[(kind, v)] = (wload([(0, wf13[:, :, v * 128:(v + 1) * 128])], 1024),
                                          wload([(0, wf13[:, :, DFF + v * 128:DFF + (v + 1) * 128])], 1024))
                    else:
                        w5r[(kind, v)] = [wload([(0, w_f2[j_ * 128:(j_ + 1) * 128, :])], 1024) for j_ in jgroups[v]]
                    w5p[0] += 1
            for gi, jg in enumerate(jgroups):
                a_ = gi % 2
                for jj, j in enumerate(jg):
                    qi = w5q.index(('in', j))
                    w5need(qi + 1)
                    (wgt, rgt), (wup, rup) = w5r[('in', j)]
                    wgt3 = wgt.rearrange("p (k n) -> p k n", k=8)
                    wup3 = wup.rearrange("p (k n) -> p k n", k=8)
                    for si, (c0, w_) in enumerate(segs):
                        bG, bU = nb(), nb()
                        S.op('pe', mm(bank(bG)[:, 0:w_], [(wgt3[:, k, :], hnT[:, k, c0:c0 + w_]) for k in range(8)]), reads=hn_seg[si] + [rgt], writes=[('ps', bG)])
                        S.op('pe', mm(bank(bU)[:, 0:w_], [(wup3[:, k, :], hnT[:, k, c0:c0 + w_]) for k in range(8)]), reads=hn_seg[si] + [rup], writes=[('ps', bU)])
                        s0 = rr('sgf', 4)
                        S.op('act', lambda e, bG=bG, s0=s0, w_=w_: e.activation(out=sg[s0][:, 0:w_], in_=bank(bG)[:, 0:w_], func=AF.Silu), reads=[('ps', bG)], writes=[('sg', s0)])
                        S.op('dve', lambda e, bU=bU, s0=s0, a_=a_, jj=jj, c0=c0, w_=w_: e.tensor_tensor(out=actT[a_][:, jj, c0:c0 + w_], in0=sg[s0][:, 0:w_], in1=bank(bU)[:, 0:w_], op=ALU.mult),
                             reads=[('ps', bU), ('sg', s0)], writes=[('actT', a_)])
                w5need(w5q.index(('out', gi)) + 1)
                w2 = w5r[('out', gi)]
                w2r = [r_ for _, r_ in w2]
                for ti, (c0, np_) in enumerate(tiles):
                    bX = nb2()
                    xy = psum[0:np_, bX * 512:(bX + 2) * 512]
                    S.op('pe', mm(bank(bX)[0:np_, :], [(actT[a_][:, jj, c0:c0 + np_], w2[jj][0][:, 0:512]) for jj in range(len(jg))]), reads=[('actT', a_)] + w2r, writes=[('ps', bX)])
                    S.op('pe', mm(bank(bX + 1)[0:np_, :], [(actT[a_][:, jj, c0:c0 + np_], w2[jj][0][:, 512:1024]) for jj in range(len(jg))]), reads=[('actT', a_)] + w2r, writes=[('ps', bX + 1)])
                    hb = hbuf(ti)[0:np_, :]
                    S.op('dve', lambda e, hb=hb, xy=xy: e.tensor_tensor(out=hb, in0=hb, in1=xy, op=ALU.add),
                         reads=[('ps', bX), ('ps', bX + 1), ('hb', ti)], writes=[('hb', ti)])
                    if gi == len(jgroups) - 1 and ti >= 1:
                        final_tile(ti - 1, *tiles[ti - 1])
                if gi == len(jgroups) - 1:
                    final_tile(len(tiles) - 1, *tiles[-1])
            if ps_ + 1 < NB:
                for which in range(3):
                    for hf in range(2):
                        c_ = 2048 + which * 768
                        p3next[0][(0, which, hf)] = wload([(0, win3[:, hf * 4:(hf + 1) * 4, c_:c_ + 256])], 1024)

        block = es.enter_context(nc.Block())
        S.emit(block)
    return nc


_NC = None


def _consts():
    half = 32
    inv = np.exp(-math.log(10000.0) * np.arange(half, dtype=np.float32) * np.float32(2.0 / 64)).astype(np.float32)

    def tab(pos):
        ang = pos.astype(np.float32)[:, None] * inv[None, :]
        return np.cos(ang).astype(np.float32), np.sin(ang).astype(np.float32)
    rope = np.zeros((3, 128, 2, 16, 32), np.float32)
    j = np.arange(128)
    for g, d in enumerate(DILS):
        for tl in range(16):
            if g == 0:
                pos = tl * 128 + j
            elif g == 1:
                r_, n_ = tl // 4, tl % 4
                pos = 4 * (128 * n_ + j) + r_
            else:
                pos = 16 * j + tl
            c, s = tab(pos)
            rope[g, :, 0, tl, :] = c
            rope[g, :, 1, tl, :] = s
    pos_s = PAST + (np.arange(TS) % ST)
    c, s = tab(pos_s)
    ropes = np.concatenate([c, s], axis=1).astype(np.float32)
    cm = np.zeros((128, 872), np.float32)
    jj = np.arange(128)[:, None]
    ii = np.arange(128)[None, :]
    cm[:, 0:128] = (jj <= ii)
    cm[:, 128:256] = (jj >= ii)
    for q in range(4):
        cm[:, 256 + q * 32:256 + (q + 1) * 32] = (jj <= (32 * q + np.arange(32)[None, :]))
    cm[:, 384:392] = (jj >= np.arange(8)[None, :])
    kb, kt = np.arange(TS)[:, None] // ST, np.arange(TS)[:, None] % ST
    qb, qt = np.arange(TS)[None, :] // ST, np.arange(TS)[None, :] % ST
    for g, d in enumerate(DILS):
        cm[0:TS, 392 + g * 32:392 + (g + 1) * 32] = (kb == qb) & (qt >= kt) & ((qt - kt) % d == 0)
    cm[:, 488:872] = np.where(cm[:, 0:384] > 0, 0.0, -30000.0)
    ident = np.eye(128, dtype=np.float32)
    return rope.reshape(3, 128, 1024), ropes, cm, ident


def kernel(x_prompt, x_sample, state_conv, state_h, cache_k_g0, cache_v_g0, cache_k_g1, cache_v_g1,
           cache_k_g2, cache_v_g2, norm1_g, w_in, b_merge, conv_w, conv_b, gate_a_w, gate_a_b,
           gate_x_w, gate_x_b, rg_lambda, w_branch_a, w_branch_b, w_out, norm2_g, w_ffn_in, w_ffn_out,
           norm_f_g):
    global _NC
    if _NC is None:
        _NC = build()
    nc = _NC
    f = lambda a: np.ascontiguousarray(np.asarray(a, dtype=np.float32))
    fm = lambda v: f(v).reshape(8, 128).T
    vecs = np.zeros((128, 96), np.float32)
    vecs[:, 0:8] = fm(norm1_g[0]); vecs[:, 8:16] = fm(norm2_g[0])
    vecs[:, 16:32] = f(b_merge[0]).reshape(16, 128).T
    for j in range(4):
        vecs[:, 32 + j * 8:40 + j * 8] = fm(conv_w[0, j])
    vecs[:, 64:72] = fm(conv_b[0]); vecs[:, 72:80] = fm(gate_a_b[0]); vecs[:, 80:88] = fm(gate_x_b[0])
    vecs[:, 88:96] = fm(rg_lambda[0])
    gfb = np.ascontiguousarray(np.broadcast_to(f(norm_f_g)[None, :], (128, D)))
    gw = np.zeros((128, 2, 8, 128), np.float32)
    for wi, gwm in enumerate((gate_a_w, gate_x_w)):
        gwm = f(gwm[0])
        for c in range(8):
            gw[0:64, wi, c, 0:64] = gwm[2 * c]
            gw[64:128, wi, c, 64:128] = gwm[2 * c + 1]
    rope, ropes, cm, ident = _consts()
    shared = dict(w_in=f(w_in[0]), w_a=f(w_branch_a[0]), w_b=f(w_branch_b[0]), w_out=f(w_out[0]),
                  w_f1=f(w_ffn_in[0]), w_f2=f(w_ffn_out[0]), vecs=vecs, gfb=gfb, gwbd=gw.reshape(128, 2048),
                  rope=rope, ropes=ropes, cmask=cm, ident=ident)
    caches_k = [f(cache_k_g0[0]), f(cache_k_g1[0]), f(cache_k_g2[0])]
    caches_v = [f(cache_v_g0[0]), f(cache_v_g1[0]), f(cache_v_g2[0])]
    xp_, xs_ = f(x_prompt), f(x_sample)
    sc_, sh_ = f(state_conv[0]), f(state_h[0])
    in_maps = []
    for c in range(NCORES):
        m = dict(shared)
        m["xp"] = xp_[c * NB:(c + 1) * NB]
        m["xs"] = xs_[c * SB:(c + 1) * SB].reshape(TS, D)
        m["sconv"] = sc_[c * SB:(c + 1) * SB].reshape(SB * 3, D)
        m["sh"] = sh_[c * SB:(c + 1) * SB]
        for g in range(3):
            m["ck%d" % g] = caches_k[g][c * SB:(c + 1) * SB].reshape(SB, WINS[g], 256)
            m["cv%d" % g] = caches_v[g][c * SB:(c + 1) * SB].reshape(SB, WINS[g], 256)
        in_maps.append(m)
    res = run_bass_kernel_spmd(nc, in_maps, core_ids=list(range(NCORES)))
    R = res.results
    cat = lambda name: np.concatenate([np.asarray(r[name]) for r in R], axis=0)
    y_prompt = cat("y_p").reshape(16, T, D)
    y_sample = cat("y_s").reshape(32, ST, D)
    pconv = cat("p_conv").reshape(1, 16, 3, D)
    ph = cat("p_h").reshape(1, 16, D)
    outs = [y_prompt, y_sample, pconv, ph]
    for g in range(3):
        outs.append(cat("p_k%d" % g).reshape(1, 16, WINS[g], 4, 64))
        outs.append(cat("p_v%d" % g).reshape(1, 16, WINS[g], 4, 64))
    outs.append(np.concatenate([np.asarray(r["s_conv"]).reshape(SB, 3, D) for r in R], axis=0).reshape(1, 32, 3, D))
    outs.append(np.concatenate([np.asarray(r["s_h"]).reshape(SB, D) for r in R], axis=0).reshape(1, 32, D))
    for g in range(3):
        outs.append(cat("s_k%d" % g).reshape(1, 32, WINS[g], 4, 64))
        outs.append(cat("s_v%d" % g).reshape(1, 32, WINS[g], 4, 64))
    return tuple(np.ascontiguousarray(o, dtype=np.float32) for o in outs)
```

```python
import math
import numpy as np
from contextlib import ExitStack
import concourse.bass as bass
import concourse.mybir as mybir
from concourse.bass_utils import run_bass_kernel_spmd

F32 = mybir.dt.float32
BF16 = mybir.dt.bfloat16
ALU = mybir.AluOpType
AF = mybir.ActivationFunctionType

D = 1024
T = 2048
NB = 2
SB = 4
ST = 8
TS = SB * ST
PAST = 16384
DFF = 2816
NJ = DFF // 128
INW = 6400
WINS = (128, 512, 2048)
DILS = (1, 4, 16)
EPS = 1e-6
NCORES = 8


class Sched:
    def __init__(self, nc, es):
        self.nc = nc
        self.es = es
        self.engs = {'pe': None, 'act': None, 'dve': None, 'pool': None, 'sp': None}
        self.ops = {e: [] for e in self.engs}
        self.cnt = {e: 0 for e in self.engs}
        self.sems = {e: es.enter_context(nc.semaphore('s_' + e)) for e in self.engs}
        self.waited = {e: {} for e in self.engs}
        self.res = {}
        self.dcnt = {}
        self.pending = {e: [] for e in self.engs}

    def _dsem(self, name):
        if name not in self.sems:
            self.sems[name] = self.es.enter_context(self.nc.semaphore('d_' + name))
            self.dcnt[name] = 0
        return self.sems[name]

    def _need(self, eng, tok, waits):
        key, val = tok
        if key == 'pe' and eng == 'pe':
            return
        if self.waited[eng].get(key, 0) >= val:
            return
        self.waited[eng][key] = val
        waits.append(tok)

    def op(self, eng, fn, reads=(), writes=(), dma=None):
        waits = []
        for tok in self.pending[eng]:
            self._need(eng, tok, waits)
        self.pending[eng] = []
        for r in reads:
            st = self.res.setdefault(r, {'w': None, 'r': []})
            if st['w'] is not None:
                self._need(eng, st['w'], waits)
        for w in writes:
            st = self.res.setdefault(w, {'w': None, 'r': []})
            if st['w'] is not None:
                self._need(eng, st['w'], waits)
            for tok in st['r']:
                self._need(eng, tok, waits)
        if dma is not None:
            self._dsem(dma)
            self.dcnt[dma] += 16
            tok = (dma, self.dcnt[dma])
            self.ops[eng].append((waits, fn, (dma, 16)))
        else:
            self.cnt[eng] += 1
            tok = (eng, self.cnt[eng])
            self.ops[eng].append((waits, fn, (eng, 1)))
        for r in reads:
            self.res[r]['r'].append(tok)
        for w in writes:
            self.res[w]['w'] = tok
            self.res[w]['r'] = []
        return tok

    def barrier(self):
        toks = [(e, c) for e, c in self.cnt.items() if c > 0 and e != 'sp']
        toks += [(d, c) for d, c in self.dcnt.items() if c > 0]
        for e in self.engs:
            self.pending[e] = list(toks)

    def emit(self, block):
        nc = self.nc
        final = [(d, c) for d, c in self.dcnt.items() if c > 0]
        final += [(e, c) for e, c in self.cnt.items() if c > 0 and e != 'sp']

        def run(name, handle, tail=False):
            for waits, fn, inc in self.ops[name]:
                for key, val in waits:
                    handle.wait_ge(self.sems[key], val)
                ins = fn(handle)
                ins.then_inc(self.sems[inc[0]], inc[1])
            if tail:
                for key, val in final:
                    handle.wait_ge(self.sems[key], val)

        block.sync(lambda e: run('sp', e, True))
        block.scalar(lambda e: run('act', e))
        block.vector(lambda e: run('dve', e))
        block.gpsimd(lambda e: run('pool', e))
        block.tensor(lambda e: run('pe', e))


def build():
    nc = bass.Bass("TRN2", target_bir_lowering=False)

    def din(name, shape):
        return nc.dram_tensor(name, list(shape), F32, kind="ExternalInput").ap()

    def dout(name, shape):
        return nc.dram_tensor(name, list(shape), F32, kind="ExternalOutput").ap()

    xp = din("xp", [NB, T, D]); xs = din("xs", [TS, D])
    sconv = din("sconv", [SB * 3, D]); sh = din("sh", [SB, D])
    ck = [din("ck%d" % g, [SB, WINS[g], 256]) for g in range(3)]
    cv = [din("cv%d" % g, [SB, WINS[g], 256]) for g in range(3)]
    w_in = din("w_in", [D, INW]); w_a = din("w_a", [D, D]); w_b = din("w_b", [256, D])
    w_out = din("w_out", [D, D]); w_f1 = din("w_f1", [D, 2 * DFF]); w_f2 = din("w_f2", [DFF, D])
    vecs = din("vecs", [128, 96]); gfb_d = din("gfb", [128, D]); gwbd_d = din("gwbd", [128, 2 * 8 * 128])
    rope_d = din("rope", [3, 128, 2 * 16 * 32]); ropes_d = din("ropes", [TS, 64])
    cmask_d = din("cmask", [128, 872]); ident_d = din("ident", [128, 128])

    y_p = dout("y_p", [NB, T, D]); y_s = dout("y_s", [TS, D])
    p_conv = dout("p_conv", [NB, 3 * D]); p_h = dout("p_h", [NB, D])
    p_k = [dout("p_k%d" % g, [NB, WINS[g], 256]) for g in range(3)]
    p_v = [dout("p_v%d" % g, [NB, WINS[g], 256]) for g in range(3)]
    s_conv = dout("s_conv", [SB * 3 * D]); s_h = dout("s_h", [SB * D])
    s_k = [dout("s_k%d" % g, [SB, WINS[g], 256]) for g in range(3)]
    s_v = [dout("s_v%d" % g, [SB, WINS[g], 256]) for g in range(3)]

    es = ExitStack()
    with es:
        def sb(name, shape, dt=F32):
            return es.enter_context(nc.sbuf_tensor(name, list(shape), dt))

        S = Sched(nc, es)
        NTMAX = T + TS
        A_XNT = 0
        A_OT = 33280
        A_X = A_OT + 16640
        ARENA = A_X + 83520
        arena = sb("arena", [128, ARENA // 4], F32)

        def aview(off, nbytes, dt, pat=None, parts=128, **kw):
            v = arena[0:parts, off // 4:(off + nbytes) // 4]
            if dt is BF16:
                v = v.bitcast(BF16)
            if pat:
                v = v.rearrange(pat, **kw)
            return v

        xnT = aview(A_XNT, 33280, BF16, "p (k n) -> p k n", k=8)
        oT = aview(A_OT, 16640, BF16, "p (h n) -> p h n", h=4)
        QKT = aview(A_X, 49152, BF16, "p (g i n) -> p g i n", g=3, i=4)
        Vt = aview(A_X + 49152, 24960, BF16, "p (t g c) -> p t g c", t=16, g=3)
        RW = 2096
        r_xp = aview(A_X, RW * 4, F32)
        r_xc = aview(A_X + RW * 4, RW * 4, F32)
        r_ra = aview(A_X + 2 * RW * 4, RW * 4, F32)
        r_iu = aview(A_X + 3 * RW * 4, RW * 4, F32)
        r_e2 = aview(A_X + 4 * RW * 4, RW * 4, F32)
        r_xcb = aview(A_X + 5 * RW * 4, 4160, BF16)
        r_gg = aview(A_X + 5 * RW * 4 + 4160, 4160, BF16)
        assert 5 * RW * 4 + 8320 <= 50240
        ghsT = aview(A_X + 50240, 33280, BF16, "p (k n) -> p k n", k=8)
        mT = aview(A_X, 33280, BF16, "p (k n) -> p k n", k=8)
        hb1 = aview(A_XNT, 32768, F32, "p (t n) -> p t n", t=8)
        hb2 = aview(A_X + 33280, 36864, F32, "p (t n) -> p t n", t=9)
        woutT = aview(A_OT, 16384, BF16, "p (k n) -> p k n", k=8)
        actT = [aview(A_OT, 12480, BF16, "p (j n) -> p j n", j=3),
                aview(A_X + 70144, 12480, BF16, "p (j n) -> p j n", j=3)]

        def hbuf(ti):
            return hb1[:, ti, :] if ti < 8 else hb2[:, ti - 8, :]

        vec = sb("vec", [128, 96])
        gfb = aview(A_OT + 12480, 4096, F32)
        gwbd = sb("gwbds", [128, 2, 8, 128], BF16)
        identf = sb("identf", [128, 128]); identb = sb("identb", [128, 128], BF16)
        ones = sb("ones", [128, 64], BF16)
        onesf = sb("onesf", [128, 64])
        cmb = sb("cmb", [128, 872], BF16)
        misc = sb("misc", [128, 6656])
        ropeT = misc[:, 4608:5632].rearrange("p (a t c) -> p a t c", a=2, t=16)
        ropeS = sb("ropeS", [TS, 2, 32])
        cA = sb("cA", [128, 8]); cA2 = sb("cA2", [128, 8])
        stg = [sb("stg%d" % i, [128, 1024]) for i in range(2)]
        NWB = 8
        wbf = [sb("wbf%d" % i, [128, 1024], BF16) for i in range(NWB)]
        xtbig = sb("xtbig", [128, 2100])
        xt = [xtbig[:, i * D:(i + 1) * D] for i in range(2)]
        xsb = [sb("xsb%d" % i, [128, D], BF16) for i in range(2)]
        ss = [sb("ss%d" % i, [128, 1]) for i in range(2)]
        rs = [sb("rs%d" % i, [128, 1]) for i in range(2)]
        qkr = [misc[:, 2048 + i * 512:2048 + (i + 1) * 512] for i in range(2)]
        tm1 = misc[:, 3072:3328].rearrange("p (h c) -> p h c", h=8)
        tm2 = misc[:, 3328:3584].rearrange("p (h c) -> p h c", h=8)
        qkb = [misc[:, 3584 + i * 256:3584 + (i + 1) * 256].bitcast(BF16) for i in range(2)]
        vf = [misc[:, 4096 + i * 256:4096 + (i + 1) * 256] for i in range(2)]
        PT = [misc[:, 6144 + i * 128:6144 + (i + 1) * 128].bitcast(BF16) for i in range(4)]
        rec = misc[:, 5632:6144]
        sg = [misc[:, i * 512:(i + 1) * 512] for i in range(4)]
        rabuf_alt = misc[:, 0:RW]
        iubuf_alt = misc[:, RW:2 * RW]
        PTx = PT + [sg[i][:, j * 128:(j + 1) * 128].bitcast(BF16) for i in range(2) for j in range(4)]
        fin = sb("fin", [128, 128])
        fins = sb("fins", [128, 128])
        scT = sb("scT", [128, 8, 12]); h0T = sb("h0T", [128, 8, 4])
        xps = sb("xps", [128, 4, 11])
        sctm = aview(A_X + 74240, 4096, F32)
        fint = sb("fint", [32, 128])
        QKTs = sb("QKTs", [128, 3, 4, TS], BF16)
        Vs = sb("Vs", [TS, 3, 256], BF16)
        kc = [sg[2][:, i * 256:(i + 1) * 256] for i in range(2)]
        vc = [sg[3][:, i * 256:(i + 1) * 256] for i in range(2)]
        vcb = [sb("vcb%d" % i, [128, 256], BF16) for i in range(3)]
        kcT = [sb("kcT%d" % i, [128, 2, 128], BF16) for i in range(2)]
        PTs = [sb("PTs%d" % i, [128, 4, 8], BF16) for i in range(2)]
        PTn = [sb("PTn%d" % i, [TS, TS], BF16) for i in range(2)]
        oTs = sb("oTs", [64, 4, TS], BF16)

        psum = es.enter_context(nc.psum_tensor("psum", [128, 8 * 512], F32))

        def bank(i):
            return psum[:, i * 512:(i + 1) * 512]

        def bankb(i):
            return psum[:, i * 512:(i + 1) * 512].bitcast(BF16)

        bctr = [0]

        nbanks = [8]

        def nb():
            bctr[0] = (bctr[0] + 1) % nbanks[0]
            return bctr[0]

        def nb2():
            b = ((bctr[0] // 2 + 1) % 4) * 2
            bctr[0] = b + 1
            return b

        ctr = {'stg': 0, 'wbf': 0, 'x': 0, 'q': 0, 'pt': 0, 'sgm': 0, 'sgf': 0, 'k': 0, 'kn': 0, 'cast': 0, 'sb': 0, 'ob': 0, 'tb': 0, 'pb': 0, 'sbs': 0, 'k2': 0, 'v3': 0, 'p2': 0, 'sb3': 0, 'ptx': 0}

        def rr(name, n):
            v = ctr[name]
            ctr[name] = (v + 1) % n
            return v

        def dma(out, in_, reads=(), writes=(), sem=None):
            S.op('sp', lambda e, o=out, i=in_: e.dma_start(out=o, in_=i), reads=reads, writes=writes, dma=sem)

        def wload(srcs, n, parts=128, dest=None, dres=None):
            s = rr('stg', 2)
            for off, ap in srcs:
                sz = 1
                for d_ in ap.shape[1:]:
                    sz *= d_
                o = stg[s][0:ap.shape[0], off:off + sz]
                if len(ap.shape) == 3:
                    o = o.rearrange("p (a b) -> p a b", a=ap.shape[1])
                dma(o, ap, writes=[('stg', s)], sem='stg%d' % s)
            if dest is None:
                w = rr('wbf', NWB)
                dest = wbf[w][0:parts, 0:n]
                dres = ('wbf', w)
            ce = rr('cast', 2)
            if ce == 0:
                S.op('dve', lambda e, o=dest, i=stg[s][0:parts, 0:n]: e.tensor_copy(out=o, in_=i), reads=[('stg', s)], writes=[dres])
            else:
                S.op('act', lambda e, o=dest, i=stg[s][0:parts, 0:n]: e.copy(out=o, in_=i), reads=[('stg', s)], writes=[dres])
            return dest, dres

        def mm(out, pairs, first=True, last=True, skip=False):
            def fn(e):
                ins = None
                n = len(pairs)
                for i, (l, r) in enumerate(pairs):
                    ins = e.matmul(out, lhsT=l, rhs=r, start=(first and i == 0), stop=(last and i == n - 1),
                                   skip_group_check=skip)
                return ins
            return fn

        win3 = w_in.rearrange("(k p) n -> p k n", p=128)
        wa3 = w_a.rearrange("(k p) n -> p k n", p=128)
        wo3 = w_out.rearrange("(k p) n -> p k n", p=128)
        wf13 = w_f1.rearrange("(k p) n -> p k n", p=128)
        wb3 = w_b.rearrange("(h p) n -> p h n", p=64)

        dma(vec[:], vecs[:, :], writes=['vec'], sem='c_vec')
        dma(identf[:], ident_d[:, :], writes=['identf'], sem='c_id')
        cmf = xt[0][:, 0:872]
        dma(cmf, cmask_d[:, :], writes=[('xt', 0)], sem='c_cm')
        dma(ropeS[:].rearrange("p a b -> p (a b)"), ropes_d[:, :], writes=['ropeS'], sem='c_rs')
        S.op('pool', lambda e: e.tensor_copy(out=identb[:], in_=identf[:]), reads=['identf'], writes=['identb'])
        S.op('pool', lambda e: e.tensor_copy(out=cmb[:], in_=cmf), reads=[('xt', 0)], writes=['cmb'])
        S.op('pool', lambda e: e.memset(ones[:], 1.0), writes=['ones'])
        S.op('pool', lambda e: e.memset(onesf[:], 1.0), writes=['onesf'])
        for h_ in range(2):
            for q_ in range(2):
                wload([(0, gwbd_d[:, (h_ * 2 + q_) * 512:(h_ * 2 + q_ + 1) * 512])], 512,
                      dest=gwbd[:, h_, q_ * 4:(q_ + 1) * 4, :].rearrange("p a b -> p (a b)"), dres='gwbd')
        S.op('act', lambda e: e.activation(out=cA[:], in_=vec[:, 88:96], func=AF.Sigmoid), reads=['vec'], writes=['cA'])
        S.op('act', lambda e: e.activation(out=cA[:], in_=cA[:], func=AF.Ln), reads=['cA'], writes=['cA'])
        S.op('act', lambda e: e.mul(out=cA2[:], in_=cA[:], mul=16.0), reads=['cA'], writes=['cA2'])
        S.op('act', lambda e: e.mul(out=cA[:], in_=cA[:], mul=8.0), reads=['cA', 'cA2'], writes=['cA'])
        G1, G2, BM, CW, CB, GAB, GXB = 0, 8, 16, 32, 64, 72, 80
        mask2 = cmb[:, 0:256]
        maskg2 = cmb[:, 256:384]
        maskC = cmb[:, 384:392]
        maskN = cmb[0:TS, 392:488]
        nm2 = cmb[:, 488:744]
        nmg2 = cmb[:, 744:872]

        def pipeline(iters, skew, group=1):
            n = len(iters)
            nst = n + max(skew)
            for s0_ in range(0, nst, group):
                for k, sk in enumerate(skew):
                    for st_ in range(s0_, min(s0_ + group, nst)):
                        t_ = st_ - sk
                        if 0 <= t_ < n and iters[t_][k] is not None:
                            iters[t_][k]()

        class Pipe:
            def __init__(self, iters, skew, desc=False):
                self.iters, self.skew, self.st = iters, skew, 0
                self.nsteps = len(iters) + max(skew)
                self.order = list(range(len(skew)))
                if desc:
                    self.order.reverse()

            def step(self, n=1):
                for _ in range(n):
                    if self.st >= self.nsteps:
                        return
                    for k in self.order:
                        t_ = self.st - self.skew[k]
                        if 0 <= t_ < len(self.iters) and self.iters[t_][k] is not None:
                            self.iters[t_][k]()
                    self.st += 1

            def finish(self):
                self.step(self.nsteps)

        def rms_stages(src_tile, np_, dstT, col0, gcol, slot, src_res, dst_res):
            stt = {}

            def B():
                S.op('act', lambda e: e.activation(out=xsb[slot][0:np_, :], in_=src_tile, func=AF.Square, accum_out=ss[slot][0:np_, :]),
                     reads=[src_res], writes=[('xsb', slot), ('ss', slot)])
                S.op('act', lambda e: e.activation(out=rs[slot][0:np_, :], in_=ss[slot][0:np_, :], func=AF.Sqrt, scale=1.0 / D, bias=EPS),
                     reads=[('ss', slot)], writes=[('rs', slot)])
                S.op('dve', lambda e: e.reciprocal(out=rs[slot][0:np_, :], in_=rs[slot][0:np_, :]), reads=[('rs', slot)], writes=[('rs', slot)])
                S.op('act', lambda e: e.mul(out=xsb[slot][0:np_, :], in_=src_tile, mul=rs[slot][0:np_, 0:1]),
                     reads=[src_res, ('rs', slot)], writes=[('xsb', slot)])

            def C():
                b = 6 + rr('tb', 2)
                stt['b'] = b

                def tr(e):
                    ins = None
                    for k in range(8):
                        ins = e.transpose(bankb(b)[:, k * 128:k * 128 + np_], xsb[slot][0:np_, k * 128:(k + 1) * 128], identb[0:np_, 0:np_])
                    return ins
                S.op('pe', tr, reads=[('xsb', slot), 'identb'], writes=[('ps', b)])

            def Dd():
                b = stt['b']
                S.op('dve', lambda e: e.tensor_tensor(
                    out=dstT[:, :, col0:col0 + np_],
                    in0=bankb(b).rearrange("p (k n) -> p k n", k=8)[:, :, 0:np_],
                    in1=vec[:, gcol:gcol + 8].unsqueeze(2).to_broadcast([128, 8, np_]), op=ALU.mult),
                    reads=[('ps', b), 'vec'], writes=[dst_res])
            return B, C, Dd

        p3next = [{}]
        for ps_ in range(NB):
            has_s = (ps_ == NB - 1)
            NT = T + (TS if has_s else 0)
            segs = [(i * 512, 512) for i in range(4)] + ([(T, TS)] if has_s else [])
            tiles = [(i * 128, 128) for i in range(16)] + ([(T, TS)] if has_s else [])
            S.barrier()
            its = []
            for ti, (c0, np_) in enumerate(tiles):
                sl = ti % 2
                src = xp[ps_, c0:c0 + 128, :] if ti < 16 else xs[:, :]
                A_ = (lambda sl=sl, np_=np_, src=src: dma(xt[sl][0:np_, :], src, writes=[('xt', sl)], sem='xt%d' % sl))
                B_, C_, D_ = rms_stages(xt[sl][0:np_, :], np_, xnT, c0, G1, sl, ('xt', sl), ('xnT', ti))
                its.append((A_, B_, C_, D_))
            p1its = its

            def sample_states():
                dma(sctm[0:12, :], sconv[:, :], writes=['sctm'], sem='c1')
                dma(sctm[12:16, :], sh[:, :], writes=['sctm'], sem='c1')
                b = nb()

                def trs(e, b=b):
                    ins = None
                    for k in range(8):
                        ins = e.transpose(bank(b)[:, k * 16:(k + 1) * 16], sctm[0:16, k * 128:(k + 1) * 128], identf[0:16, 0:16])
                    return ins
                S.op('pe', trs, reads=['sctm', 'identf'], writes=[('ps', b)])
                S.op('dve', lambda e, b=b: e.tensor_copy(out=scT[:], in_=bank(b)[:, 0:128].rearrange("p (k n) -> p k n", k=8)[:, :, 0:12]),
                     reads=[('ps', b)], writes=['scT'])
                S.op('dve', lambda e, b=b: e.tensor_copy(out=h0T[:], in_=bank(b)[:, 0:128].rearrange("p (k n) -> p k n", k=8)[:, :, 12:16]),
                     reads=[('ps', b)], writes=['h0T'])
            xn_all = [('xnT', i) for i in range(len(tiles))]

            p3pre = p3next[0]
            p3next[0] = {}

            def p3a_w(g, which, hf):
                if (g, which, hf) not in p3pre:
                    c_ = 2048 + which * 768 + g * 256
                    p3pre[(g, which, hf)] = wload([(0, win3[:, hf * 4:(hf + 1) * 4, c_:c_ + 256])], 1024)
                return p3pre[(g, which, hf)]

            def p3a_iters(g):
                grp = {}

                def ensure_w(g=g, grp=grp):
                    if 'rhsl' in grp:
                        return
                    wqs = []
                    for which in range(3):
                        halves = []
                        for hf in range(2):
                            halves.append(p3a_w(g, which, hf))
                        wqs.append(halves)
                    if g + 1 < 3:
                        p3a_w(g + 1, 0, 0)
                        p3a_w(g + 1, 0, 1)

                    def wsl(which, k):
                        t_, _ = wqs[which][k // 4]
                        return t_.rearrange("p (k n) -> p k n", k=4)[:, k % 4, :]
                    grp['wres'] = [r_ for hv in wqs for (_, r_) in hv]
                    grp['rhsl'] = [(wsl(0, k), wsl(1, k), wsl(2, k)) for k in range(8)]
                ntl = 16 + (1 if has_s else 0)
                its = []
                for tl in range(ntl):
                    if tl < 16:
                        np_ = 128
                        if g == 0:
                            colsel = slice(tl * 128, (tl + 1) * 128)
                        elif g == 1:
                            r_, n_ = tl // 4, tl % 4
                            colsel = slice(512 * n_ + r_, 512 * n_ + r_ + 512, 4)
                        else:
                            colsel = slice(tl, T, 16)
                        cosb = ropeT[:, 0, tl, :]
                        sinb = ropeT[:, 1, tl, :]
                    else:
                        np_ = TS
                        colsel = slice(T, T + TS)
                        cosb = ropeS[:, 0, :]
                        sinb = ropeS[:, 1, :]
                    stt = {}

                    def A_(stt=stt, colsel=colsel, np_=np_, tl=tl, g=g, grp=grp, ensure_w=ensure_w):
                        ensure_w()
                        rhsl, wres = grp['rhsl'], grp['wres']
                        bA = rr('pb', 3) * 2
                        bB = bA + 1
                        stt['bA'], stt['bB'] = bA, bB

                        def qkv(e):
                            ins = None
                            for k in range(8):
                                l = xnT[:, k, colsel]
                                e.matmul(bank(bA)[0:np_, 0:256], lhsT=l, rhs=rhsl[k][0], start=(k == 0), stop=(k == 7), skip_group_check=True)
                                e.matmul(bank(bA)[0:np_, 256:512], lhsT=l, rhs=rhsl[k][1], start=False, stop=(k == 7), skip_group_check=True)
                                ins = e.matmul(bank(bB)[0:np_, 0:256], lhsT=l, rhs=rhsl[k][2], start=(k == 0), stop=(k == 7))
                            return ins
                        if tl >= 16:
                            xr_ = [('xnT', 16)]
                        elif g == 0:
                            xr_ = [('xnT', tl)]
                        elif g == 1:
                            xr_ = [('xnT', 4 * (tl % 4) + i_) for i_ in range(4)]
                        else:
                            xr_ = [('xnT', i_) for i_ in range(16)]
                        S.op('pe', qkv, reads=xr_ + wres, writes=[('ps', bA), ('ps', bB)])

                    def B_(stt=stt, np_=np_, cosb=cosb, sinb=sinb, tl=tl, g=g):
                        if tl == 0:
                            dma(ropeT[:].rearrange("p a t c -> p (a t c)"), rope_d[g, :, :], writes=['ropeT'], sem='rope')
                        bA, bB = stt['bA'], stt['bB']
                        qs = rr('q', 2)
                        stt['qs'] = qs
                        x3 = bank(bA)[0:np_, :].rearrange("p (h c) -> p h c", h=8)
                        o3 = qkr[qs][0:np_, :].rearrange("p (h c) -> p h c", h=8)
                        cb_ = cosb[0:np_].unsqueeze(1).to_broadcast([np_, 8, 32])
                        sb_ = sinb[0:np_].unsqueeze(1).to_broadcast([np_, 8, 32])
                        t1 = tm1[0:np_]
                        t2 = tm2[0:np_]
                        rd = [('ps', bA), 'ropeT', 'ropeS']
                        S.op('dve', lambda e: e.tensor_tensor(out=t1, in0=x3[:, :, 0:32], in1=cb_, op=ALU.mult), reads=rd, writes=['tm1'])
                        S.op('dve', lambda e: e.tensor_tensor(out=t2, in0=x3[:, :, 32:64], in1=sb_, op=ALU.mult), reads=rd, writes=['tm2'])
                        S.op('dve', lambda e: e.tensor_tensor(out=o3[:, :, 0:32], in0=t1, in1=t2, op=ALU.subtract),
                             reads=['tm1', 'tm2'], writes=[('qkr', qs)])
                        S.op('dve', lambda e: e.tensor_tensor(out=t1, in0=x3[:, :, 32:64], in1=cb_, op=ALU.mult), reads=rd, writes=['tm1'])
                        S.op('dve', lambda e: e.tensor_tensor(out=t2, in0=x3[:, :, 0:32], in1=sb_, op=ALU.mult), reads=rd, writes=['tm2'])
                        S.op('dve', lambda e: e.tensor_tensor(out=o3[:, :, 32:64], in0=t1, in1=t2, op=ALU.add),
                             reads=['tm1', 'tm2'], writes=[('qkr', qs)])
                        S.op('act', lambda e: e.copy(out=qkb[qs][0:np_, :], in_=qkr[qs][0:np_, :]), reads=[('qkr', qs)], writes=[('qkb', qs)])
                        S.op('act', lambda e: e.copy(out=vf[qs][0:np_, :], in_=bank(bB)[0:np_, 0:256]), reads=[('ps', bB)], writes=[('vf', qs)])

                    def C_(stt=stt, np_=np_):
                        qs = stt['qs']
                        bC = 6 + rr('tb', 2)
                        stt['bC'] = bC

                        def trq(e):
                            ins = None
                            for i in range(4):
                                ins = e.transpose(bankb(bC)[:, i * 128:i * 128 + np_], qkb[qs][0:np_, i * 128:(i + 1) * 128], identb[0:np_, 0:np_])
                            return ins
                        S.op('pe', trq, reads=[('qkb', qs), 'identb'], writes=[('ps', bC)])

                    def D_(stt=stt, np_=np_, tl=tl, g=g):
                        qs, bC = stt['qs'], stt['bC']
                        W = WINS[g]
                        if tl < 16:
                            dst = None
                            if g == 0 and tl == 15:
                                dst = lambda o: o[ps_, 0:128, :]
                            elif g == 1 and tl % 4 == 3:
                                dst = lambda o: o[ps_, (tl // 4):512:4, :]
                            elif g == 2:
                                dst = lambda o: o[ps_, tl:T:16, :]
                            if dst is not None:
                                dma(dst(p_k[g]), qkr[qs][:, 256:512], reads=[('qkr', qs)], sem='qkr%d' % qs)
                                dma(dst(p_v[g]), vf[qs][:, :], reads=[('vf', qs)], sem='vf%d' % qs)
                        else:
                            for b_ in range(SB):
                                dma(s_k[g][b_, W - ST:W, :], qkr[qs][b_ * ST:(b_ + 1) * ST, 256:512], reads=[('qkr', qs)], sem='qkr%d' % qs)
                                dma(s_v[g][b_, W - ST:W, :], vf[qs][b_ * ST:(b_ + 1) * ST, :], reads=[('vf', qs)], sem='vf%d' % qs)
                        if tl < 16:
                            S.op('pool', lambda e: e.tensor_copy(out=Vt[:, tl, g, :].rearrange("p (h c) -> p h c", c=65)[:, :, 0:64],
                                                                 in_=vf[qs][:, :].rearrange("p (h c) -> p h c", c=64)),
                                 reads=[('vf', qs)], writes=[('Vt', g, tl)])
                        else:
                            S.op('pool', lambda e: e.tensor_copy(out=Vs[:, g, :], in_=vf[qs][0:TS, :]), reads=[('vf', qs)], writes=['Vs'])

                    def Da_(stt=stt, np_=np_, tl=tl, g=g):
                        bC = stt['bC']
                        srcT = bankb(bC)[:, 0:512].rearrange("p (i n) -> p i n", i=4)[:, :, 0:np_]
                        if tl < 16:
                            S.op('act', lambda e: e.copy(out=QKT[:, g, :, tl * 128:(tl + 1) * 128], in_=srcT),
                                 reads=[('ps', bC)], writes=[('QKT', g, tl)])
                        else:
                            S.op('act', lambda e: e.copy(out=QKTs[:, g, :, :], in_=srcT), reads=[('ps', bC)], writes=['QKTs'])
                    its.append((A_, B_, C_, D_, Da_))
                return its

            for t_ in range(2):
                p1its[t_][0]()
                p1its[t_] = (None,) + tuple(p1its[t_][1:])
            for which in range(3):
                for hf in range(2):
                    p3a_w(0, which, hf)
            g0its = p3a_iters(0)
            allits = [(a_[0], a_[1], b_[4], a_[2], b_[0], b_[1], a_[3], b_[2], b_[3]) for a_, b_ in zip(p1its, g0its)]
            for g in (1, 2):
                allits += [(None, None, it_[4], None, it_[0], it_[1], None, it_[2], it_[3]) for it_ in p3a_iters(g)]
            pipeline(allits, [0, 0, 5, 1, 2, 3, 1, 4, 4])
            if has_s:
                sample_states()

            S.op('pool', lambda e: e.memset(Vt[:, :, :, :].rearrange("p t g (h c) -> p (t g h) c", c=65)[:, :, 64:65], 1.0), writes=['Vt1'])
            its = []
            deferred = []
            pe_deferred = []

            def flush_fin():
                while pe_deferred:
                    pe_deferred.pop(0)[1]()
                while deferred:
                    deferred.pop(0)[1]()
            for hh in range(4):
                ch, pb = hh // 2, 64 * (hh % 2)
                units = []
                for r_ in range(4):
                    for n_ in range(4):
                        outs = [(n_, slice(r_, 512, 4), slice(0, 128))]
                        if n_ < 3:
                            outs.append((n_ + 1, slice(r_, 512, 4), slice(128, 256)))
                        units.append((1, r_ * 4 + n_, 256 if n_ < 3 else 128, mask2, outs, None))
                for r_ in range(16):
                    units.append((2, r_, 128, mask2[:, 0:128], [(q_, slice(r_, 512, 16), slice(32 * q_, 32 * q_ + 32)) for q_ in range(4)], None))
                for n_ in range(16):
                    q_, m_ = n_ // 4, n_ % 4
                    if m_ < 3:
                        outs = [(q_, slice(m_ * 128, m_ * 128 + 256), slice(0, 256))]
                    else:
                        outs = [(q_, slice(384, 512), slice(0, 128))]
                        if n_ < 15:
                            outs.append((q_ + 1, slice(0, 128), slice(128, 256)))
                    units.append((0, n_, 256 if n_ < 15 else 128, mask2, outs, q_ if m_ == 3 else None))
                started = set()
                for (g, tk, nq, msk, outs, finq) in units:
                    stt = {}
                    flags = []
                    for (bq, oc, pc) in outs:
                        flags.append(bq not in started)
                        started.add(bq)

                    def A_(stt=stt, g=g, tk=tk, nq=nq, ch=ch, pb=pb):
                        bS = 4 + rr('sb3', 3)
                        stt['bS'] = bS
                        QTq = QKT[pb:pb + 64, g, ch, tk * 128:tk * 128 + nq]
                        KTc = QKT[pb:pb + 64, g, 2 + ch, tk * 128:(tk + 1) * 128]
                        rdq = [('QKT', g, tk)] + ([('QKT', g, tk + 1)] if nq > 128 else [])
                        S.op('pe', lambda e: e.matmul(bank(bS)[:, 0:nq], lhsT=KTc, rhs=QTq, start=True, stop=True), reads=rdq, writes=[('ps', bS)])

                    def B_(stt=stt, nq=nq, msk=msk):
                        bS = stt['bS']
                        pt = rr('ptx', 12)
                        stt['pt'] = pt
                        S.op('act', lambda e: e.activation(out=PTx[pt][:, 0:nq], in_=bank(bS)[:, 0:nq], func=AF.Exp, scale=0.125),
                             reads=[('ps', bS)], writes=[('PTx', pt)])
                        S.op('dve', lambda e: e.tensor_tensor(out=PTx[pt][:, 0:nq], in0=PTx[pt][:, 0:nq], in1=msk[:, 0:nq], op=ALU.mult),
                             reads=[('PTx', pt), 'cmb'], writes=[('PTx', pt)])
                        if deferred:
                            deferred.pop(0)[1]()

                    def C_(stt=stt, g=g, tk=tk, outs=outs, flags=flags, finq=finq, hh=hh):
                        pt = stt['pt']
                        lhs = Vt[:, tk, g, hh * 65:(hh + 1) * 65]
                        newb = {bq for (bq, _, _), fl in zip(outs, flags) if fl}
                        if newb & ({b_ for b_, _ in deferred} | {b_ for b_, _ in pe_deferred}):
                            flush_fin()

                        def pv(e):
                            ins = None
                            for (bq, oc, pc), fl in zip(outs, flags):
                                ins = e.matmul(bank(bq)[0:65, oc], lhsT=lhs, rhs=PTx[pt][:, pc], start=fl, stop=False, skip_group_check=True)
                            return ins
                        S.op('pe', pv, reads=[('PTx', pt), ('Vt', g, tk), 'Vt1'], writes=[('ps', bq) for (bq, _, _) in outs])
                        if finq is not None:
                            flush_fin()
                        elif pe_deferred:
                            pe_deferred.pop(0)[1]()
                        if finq is not None:
                            bO = finq
                            bD = 7
                            S.op('act', lambda e: e.copy(out=rec[64:65, :], in_=bank(bO)[64:65, :]), reads=[('ps', bO)], writes=['recd'])

                            def bcast(bO=bO, bD=bD, hh=hh):
                                S.op('pe', lambda e: e.matmul(bank(bD)[0:64, :], lhsT=onesf[64:65, 0:64], rhs=rec[64:65, :], start=True, stop=True),
                                     reads=['recd', 'onesf'], writes=[('ps', bD)])
                                for pc_ in range(4):
                                    def piece(pc_=pc_):
                                        cs = slice(pc_ * 128, (pc_ + 1) * 128)
                                        S.op('dve', lambda e: e.reciprocal(out=rec[0:64, cs], in_=bank(bD)[0:64, cs]), reads=[('ps', bD)], writes=[('rec', pc_)])
                                        S.op('dve', lambda e: e.tensor_tensor(out=oT[0:64, hh, bO * 512 + pc_ * 128:bO * 512 + (pc_ + 1) * 128],
                                                                              in0=bank(bO)[0:64, cs], in1=rec[0:64, cs], op=ALU.mult),
                                             reads=[('ps', bO), ('rec', pc_)], writes=[('oT', bO)])
                                    deferred.append((bO, piece))
                            pe_deferred.append((bO, bcast))
                    its.append((A_, B_, C_))
            pipeline(its, [0, 1, 8], group=2)
            flush_fin()

            p3c = None
            if has_s:
                for g in range(3):
                    W = WINS[g]
                    fl = lambda ap: ap.rearrange("r c -> (r c)").rearrange("(a x) -> a x", x=2048)
                    for b_ in range(SB):
                        dma(fl(s_k[g][b_, 0:W - ST, :]), fl(ck[g][b_, ST:W, :]), sem='cpy')
                        dma(fl(s_v[g][b_, 0:W - ST, :]), fl(cv[g][b_, ST:W, :]), sem='cpy')
                bO = 7
                Osb = bank(bO)[:, 0:128].rearrange("p (h n) -> p h n", h=4)
                its = []
                firstO = True
                for g in range(3):
                    for hh in range(4):
                        stt = {}

                        def N1(stt=stt, g=g, hh=hh):
                            ch, pb = hh // 2, 64 * (hh % 2)
                            bS = 5 + (hh % 2)
                            stt['bS'] = bS
                            S.op('pe', lambda e: e.matmul(bank(bS)[0:TS, 0:TS], lhsT=QKTs[pb:pb + 64, g, 2 + ch, :],
                                                          rhs=QKTs[pb:pb + 64, g, ch, :], start=True, stop=True),
                                 reads=['QKTs'], writes=[('ps', bS)])

                        def N2(stt=stt, g=g, hh=hh, firstO=firstO):
                            bS = stt['bS']
                            pn = rr('kn', 2)
                            S.op('act', lambda e: e.activation(out=PTn[pn][:, :], in_=bank(bS)[0:TS, 0:TS], func=AF.Exp, scale=0.125),
                                 reads=[('ps', bS)], writes=[('PTn', pn)])
                            S.op('pool', lambda e: e.tensor_tensor(out=PTn[pn][:, :], in0=PTn[pn][:, :], in1=maskN[:, g * 32:(g + 1) * 32], op=ALU.mult),
                                 reads=[('PTn', pn), 'cmb'], writes=[('PTn', pn)])

                            def pvn(e):
                                e.matmul(Osb[0:64, hh, :], lhsT=Vs[:, g, hh * 64:(hh + 1) * 64], rhs=PTn[pn][:, :], start=firstO, stop=False, skip_group_check=True)
                                return e.matmul(Osb[64:128, hh, :], lhsT=ones[0:TS, :], rhs=PTn[pn][:, :], start=firstO, stop=False, skip_group_check=True)
                            S.op('pe', pvn, reads=[('PTn', pn), 'Vs', 'ones'], writes=[('ps', bO)])
                        its.append((N1, N2, None, None, None))
                        firstO = False
                for b_ in range(SB):
                    for g in range(3):
                        d = DILS[g]
                        ncls = min(d, ST)
                        nq = max(ST // d, 1)
                        for r_ in range(ncls):
                            stt = {}
                            qcols = slice(b_ * ST + r_, (b_ + 1) * ST, d)

                            def U0(stt=stt, g=g, b_=b_, r_=r_, d=d):
                                ks = rr('k', 2)
                                stt['ks'] = ks
                                dma(kc[ks][:, :], ck[g][b_, r_:WINS[g]:d, :], writes=[('kc', ks)], sem='kc%d' % ks)
                                dma(vc[ks][:, :], cv[g][b_, r_:WINS[g]:d, :], writes=[('vc', ks)], sem='vc%d' % ks)

                            def U1(stt=stt):
                                ks = stt['ks']
                                bT = 4
                                stt['bT'] = bT

                                def trk(e):
                                    e.transpose(bank(bT)[:, 0:128], kc[ks][:, 0:128], identf[:, :])
                                    return e.transpose(bank(bT)[:, 128:256], kc[ks][:, 128:256], identf[:, :])
                                S.op('pe', trk, reads=[('kc', ks), 'identf'], writes=[('ps', bT)])

                            def U2(stt=stt):
                                ks, bT = stt['ks'], stt['bT']
                                k2 = rr('k2', 2)
                                v3 = rr('v3', 3)
                                stt['k2'], stt['v3'] = k2, v3
                                S.op('act', lambda e: e.copy(out=kcT[k2][:, :, :], in_=bank(bT)[:, 0:256].rearrange("p (a n) -> p a n", a=2)),
                                     reads=[('ps', bT)], writes=[('kcT', k2)])
                                S.op('pool', lambda e: e.tensor_copy(out=vcb[v3][:, :], in_=vc[ks][:, :]), reads=[('vc', ks)], writes=[('vcb', v3)])

                            def U3(stt=stt, g=g, qcols=qcols, nq=nq):
                                k2 = stt['k2']

                                def scs(e):
                                    ins = None
                                    for hh in range(4):
                                        ch, pb = hh // 2, 64 * (hh % 2)
                                        ins = e.matmul(bank(5 + hh % 2)[:, ch * 8:ch * 8 + nq], lhsT=kcT[k2][pb:pb + 64, ch, :], rhs=QKTs[pb:pb + 64, g, ch, qcols],
                                                       start=(ch == 0), stop=True, skip_group_check=True)
                                    return ins
                                S.op('pe', scs, reads=[('kcT', k2), 'QKTs'], writes=[('ps', 5), ('ps', 6)])

                            def U4(stt=stt, g=g, qcols=qcols, nq=nq):
                                v3 = stt['v3']
                                p2 = rr('p2', 2)
                                for par in range(2):
                                    S.op('act', lambda e, par=par: e.activation(out=PTs[p2][:, par:4:2, 0:nq],
                                                                                in_=bank(5 + par)[:, 0:16].rearrange("p (h n) -> p h n", h=2)[:, :, 0:nq],
                                                                                func=AF.Exp, scale=0.125),
                                         reads=[('ps', 5 + par)], writes=[('PTs', p2)])
                                if nq > 1:
                                    S.op('pool', lambda e: e.tensor_tensor(out=PTs[p2][:, :, 0:nq], in0=PTs[p2][:, :, 0:nq],
                                                                           in1=maskC[:, 0:nq].unsqueeze(1).to_broadcast([128, 4, nq]), op=ALU.mult),
                                         reads=[('PTs', p2), 'cmb'], writes=[('PTs', p2)])

                                def pvs(e):
                                    ins = None
                                    for hh in range(4):
                                        e.matmul(Osb[0:64, hh, qcols], lhsT=vcb[v3][:, hh * 64:(hh + 1) * 64], rhs=PTs[p2][:, hh, 0:nq],
                                                 start=False, stop=False, skip_group_check=True)
                                        ins = e.matmul(Osb[64:128, hh, qcols], lhsT=ones[:, :], rhs=PTs[p2][:, hh, 0:nq],
                                                       start=False, stop=False, skip_group_check=True)
                                    return ins
                                S.op('pe', pvs, reads=[('PTs', p2), ('vcb', v3), 'ones'], writes=[('ps', bO)])
                            its.append((U0, U1, U2, U3, U4))

                def p3c_fin():
                    S.op('dve', lambda e: e.reciprocal(out=rec[64:128, 0:128], in_=bank(bO)[64:128, 0:128]), reads=[('ps', bO)], writes=['rec', 'recd'])
                    S.op('dve', lambda e: e.tensor_tensor(out=oT[0:64, :, T:T + TS], in0=Osb[0:64, :, :],
                                                          in1=rec[64:128, 0:128].rearrange("p (h n) -> p h n", h=4), op=ALU.mult),
                         reads=[('ps', bO), 'rec'], writes=[('oT', 4)])
                p3c = Pipe(its, [0, 1, 2, 3, 4], desc=True)
                p3c.finish()
                p3c_fin()
                p3c = None

            wts = {}

            def W_(c):
                wx, rx = wload([(0, win3[:, :, c * 128:(c + 1) * 128])], 1024)
                wg, rg = wload([(0, win3[:, :, 1024 + c * 128:1024 + (c + 1) * 128])], 1024)
                wts[c] = (wx.rearrange("p (k n) -> p k n", k=8), rx, wg.rearrange("p (k n) -> p k n", k=8), rg)
            W_(0)
            W_(1)
            W_(2)
            S.barrier()
            xcbuf = [r_xc, xtbig[:, 0:RW]]
            rabuf = [r_ra, rabuf_alt]
            iubuf = [r_iu, iubuf_alt]

            nsg = len(segs)
            raall = [('ra', i) for i in range(nsg)]
            iuall = [('iu', i) for i in range(nsg)]
            ggall = [('gg', i) for i in range(nsg)]
            xpall = [('xp', 'h')] + [('xp', i) for i in range(4)]

            def A1_(c):
                wx3, rx, _, _ = wts[c]
                xc_ = xcbuf[c % 2]
                XC, XCS = ('xc', c % 2), ('xcS', c % 2)
                S.op('pool', lambda e: e.memset(r_xp[:, 0:3], 0.0), writes=[('xp', 'h')])
                for si, (c0, w_) in enumerate(segs):
                    b = nb()
                    S.op('pe', mm(bank(b)[:, 0:w_], [(wx3[:, k, :], xnT[:, k, c0:c0 + w_]) for k in range(8)]), reads=xn_all + [rx], writes=[('ps', b)])
                    if si < 4:
                        S.op('dve', lambda e, b=b, c0=c0, w_=w_: e.tensor_copy(out=r_xp[:, 3 + c0:3 + c0 + w_], in_=bank(b)[:, 0:w_]), reads=[('ps', b)], writes=[('xp', si)])
                    else:
                        S.op('dve', lambda e, b=b: e.tensor_copy(out=xps[:, :, 3:11], in_=bank(b)[:, 0:TS].rearrange("p (a n) -> p a n", a=4)),
                             reads=[('ps', b)], writes=['xps'])
                        S.op('pool', lambda e: e.tensor_copy(out=xps[:, :, 0:3], in_=scT[:, c, :].rearrange("p (a n) -> p a n", a=4)),
                             reads=['scT'], writes=['xps'])
                cwl = [vec[:, CW + j * 8 + c:CW + j * 8 + c + 1] for j in range(4)]
                cbv = vec[:, CB + c:CB + c + 1]
                S.op('dve', lambda e: e.tensor_scalar(out=xc_[:, 0:T], in0=r_xp[:, 0:T], scalar1=cwl[0], scalar2=cbv, op0=ALU.mult, op1=ALU.add),
                     reads=xpall + ['vec'], writes=[XC])
                for j in range(1, 4):
                    S.op('dve', lambda e, j=j: e.scalar_tensor_tensor(out=xc_[:, 0:T], in0=r_xp[:, j:j + T], scalar=cwl[j], in1=xc_[:, 0:T], op0=ALU.mult, op1=ALU.add),
                         reads=xpall + [XC, 'vec'], writes=[XC])
                S.op('pool', lambda e: e.tensor_copy(out=fin[:, 0:24].rearrange("p (j c) -> p j c", j=3)[:, :, c], in_=r_xp[:, T:T + 3]),
                     reads=xpall, writes=['fin'])
                if has_s:
                    xcs = xc_[:, T:T + TS].rearrange("p (a n) -> p a n", a=4)
                    S.op('dve', lambda e: e.tensor_scalar(out=xcs, in0=xps[:, :, 0:8], scalar1=cwl[0], scalar2=cbv, op0=ALU.mult, op1=ALU.add),
                         reads=['xps', 'vec'], writes=[XCS])
                    for j in range(1, 4):
                        S.op('dve', lambda e, j=j: e.scalar_tensor_tensor(out=xcs, in0=xps[:, :, j:j + 8], scalar=cwl[j], in1=xcs, op0=ALU.mult, op1=ALU.add),
                             reads=['xps', XCS, 'vec'], writes=[XCS])
                    S.op('pool', lambda e: e.tensor_copy(out=fins[:, 0:96].rearrange("p (b j c) -> p b j c", b=4, j=3)[:, :, :, c], in_=xps[:, :, 8:11]),
                         reads=['xps'], writes=['fins'])

            def A2_(c):
                r_ra, r_iu = rabuf[c % 2], iubuf[c % 2]
                xc_ = xcbuf[c % 2]
                XC, XCS = ('xc', c % 2), ('xcS', c % 2)
                for si, (c0, w_) in enumerate(segs):
                    S.op('act', lambda e, c0=c0, w_=w_: e.copy(out=r_xcb[:, c0:c0 + w_], in_=xc_[:, c0:c0 + w_]), reads=[XC, XCS], writes=[('xcb', si)])
                for si, (c0, w_) in enumerate(segs):
                    bR = nb()
                    bI = nb()
                    S.op('pe', lambda e, bR=bR, c0=c0, w_=w_: e.matmul(bank(bR)[:, 0:w_], lhsT=gwbd[:, 0, c, :], rhs=r_xcb[:, c0:c0 + w_], start=True, stop=True),
                         reads=[('xcb', si), 'gwbd'], writes=[('ps', bR)])
                    S.op('pe', lambda e, bI=bI, c0=c0, w_=w_: e.matmul(bank(bI)[:, 0:w_], lhsT=gwbd[:, 1, c, :], rhs=r_xcb[:, c0:c0 + w_], start=True, stop=True),
                         reads=[('xcb', si), 'gwbd'], writes=[('ps', bI)])
                    S.op('act', lambda e, bR=bR, c0=c0, w_=w_: e.activation(out=r_ra[:, c0:c0 + w_], in_=bank(bR)[:, 0:w_], func=AF.Sigmoid, bias=vec[:, GAB + c:GAB + c + 1]),
                         reads=[('ps', bR), 'vec'], writes=[('ra', c % 2, si)])
                    S.op('act', lambda e, bI=bI, c0=c0, w_=w_: e.activation(out=r_iu[:, c0:c0 + w_], in_=bank(bI)[:, 0:w_], func=AF.Sigmoid, bias=vec[:, GXB + c:GXB + c + 1]),
                         reads=[('ps', bI), 'vec'], writes=[('iu', c % 2, si)])

            def B_(c):
                r_ra, r_iu = rabuf[c % 2], iubuf[c % 2]
                raall = [('ra', c % 2, i) for i in range(nsg)]
                iuall = [('iu', c % 2, i) for i in range(nsg)]
                _, _, wg3, rg = wts[c]
                xc_ = xcbuf[c % 2]
                XC, XCS = ('xc', c % 2), ('xcS', c % 2)
                S.op('act', lambda e: e.activation(out=r_e2[:, 0:NT], in_=r_ra[:, 0:NT], func=AF.Exp, scale=cA2[:, c:c + 1]), reads=raall + ['cA2'], writes=['e2'])
                S.op('act', lambda e: e.activation(out=r_ra[:, 0:NT], in_=r_ra[:, 0:NT], func=AF.Exp, scale=cA[:, c:c + 1]), reads=raall + ['cA', 'e2'], writes=raall)
                S.op('act', lambda e: e.activation(out=r_e2[:, 0:NT], in_=r_e2[:, 0:NT], func=AF.Sqrt, scale=-1.0, bias=1.0), reads=['e2'], writes=['e2'])
                S.op('dve', lambda e: e.tensor_tensor(out=r_iu[:, 0:NT], in0=r_iu[:, 0:NT], in1=xc_[:, 0:NT], op=ALU.mult), reads=iuall + [XC, XCS], writes=iuall)
                S.op('dve', lambda e: e.tensor_tensor(out=r_iu[:, 0:NT], in0=r_iu[:, 0:NT], in1=r_e2[:, 0:NT], op=ALU.mult), reads=iuall + ['e2'], writes=iuall)
                S.op('dve', lambda e: e.tensor_tensor_scan(out=xc_[:, 0:T], data0=r_ra[:, 0:T], data1=r_iu[:, 0:T], initial=0.0, op0=ALU.mult, op1=ALU.add),
                     reads=raall + iuall, writes=[XC])
                S.op('pool', lambda e: e.tensor_copy(out=fin[:, 24 + c:25 + c], in_=xc_[:, T - 1:T]), reads=[XC], writes=['fin'])
                if has_s:
                    for b_ in range(SB):
                        sl_ = slice(T + b_ * ST, T + (b_ + 1) * ST)
                        S.op('dve', lambda e, sl_=sl_, b_=b_: e.tensor_tensor_scan(out=xc_[:, sl_], data0=r_ra[:, sl_], data1=r_iu[:, sl_],
                                                                               initial=h0T[:, c, b_:b_ + 1], op0=ALU.mult, op1=ALU.add),
                             reads=raall + iuall + ['h0T'], writes=[XCS])
                    S.op('pool', lambda e: e.tensor_copy(out=fins[:, 96:128].rearrange("p (b c) -> p b c", b=4)[:, :, c],
                                                         in_=xc_[:, T:T + TS].rearrange("p (b t) -> p b t", b=4)[:, :, ST - 1]),
                         reads=[XCS], writes=['fins'])
                for si, (c0, w_) in enumerate(segs):
                    b = nb()
                    S.op('pe', mm(bank(b)[:, 0:w_], [(wg3[:, k, :], xnT[:, k, c0:c0 + w_]) for k in range(8)]), reads=xn_all + [rg], writes=[('ps', b)])
                    S.op('act', lambda e, b=b, c0=c0, w_=w_: e.activation(out=r_gg[:, c0:c0 + w_], in_=bank(b)[:, 0:w_], func=AF.Gelu_apprx_tanh),
                         reads=[('ps', b)], writes=[('gg', si)])
                S.op('pool', lambda e: e.tensor_tensor(out=ghsT[:, c, 0:NT], in0=r_gg[:, 0:NT], in1=xc_[:, 0:NT], op=ALU.mult),
                     reads=ggall + [XC, XCS], writes=[('ghsT', c)])

            npump = (p3c.nsteps + 22) // 23 if p3c is not None else 0

            def pump():
                if p3c is not None:
                    p3c.step(npump)
            A1_(0)
            A2_(0)
            for c in range(1, 8):
                A1_(c)
                B_(c - 1)
                if c + 2 < 8:
                    W_(c + 2)
                A2_(c)
            B_(7)
            b = nb()
            S.op('pe', lambda e, b=b: e.transpose(bank(b)[0:32, 0:128], fin[:, 0:32], identf[:, :]), reads=['fin', 'identf'], writes=[('ps', b)])
            S.op('act', lambda e, b=b: e.copy(out=fint[0:32, 0:128], in_=bank(b)[0:32, 0:128]), reads=[('ps', b)], writes=['fint'])
            dma(p_conv[ps_, :].rearrange("(r p) -> r p", p=128), fint[0:24, 0:128], reads=['fint'], sem='fin')
            dma(p_h[ps_, :].rearrange("(r p) -> r p", p=128), fint[24:32, 0:128], reads=['fint'], sem='fin')
            if has_s:
                b = nb()
                S.op('pe', lambda e, b=b: e.transpose(bank(b)[:, 0:128], fins[:, :], identf[:, :]), reads=['fins', 'identf'], writes=[('ps', b)])
                S.op('act', lambda e, b=b: e.copy(out=fin[:, :], in_=bank(b)[:, 0:128]), reads=[('ps', b)], writes=['fin'])
                dma(s_conv.rearrange("(r p) -> r p", p=128), fin[0:96, :], reads=['fin'], sem='fin')
                dma(s_h.rearrange("(r p) -> r p", p=128), fin[96:128, :], reads=['fin'], sem='fin')

            GA0 = 2048 + 3 * 768
            def w4a(o):
                return (wload([(0, wga3_src(o))], 1024), wload([(0, wgb3_src(o))], 1024),
                        wload([(0, wa3[:, :, o * 128:(o + 1) * 128])], 1024),
                        wload([(0, wb3[:, :, o * 128:(o + 1) * 128])], 512, parts=64))
            wga3_src = lambda o: win3[:, :, GA0 + o * 128:GA0 + (o + 1) * 128]
            wgb3_src = lambda o: win3[:, :, GA0 + D + o * 128:GA0 + D + (o + 1) * 128]
            w4 = {0: w4a(0)}
            S.barrier()
            for o in range(8):
                if o + 1 < 8:
                    w4[o + 1] = w4a(o + 1)
                (wga_, rga_), (wgb_, rgb_), (wa_, ra_), (wb_, rb_) = w4[o]
                wa_3 = wa_.rearrange("p (k n) -> p k n", k=8)
                wga3 = wga_.rearrange("p (k n) -> p k n", k=8)
                wgb3 = wgb_.rearrange("p (k n) -> p k n", k=8)
                wb_3 = wb_.rearrange("p (k n) -> p k n", k=4)
                ghall = [('ghsT', i) for i in range(8)]
                otall = [('oT', i) for i in range(5)]
                for si, (c0, w_) in enumerate(segs):
                    bA_, bB_, bGA, bGB = nb(), nb(), nb(), nb()
                    S.op('pe', mm(bank(bGA)[:, 0:w_], [(wga3[:, k, :], xnT[:, k, c0:c0 + w_]) for k in range(8)]), reads=xn_all + [rga_], writes=[('ps', bGA)])
                    S.op('pe', mm(bank(bGB)[:, 0:w_], [(wgb3[:, k, :], xnT[:, k, c0:c0 + w_]) for k in range(8)]), reads=xn_all + [rgb_], writes=[('ps', bGB)])
                    S.op('pe', mm(bank(bA_)[:, 0:w_], [(wa_3[:, k, :], ghsT[:, k, c0:c0 + w_]) for k in range(8)]), reads=ghall + [ra_], writes=[('ps', bA_)])
                    S.op('pe', mm(bank(bB_)[:, 0:w_], [(wb_3[:, k, :], oT[0:64, k, c0:c0 + w_]) for k in range(4)]), reads=otall + [rb_], writes=[('ps', bB_)])
                    s0, s1 = rr('sgm', 2) * 2, None
                    s1 = s0 + 1
                    S.op('act', lambda e, bGA=bGA, s0=s0, o=o, w_=w_: e.activation(out=sg[s0][:, 0:w_], in_=bank(bGA)[:, 0:w_], func=AF.Sigmoid, bias=vec[:, BM + o:BM + o + 1]),
                         reads=[('ps', bGA), 'vec'], writes=[('sg', s0)])
                    S.op('act', lambda e, bGB=bGB, s1=s1, o=o, w_=w_: e.activation(out=sg[s1][:, 0:w_], in_=bank(bGB)[:, 0:w_], func=AF.Sigmoid, bias=vec[:, BM + 8 + o:BM + 9 + o]),
                         reads=[('ps', bGB), 'vec'], writes=[('sg', s1)])
                    S.op('dve', lambda e, bA_=bA_, s0=s0, w_=w_: e.tensor_tensor(out=sg[s0][:, 0:w_], in0=sg[s0][:, 0:w_], in1=bank(bA_)[:, 0:w_], op=ALU.mult),
                         reads=[('ps', bA_), ('sg', s0)], writes=[('sg', s0)])
                    S.op('dve', lambda e, bB_=bB_, s1=s1, w_=w_: e.tensor_tensor(out=sg[s1][:, 0:w_], in0=sg[s1][:, 0:w_], in1=bank(bB_)[:, 0:w_], op=ALU.mult),
                         reads=[('ps', bB_), ('sg', s1)], writes=[('sg', s1)])
                    S.op('pool', lambda e, s0=s0, s1=s1, o=o, c0=c0, w_=w_: e.tensor_tensor(out=mT[:, o, c0:c0 + w_], in0=sg[s0][:, 0:w_], in1=sg[s1][:, 0:w_], op=ALU.add),
                         reads=[('sg', s0), ('sg', s1)], writes=[('mT', o)])

            wo_ = [wload([(0, wo3[:, k2, :])], 1024) for k2 in range(8)]
            wor_ = [r_ for _, r_ in wo_]
            S.barrier()
            mall = [('mT', i) for i in range(8)]
            its = []
            for ti, (c0, np_) in enumerate(tiles):
                stt = {}
                sl = ti % 2
                src = xp[ps_, c0:c0 + 128, :] if ti < 16 else xs[:, :]
                hb = hbuf(ti)[0:np_, :]

                def A_(stt=stt, ti=ti, c0=c0, np_=np_, sl=sl, src=src):
                    bX = rr('pb', 3) * 2
                    stt['bX'] = bX
                    S.op('pe', mm(bank(bX)[0:np_, :], [(mT[:, k, c0:c0 + np_], wo_[k][0][:, 0:512]) for k in range(8)]), reads=mall + wor_, writes=[('ps', bX)])
                    S.op('pe', mm(bank(bX + 1)[0:np_, :], [(mT[:, k, c0:c0 + np_], wo_[k][0][:, 512:1024]) for k in range(8)]), reads=mall + wor_, writes=[('ps', bX + 1)])
                    dma(xt[sl][0:np_, :], src, writes=[('xt', sl)], sem='xt%d' % sl)
                Bn, Cn, Dn = rms_stages(hb, np_, mT, c0, G2, sl, ('hb', ti), ('hnT', ti))

                def B_(stt=stt, ti=ti, np_=np_, sl=sl, hb=hb, Bn=Bn):
                    bX = stt['bX']
                    xy = psum[0:np_, bX * 512:(bX + 2) * 512]
                    S.op('dve', lambda e: e.tensor_tensor(out=hb, in0=xy, in1=xt[sl][0:np_, :], op=ALU.add),
                         reads=[('ps', bX), ('ps', bX + 1), ('xt', sl)], writes=[('hb', ti)])
                    Bn()
                its.append((A_, B_, Cn, Dn))
            pipeline(its, [0, 1, 2, 2])
            hnT = mT
            hn_all = [('hnT', i) for i in range(len(tiles))]
            hn_seg = [[('hnT', 4 * si_ + i_) for i_ in range(4)] if si_ < 4 else [('hnT', 16)] for si_ in range(len(segs))]

            dma(gfb, gfb_d[:, :], writes=['gfb'], sem='c_gfb')

            def final_tile(ti, c0, np_):
                sl = rr('x', 2)
                hb = hbuf(ti)[0:np_, :]
                S.op('act', lambda e: e.activation(out=xsb[sl][0:np_, :], in_=hb, func=AF.Square, accum_out=ss[sl][0:np_, :]),
                     reads=[('hb', ti)], writes=[('xsb', sl), ('ss', sl)])
                S.op('act', lambda e: e.activation(out=rs[sl][0:np_, :], in_=ss[sl][0:np_, :], func=AF.Sqrt, scale=1.0 / D, bias=EPS),
                     reads=[('ss', sl)], writes=[('rs', sl)])
                S.op('dve', lambda e: e.reciprocal(out=rs[sl][0:np_, :], in_=rs[sl][0:np_, :]), reads=[('rs', sl)], writes=[('rs', sl)])
                S.op('dve', lambda e: e.scalar_tensor_tensor(out=xt[sl][0:np_, :], in0=hb, scalar=rs[sl][0:np_, 0:1], in1=gfb[0:np_, :], op0=ALU.mult, op1=ALU.mult),
                     reads=[('hb', ti), ('rs', sl), 'gfb'], writes=[('xt', sl)])
                dst = y_p[ps_, c0:c0 + 128, :] if ti < 16 else y_s[:, :]
                dma(dst, xt[sl][0:np_, :], reads=[('xt', sl)], sem='xt%d' % sl)

            jgroups = [list(range(j0, min(j0 + 3, NJ))) for j0 in range(0, NJ, 3)]
            w5q, w5r, w5p = [], {}, [0]
            for gi, jg in enumerate(jgroups):
                for j in jg:
                    w5q.append(('in', j))
                w5q.append(('out', gi))

            def w5need(upto):
                while w5p[0] <= min(upto, len(w5q) - 1):
                    kind, v = w5q[w5p[0]]
                    if kind == 'in':
                        w5r[(kind, v)] = (wload([(0, wf13[:, :, v * 128:(v + 1) * 128])], 1024),
                                          wload([(0, wf13[:, :, DFF + v * 128:DFF + (v + 1) * 128])], 1024))
                    else:
                        w5r[(kind, v)] = [wload([(0, w_f2[j_ * 128:(j_ + 1) * 128, :])], 1024) for j_ in jgroups[v]]
                    w5p[0] += 1
            for gi, jg in enumerate(jgroups):
                a_ = gi % 2
                for jj, j in enumerate(jg):
                    qi = w5q.index(('in', j))
                    w5need(qi + 1)
                    (wgt, rgt), (wup, rup) = w5r[('in', j)]
                    wgt3 = wgt.rearrange("p (k n) -> p k n", k=8)
                    wup3 = wup.rearrange("p (k n) -> p k n", k=8)
                    for si, (c0, w_) in enumerate(segs):
                        bG, bU = nb(), nb()
                        S.op('pe', mm(bank(bG)[:, 0:w_], [(wgt3[:, k, :], hnT[:, k, c0:c0 + w_]) for k in range(8)]), reads=hn_seg[si] + [rgt], writes=[('ps', bG)])
                        S.op('pe', mm(bank(bU)[:, 0:w_], [(wup3[:, k, :], hnT[:, k, c0:c0 + w_]) for k in range(8)]), reads=hn_seg[si] + [rup], writes=[('ps', bU)])
                        s0 = rr('sgf', 4)
                        S.op('act', lambda e, bG=bG, s0=s0, w_=w_: e.activation(out=sg[s0][:, 0:w_], in_=bank(bG)[:, 0:w_], func=AF.Silu), reads=[('ps', bG)], writes=[('sg', s0)])
                        S.op('dve', lambda e, bU=bU, s0=s0, a_=a_, jj=jj, c0=c0, w_=w_: e.tensor_tensor(out=actT[a_][:, jj, c0:c0 + w_], in0=sg[s0][:, 0:w_], in1=bank(bU)[:, 0:w_], op=ALU.mult),
                             reads=[('ps', bU), ('sg', s0)], writes=[('actT', a_)])
                w5need(w5q.index(('out', gi)) + 1)
                w2 = w5r[('out', gi)]
                w2r = [r_ for _, r_ in w2]
                for ti, (c0, np_) in enumerate(tiles):
                    bX = nb2()
                    xy = psum[0:np_, bX * 512:(bX + 2) * 512]
                    S.op('pe', mm(bank(bX)[0:np_, :], [(actT[a_][:, jj, c0:c0 + np_], w2[jj][0][:, 0:512]) for jj in range(len(jg))]), reads=[('actT', a_)] + w2r, writes=[('ps', bX)])
                    S.op('pe', mm(bank(bX + 1)[0:np_, :], [(actT[a_][:, jj, c0:c0 + np_], w2[jj][0][:, 512:1024]) for jj in range(len(jg))]), reads=[('actT', a_)] + w2r, writes=[('ps', bX + 1)])
                    hb = hbuf(ti)[0:np_, :]
                    S.op('dve', lambda e, hb=hb, xy=xy: e.tensor_tensor(out=hb, in0=hb, in1=xy, op=ALU.add),
                         reads=[('ps', bX), ('ps', bX + 1), ('hb', ti)], writes=[('hb', ti)])
                    if gi == len(jgroups) - 1 and ti >= 1:
                        final_tile(ti - 1, *tiles[ti - 1])
                if gi == len(jgroups) - 1:
                    final_tile(len(tiles) - 1, *tiles[-1])
            if ps_ + 1 < NB:
                for which in range(3):
                    for hf in range(2):
                        c_ = 2048 + which * 768
                        p3next[0][(0, which, hf)] = wload([(0, win3[:, hf * 4:(hf + 1) * 4, c_:c_ + 256])], 1024)

        block = es.enter_context(nc.Block())
        S.emit(block)
    return nc


_NC = None


def _consts():
    half = 32
    inv = np.exp(-math.log(10000.0) * np.arange(half, dtype=np.float32) * np.float32(2.0 / 64)).astype(np.float32)

    def tab(pos):
        ang = pos.astype(np.float32)[:, None] * inv[None, :]
        return np.cos(ang).astype(np.float32), np.sin(ang).astype(np.float32)
    rope = np.zeros((3, 128, 2, 16, 32), np.float32)
    j = np.arange(128)
    for g, d in enumerate(DILS):
        for tl in range(16):
            if g == 0:
                pos = tl * 128 + j
            elif g == 1:
                r_, n_ = tl // 4, tl % 4
                pos = 4 * (128 * n_ + j) + r_
            else:
                pos = 16 * j + tl
            c, s = tab(pos)
            rope[g, :, 0, tl, :] = c
            rope[g, :, 1, tl, :] = s
    pos_s = PAST + (np.arange(TS) % ST)
    c, s = tab(pos_s)
    ropes = np.concatenate([c, s], axis=1).astype(np.float32)
    cm = np.zeros((128, 872), np.float32)
    jj = np.arange(128)[:, None]
    ii = np.arange(128)[None, :]
    cm[:, 0:128] = (jj <= ii)
    cm[:, 128:256] = (jj >= ii)
    for q in range(4):
        cm[:, 256 + q * 32:256 + (q + 1) * 32] = (jj <= (32 * q + np.arange(32)[None, :]))
    cm[:, 384:392] = (jj >= np.arange(8)[None, :])
    kb, kt = np.arange(TS)[:, None] // ST, np.arange(TS)[:, None] % ST
    qb, qt = np.arange(TS)[None, :] // ST, np.arange(TS)[None, :] % ST
    for g, d in enumerate(DILS):
        cm[0:TS, 392 + g * 32:392 + (g + 1) * 32] = (kb == qb) & (qt >= kt) & ((qt - kt) % d == 0)
    cm[:, 488:872] = np.where(cm[:, 0:384] > 0, 0.0, -30000.0)
    ident = np.eye(128, dtype=np.float32)
    return rope.reshape(3, 128, 1024), ropes, cm, ident


def kernel(x_prompt, x_sample, state_conv, state_h, cache_k_g0, cache_v_g0, cache_k_g1, cache_v_g1,
           cache_k_g2, cache_v_g2, norm1_g, w_in, b_merge, conv_w, conv_b, gate_a_w, gate_a_b,
           gate_x_w, gate_x_b, rg_lambda, w_branch_a, w_branch_b, w_out, norm2_g, w_ffn_in, w_ffn_out,
           norm_f_g):
    global _NC
    if _NC is None:
        _NC = build()
    nc = _NC
    f = lambda a: np.ascontiguousarray(np.asarray(a, dtype=np.float32))
    fm = lambda v: f(v).reshape(8, 128).T
    vecs = np.zeros((128, 96), np.float32)
    vecs[:, 0:8] = fm(norm1_g[0]); vecs[:, 8:16] = fm(norm2_g[0])
    vecs[:, 16:32] = f(b_merge[0]).reshape(16, 128).T
    for j in range(4):
        vecs[:, 32 + j * 8:40 + j * 8] = fm(conv_w[0, j])
    vecs[:, 64:72] = fm(conv_b[0]); vecs[:, 72:80] = fm(gate_a_b[0]); vecs[:, 80:88] = fm(gate_x_b[0])
    vecs[:, 88:96] = fm(rg_lambda[0])
    gfb = np.ascontiguousarray(np.broadcast_to(f(norm_f_g)[None, :], (128, D)))
    gw = np.zeros((128, 2, 8, 128), np.float32)
    for wi, gwm in enumerate((gate_a_w, gate_x_w)):
        gwm = f(gwm[0])
        for c in range(8):
            gw[0:64, wi, c, 0:64] = gwm[2 * c]
            gw[64:128, wi, c, 64:128] = gwm[2 * c + 1]
    rope, ropes, cm, ident = _consts()
    shared = dict(w_in=f(w_in[0]), w_a=f(w_branch_a[0]), w_b=f(w_branch_b[0]), w_out=f(w_out[0]),
                  w_f1=f(w_ffn_in[0]), w_f2=f(w_ffn_out[0]), vecs=vecs, gfb=gfb, gwbd=gw.reshape(128, 2048),
                  rope=rope, ropes=ropes, cmask=cm, ident=ident)
    caches_k = [f(cache_k_g0[0]), f(cache_k_g1[0]), f(cache_k_g2[0])]
    caches_v = [f(cache_v_g0[0]), f(cache_v_g1[0]), f(cache_v_g2[0])]
    xp_, xs_ = f(x_prompt), f(x_sample)
    sc_, sh_ = f(state_conv[0]), f(state_h[0])
    in_maps = []
    for c in range(NCORES):
        m = dict(shared)
        m["xp"] = xp_[c * NB:(c + 1) * NB]
        m["xs"] = xs_[c * SB:(c + 1) * SB].reshape(TS, D)
        m["sconv"] = sc_[c * SB:(c + 1) * SB].reshape(SB * 3, D)
        m["sh"] = sh_[c * SB:(c + 1) * SB]
        for g in range(3):
            m["ck%d" % g] = caches_k[g][c * SB:(c + 1) * SB].reshape(SB, WINS[g], 256)
            m["cv%d" % g] = caches_v[g][c * SB:(c + 1) * SB].reshape(SB, WINS[g], 256)
        in_maps.append(m)
    res = run_bass_kernel_spmd(nc, in_maps, core_ids=list(range(NCORES)))
    R = res.results
    cat = lambda name: np.concatenate([np.asarray(r[name]) for r in R], axis=0)
    y_prompt = cat("y_p").reshape(16, T, D)
    y_sample = cat("y_s").reshape(32, ST, D)
    pconv = cat("p_conv").reshape(1, 16, 3, D)
    ph = cat("p_h").reshape(1, 16, D)
    outs = [y_prompt, y_sample, pconv, ph]
    for g in range(3):
        outs.append(cat("p_k%d" % g).reshape(1, 16, WINS[g], 4, 64))
        outs.append(cat("p_v%d" % g).reshape(1, 16, WINS[g], 4, 64))
    outs.append(np.concatenate([np.asarray(r["s_conv"]).reshape(SB, 3, D) for r in R], axis=0).reshape(1, 32, 3, D))
    outs.append(np.concatenate([np.asarray(r["s_h"]).reshape(SB, D) for r in R], axis=0).reshape(1, 32, D))
    for g in range(3):
        outs.append(cat("s_k%d" % g).reshape(1, 32, WINS[g], 4, 64))
        outs.append(cat("s_v%d" % g).reshape(1, 32, WINS[g], 4, 64))
    return tuple(np.ascontiguousarray(o, dtype=np.float32) for o in outs)
```
